# Optimizing a Trainium2 kernel written in Bass

```python
import jax, jax.numpy as jnp
from jax import lax
import numpy as np

D_MODEL = 1024
BATCH = 1
SEQ = 16384
DEPTH = 2

CHUNK = 64
PLE_DIM = 256
HG_HEADS = 4
HG_DK = 128
HG_DV = 128
HG_WIDTH = HG_HEADS * HG_DV
SB_HEADS = 8
SB_DH = 64
SB_WIDTH = SB_HEADS * SB_DH
Q_BLOCK = 128
D_FF = 2816
CONV_W = 3
DN_ALPHA = (2 * DEPTH) ** 0.25
DN_BETA = (8 * DEPTH) ** -0.25
LN_EPS = 1e-5
RMS_EPS = 1e-6
F_FLOOR = 1e-30

SPLITS = (HG_HEADS * HG_DK, HG_HEADS * HG_DK, HG_WIDTH, HG_WIDTH,
          SB_WIDTH, SB_WIDTH, SB_WIDTH, D_MODEL, D_MODEL)
IN_COLS = int(sum(SPLITS))
SPLIT_IDX = tuple(int(v) for v in np.cumsum(SPLITS)[:-1])

kernel_name = "hgrn2_stickbreaking_gated_hybrid_deepnorm"


def layer_norm(x, g, b):
    x32 = x.astype(jnp.float32)
    mu = jnp.mean(x32, axis=-1, keepdims=True)
    var = jnp.mean(jnp.square(x32 - mu), axis=-1, keepdims=True)
    y = (x32 - mu) * lax.rsqrt(var + LN_EPS) * g.astype(jnp.float32) + b.astype(jnp.float32)
    return y.astype(x.dtype)


def rms_norm(x, g):
    x32 = x.astype(jnp.float32)
    return x32 * lax.rsqrt(jnp.mean(jnp.square(x32), axis=-1, keepdims=True) + RMS_EPS) * g.astype(jnp.float32)


def hgrn2_recurrence(q, f_logit, v, lb):
    B, S, H, Dk = q.shape
    Dv = v.shape[-1]
    n = S // CHUNK
    f32 = jnp.float32
    z = f_logit.astype(f32)
    lb = lb.astype(f32)
    q = jax.nn.silu(q.astype(f32))
    f_gate = lb + (1.0 - lb) * jax.nn.sigmoid(z)
    log_f = jnp.log(jnp.maximum(f_gate, F_FLOOR))
    k = (1.0 - lb) * jax.nn.sigmoid(-z)
    v = v.astype(f32)

    def to_chunks(a):
        return jnp.moveaxis(a.reshape(B, n, CHUNK, *a.shape[2:]), 1, 0)

    causal = jnp.tril(jnp.ones((CHUNK, CHUNK), dtype=bool))[None, :, :, None, None]

    def step(state, inp):
        qc, kc, vc, gc = inp
        b = jnp.cumsum(gc, axis=1)
        o_inter = jnp.einsum('bthk,bhkv->bthv', qc * jnp.exp(b), state)
        diff = jnp.where(causal, b[:, :, None] - b[:, None, :], 0.0)
        decay = jnp.where(causal, jnp.exp(diff), 0.0)
        scores = jnp.einsum('bthk,bshk,btshk->btsh', qc, kc, decay)
        o_intra = jnp.einsum('btsh,bshv->bthv', scores, vc)
        b_last = b[:, -1]
        k_dec = kc * jnp.exp(b_last[:, None] - b)
        state = state * jnp.exp(b_last)[..., None] + jnp.einsum('bshk,bshv->bhkv', k_dec, vc)
        return state, o_inter + o_intra

    state0 = jnp.zeros((B, H, Dk, Dv), f32)
    _, o = lax.scan(step, state0, (to_chunks(q), to_chunks(k), to_chunks(v), to_chunks(log_f)))
    return jnp.moveaxis(o, 0, 1).reshape(B, S, H, Dv)


def stick_breaking_attention(q, k, v):
    B, S, H, Dh = q.shape
    nblk = S // Q_BLOCK
    qh = jnp.transpose(q, (0, 2, 1, 3)) * (Dh ** -0.5)
    kh = jnp.transpose(k, (0, 2, 1, 3))
    vh = jnp.transpose(v, (0, 2, 1, 3)).astype(jnp.float32)
    q_blocks = jnp.moveaxis(qh.reshape(B, H, nblk, Q_BLOCK, Dh), 2, 0)
    key_pos = jnp.arange(S)

    def block(args):
        qi, bi = args
        z = jnp.einsum('bhtd,bhsd->bhts', qi, kh).astype(jnp.float32)
        t_pos = bi * Q_BLOCK + jnp.arange(Q_BLOCK)
        mask = (key_pos[None, :] < t_pos[:, None])[None, None]
        log_fail = jnp.where(mask, jax.nn.log_sigmoid(-z), 0.0)
        suffix = lax.cumsum(log_fail, axis=3, reverse=True) - log_fail
        log_w = jnp.where(mask, jax.nn.log_sigmoid(z) + suffix, 0.0)
        w = jnp.where(mask, jnp.exp(log_w), 0.0)
        return jnp.einsum('bhts,bhsd->bhtd', w, vh)

    out = lax.map(block, (q_blocks, jnp.arange(nblk)))
    return jnp.transpose(out, (1, 0, 3, 2, 4)).reshape(B, S, H * Dh)


def causal_depthwise_conv(h, w, b):
    S = h.shape[1]
    hp = jnp.pad(h, ((0, 0), (CONV_W - 1, 0), (0, 0)))
    out = b
    for j in range(CONV_W):
        out = out + hp[:, j:j + S] * w[j]
    return out


def hybrid_layer(x, p_i, lb, w_in, hg_norm_g, w_a, w_b, w_out, ln1_g, ln1_b,
                 w_up, conv_w, conv_b, w_down, w_pe, w_pg, ln2_g, ln2_b):
    B, S, _ = x.shape
    dt = x.dtype
    proj = x @ w_in
    qa, fa, ia, ga, qb, kb, vb, gate_a, gate_b = jnp.split(proj, SPLIT_IDX, axis=-1)

    oa = hgrn2_recurrence(qa.reshape(B, S, HG_HEADS, HG_DK), fa.reshape(B, S, HG_HEADS, HG_DK),
                          ia.reshape(B, S, HG_HEADS, HG_DV), lb.reshape(HG_HEADS, HG_DK))
    oa = rms_norm(oa, hg_norm_g.reshape(HG_HEADS, HG_DV)) * jax.nn.silu(
        ga.reshape(B, S, HG_HEADS, HG_DV).astype(jnp.float32))
    ya = oa.reshape(B, S, HG_WIDTH).astype(dt) @ w_a

    ob = stick_breaking_attention(qb.reshape(B, S, SB_HEADS, SB_DH), kb.reshape(B, S, SB_HEADS, SB_DH),
                                  vb.reshape(B, S, SB_HEADS, SB_DH))
    yb = ob.astype(dt) @ w_b

    merged = jax.nn.sigmoid(gate_a) * ya + jax.nn.sigmoid(gate_b) * yb
    x1 = layer_norm(DN_ALPHA * x + merged @ w_out, ln1_g, ln1_b)

    up = causal_depthwise_conv(x1 @ w_up, conv_w, conv_b)
    c_val, c_gate = jnp.split(up, 2, axis=-1)
    ffn = (jax.nn.gelu(c_gate) * c_val) @ w_down

    ple = (p_i @ w_pe) * jax.nn.sigmoid(x1 @ w_pg)
    return layer_norm(DN_ALPHA * x1 + ffn + ple, ln2_g, ln2_b)


def setup_inputs(seed: int = 0) -> dict:
    key = jax.random.key(seed)
    ks = jax.random.split(key, 20)
    f32 = jnp.float32
    nrm = lambda k, shape, s: jax.random.normal(k, shape, f32) * s
    return {
        "x": nrm(ks[0], (BATCH, SEQ, D_MODEL), 1.0),
        "p": nrm(ks[1], (DEPTH, BATCH, SEQ, PLE_DIM), 1.0),
        "lb_logits": 1.0 + nrm(ks[2], (DEPTH, HG_HEADS * HG_DK), 0.1),
        "w_in": nrm(ks[3], (DEPTH, D_MODEL, IN_COLS), D_MODEL ** -0.5),
        "hg_norm_g": 1.0 + nrm(ks[4], (DEPTH, HG_WIDTH), 0.02),
        "w_a": nrm(ks[5], (DEPTH, HG_WIDTH, D_MODEL), HG_WIDTH ** -0.5),
        "w_b": nrm(ks[6], (DEPTH, SB_WIDTH, D_MODEL), SB_WIDTH ** -0.5),
        "w_out": nrm(ks[7], (DEPTH, D_MODEL, D_MODEL), DN_BETA * D_MODEL ** -0.5),
        "ln1_g": 1.0 + nrm(ks[8], (DEPTH, D_MODEL), 0.02),
        "ln1_b": nrm(ks[9], (DEPTH, D_MODEL), 0.02),
        "w_up": nrm(ks[10], (DEPTH, D_MODEL, 2 * D_FF), D_MODEL ** -0.5),
        "conv_w": nrm(ks[11], (DEPTH, CONV_W, 2 * D_FF), CONV_W ** -0.5),
        "conv_b": nrm(ks[12], (DEPTH, 2 * D_FF), 0.02),
        "w_down": nrm(ks[13], (DEPTH, D_FF, D_MODEL), DN_BETA * D_FF ** -0.5),
        "w_pe": nrm(ks[14], (DEPTH, PLE_DIM, D_MODEL), PLE_DIM ** -0.5),
        "w_pg": nrm(ks[15], (DEPTH, D_MODEL, D_MODEL), D_MODEL ** -0.5),
        "ln2_g": 1.0 + nrm(ks[16], (DEPTH, D_MODEL), 0.02),
        "ln2_b": nrm(ks[17], (DEPTH, D_MODEL), 0.02),
    }


def reference(x, p, lb_logits, w_in, hg_norm_g, w_a, w_b, w_out, ln1_g, ln1_b,
              w_up, conv_w, conv_b, w_down, w_pe, w_pg, ln2_g, ln2_b):
    sm = jax.nn.softmax(lb_logits.astype(jnp.float32), axis=0)
    lower_bounds = jnp.cumsum(sm, axis=0) - sm[0]
    h = x
    for i in range(DEPTH):
        h = hybrid_layer(h, p[i], lower_bounds[i], w_in[i], hg_norm_g[i], w_a[i], w_b[i], w_out[i],
                         ln1_g[i], ln1_b[i], w_up[i], conv_w[i], conv_b[i], w_down[i],
                         w_pe[i], w_pg[i], ln2_g[i], ln2_b[i])
    return h
```

```python
import contextlib
import types
import numpy as np
import concourse.bass as bass
import concourse.mybir as mybir
from concourse.bass_utils import run_bass_kernel_spmd

F32 = mybir.dt.float32
BF16 = mybir.dt.bfloat16
AF = mybir.ActivationFunctionType
ALU = mybir.AluOpType

D_MODEL = 1024
SEQ = 16384
DEPTH = 2
NCORES = 8
D_FF = 2816
DN_ALPHA = (2 * DEPTH) ** 0.25
LN_EPS = 1e-5
RMS_EPS = 1e-6
F_FLOOR = 1e-30


class Sched:
    WIN = 512

    def __init__(self, nc, es):
        self.nc = nc
        self.es = es
        self.engs = {"pe": nc.tensor, "act": nc.scalar, "dve": nc.vector, "pool": nc.gpsimd, "sp": nc.sync}
        self.ops = []
        self.seq = {}
        self.last_w = {}
        self.readers = {}
        self.out_ids = []

    def _deps(self, reads, writes):
        d = []
        for b in reads:
            if b in self.last_w:
                d.append(self.last_w[b])
        for b in writes:
            if b in self.last_w:
                d.append(self.last_w[b])
            d.extend(self.readers.get(b, []))
        return d

    def _commit(self, i, reads, writes):
        for b in reads:
            self.readers.setdefault(b, []).append(i)
        for b in writes:
            self.last_w[b] = i
            self.readers[b] = []

    def _add(self, kind, e, payload, stream, reads, writes):
        deps = self._deps(reads, writes)
        sq = self.seq.get(stream, 0)
        self.seq[stream] = sq + 1
        i = len(self.ops)
        self.ops.append((kind, e, payload, deps, stream, sq))
        self._commit(i, reads, writes)
        return i

    @staticmethod
    def _freeze(fn):
        if fn.__closure__ is None:
            return fn
        cells = []
        for c in fn.__closure__:
            try:
                cells.append(types.CellType(c.cell_contents))
            except ValueError:
                cells.append(c)
        return types.FunctionType(fn.__code__, fn.__globals__, fn.__name__, fn.__defaults__, tuple(cells))

    def op(self, e, fn, reads=(), writes=()):
        return self._add("op", e, self._freeze(fn), e, reads, writes)

    def dma(self, q, out, in_, key, reads=(), writes=(), is_output=False):
        i = self._add("dma", q, (out, in_), "dma_" + key, reads, writes)
        if is_output:
            self.out_ids.append(i)
        return i

    def finish(self):
        ops = self.ops
        n = len(ops)
        seen = {e: {} for e in self.engs}
        waits = [[] for _ in range(n)]
        needs = [False] * n
        for i, (kind, e, payload, deps, stream, sq) in enumerate(ops):
            best = {}
            for d in deps:
                ds, dq = ops[d][4], ops[d][5]
                if e == "pe" and ds == "pe":
                    continue
                if seen[e].get(ds, -1) >= dq:
                    continue
                if ds not in best or best[ds][1] < dq:
                    best[ds] = (d, dq)
            for ds, (d, dq) in best.items():
                waits[i].append(d)
                seen[e][ds] = dq
                needs[d] = True
        final_waits = []
        lastout = {}
        for i in self.out_ids:
            lastout[ops[i][4]] = i
        for i in ops and self.out_ids:
            needs[i] = True
        cnt = {}
        sems = {}
        tok = [None] * n
        for i, (kind, e, payload, deps, stream, sq) in enumerate(ops):
            if kind == "dma" or needs[i]:
                step = 16 if kind == "dma" else 1
                c = cnt.get(stream, 0) + step
                cnt[stream] = c
                w = (c - 1) // self.WIN
                if (stream, w) not in sems:
                    sems[(stream, w)] = self.es.enter_context(self.nc.semaphore(f"s_{stream}_{w}"))
                tok[i] = (sems[(stream, w)], c - w * self.WIN, step)
        for i, (kind, e, payload, deps, stream, sq) in enumerate(ops):
            eng = self.engs[e]
            for d in waits[i]:
                eng.wait_ge(tok[d][0], tok[d][1])
            if kind == "dma":
                ins = eng.dma_start(out=payload[0], in_=payload[1])
            else:
                ins = payload(eng)
            if tok[i] is not None:
                ins.then_inc(tok[i][0], tok[i][2])
        done = set()
        for i in reversed(self.out_ids):
            key = id(tok[i][0])
            if key in done:
                continue
            done.add(key)
            self.engs["sp"].wait_ge(tok[i][0], tok[i][1])


C_ID, C_J, C_A, C_A2, C_SH, C_SH2, C_NEG, C_MD, C_MH, C_RST = 0, 128, 256, 384, 512, 640, 768, 896, 1024, 1088
CW_M = 1600


def consts_M():
    c = np.zeros((128, CW_M), np.float32)
    i = np.arange(128)
    c[i, C_ID + i] = 1.0
    c[i, C_J + (127 - i)] = 1.0
    for j in range(128):
        if 126 - j >= 0:
            c[126 - j, C_A + j] += 1.0
        c[127 - j, C_A + j] -= 1.0
    c[127, C_A2 + 127] = 1.0
    for t in range(1, 128):
        c[t - 1, C_SH + t] = 1.0
    c[127, C_SH2 + 0] = 1.0
    ii, jj = np.meshgrid(i, i, indexing="ij")
    c[:, C_NEG:C_NEG + 128] = np.where(ii + jj <= 127, -30000.0, 0.0)
    c[:, C_MD:C_MD + 128] = np.where(ii + jj >= 128, 1.0, 0.0)
    s64 = np.arange(64)
    ss, tt = np.meshgrid(s64, s64, indexing="ij")
    c[0:64, C_MH:C_MH + 64] = np.where(ss <= tt, 1.0, 0.0)
    r = np.ones(512, np.float32)
    r[0::64] = 0.0
    c[:, C_RST:C_RST + 512] = r[None, :]
    return c


def build_M(S, layer, upto=9):
    NQB = S // 128
    NTT = S // 512
    nc = bass.Bass("TRN2", target_bir_lowering=False)
    xT = nc.dram_tensor("xT", [D_MODEL, S], F32, kind="ExternalInput").ap()
    wM = nc.dram_tensor("wM", [128, 8, 512], F32, kind="ExternalInput").ap()
    lbl = nc.dram_tensor("lbl", [128, DEPTH], F32, kind="ExternalInput").ap()
    cst = nc.dram_tensor("cst", [128, CW_M], F32, kind="ExternalInput").ap()
    ob_o = nc.dram_tensor("ob", [S, 64], F32, kind="ExternalOutput").ap()
    oaT_o = nc.dram_tensor("oaT", [64, S], F32, kind="ExternalOutput").ap()
    xT_v = xT.rearrange("(kc p) t -> p kc t", p=128)

    with contextlib.ExitStack() as es:
        def sb(name, shape, dt):
            return es.enter_context(nc.sbuf_tensor(name, shape, dt))

        def ps(name, shape, dt):
            return es.enter_context(nc.psum_tensor(name, shape, dt))

        cf = sb("cf", [128, CW_M], F32)
        cb = sb("cb", [128, CW_M], BF16)
        wb = sb("wb", [128, 8, 512], BF16)
        QT = sb("QT", [128, S], BF16)
        KTr = sb("KTr", [128, S], BF16)
        DVr = sb("DVr", [128, NQB, 64], BF16)
        Vf = sb("Vf", [128, NQB, 64], F32)
        xb = [sb(f"xb{i}", [128, 8, 512], BF16) for i in range(2)]
        Kb = sb("Kb", [128, 4, 64], BF16)
        vhb = sb("vhb", [64, 8, 64], BF16)
        zeros = sb("zeros", [128, 1024], F32)
        lbt = sb("lbt", [128, 16], F32)
        bb = sb("bb", [128, 512], F32)
        e1 = sb("e1", [128, 512], F32)
        e2 = sb("e2", [128, 512], F32)
        qp = sb("qp", [128, 512], BF16)
        kp = sb("kp", [128, 512], BF16)
        sc8 = sb("sc8", [128, 64], F32)
        state = sb("state", [128, 64], F32)
        srb = [sb(f"srb{i}", [128, 64], BF16) for i in range(2)]
        scm = [sb(f"scm{i}", [64, 64], BF16) for i in range(2)]
        kTs = [sb(f"kTs{i}", [64, 128], BF16) for i in range(2)]
        utmp = sb("utmp", [128, 64], F32)
        oas = [sb(f"oas{i}", [64, 512], F32) for i in range(2)]
        omb = [sb(f"omb{i}", [128, 1024], F32) for i in range(2)]
        Eb = [sb(f"Eb{i}", [128, 1024], BF16) for i in range(4)]
        ETs = [sb(f"ETs{i}", [128, 1024], BF16) for i in range(2)]
        qs, sg, ff, kk = omb[0][:, 0:512], omb[0][:, 512:1024], omb[1][:, 0:512], omb[1][:, 512:1024]
        gg = ff
        obs = [sb(f"obs{i}", [128, 64], F32) for i in range(2)]
        Zp = [ps(f"Zp{i}", [128, 1024], F32) for i in range(2)]
        Ap = [ps(f"Ap{i}", [128, 512], F32) for i in range(2)]
        Fp = [Zp[0][:, 0:512], Zp[0][:, 512:1024], Zp[1][:, 0:512], Ap[0][:, :], Ap[1][:, :], Zp[1][:, 512:1024]]
        ET32 = ps("ET32", [128, 1024], F32)

        identb = cb[:, C_ID:C_ID + 128]
        Jb = cb[:, C_J:C_J + 128]
        NEGb = cb[:, C_NEG:C_NEG + 128]
        MDb = cb[:, C_MD:C_MD + 128]
        Af = cf[:, C_A:C_A + 128]
        A2f = cf[:, C_A2:C_A2 + 128]
        SHf = cf[:, C_SH:C_SH + 128]
        SH2f = cf[:, C_SH2:C_SH2 + 128]
        MHf = cf[0:64, C_MH:C_MH + 64]
        RSTf = cf[:, C_RST:C_RST + 512]

        with nc.Block() as block:
            @block.sync
            def _(_e):
                sc = Sched(nc, es)
                sc.dma("sp", cf[:], cst[:], "cf", writes=["cf"])
                sc.dma("pool", cb[:], cst[:], "cb", writes=["cb"])
                sc.dma("pool", wb[:], wM[:], "wb", writes=["wb"])
                sc.dma("sp", lbt[:, 0:DEPTH], lbl[:], "lbt", writes=["lbt"])
                sc.op("dve", lambda e: e.memset(zeros[:], 0.0), writes=["zeros"])
                sc.op("dve", lambda e: e.memset(state[:], 0.0), writes=["state"])
                sc.op("pool", lambda e: e.memset(QT[64:128, :], 0.0), writes=["QTpad"])
                sc.op("pool", lambda e: e.memset(KTr[64:128, :], 0.0), writes=["KTpad"])
                LB, OM, NOM = lbt[:, 8:9], lbt[:, 9:10], lbt[:, 10:11]
                mx = lbt[:, 4:5]
                sc.op("dve", lambda e: e.tensor_reduce(out=mx, in_=lbt[:, 0:DEPTH], axis=mybir.AxisListType.X, op=ALU.max),
                      reads=["lbt"], writes=["lb_mx"])
                sc.op("dve", lambda e: e.tensor_scalar(out=lbt[:, 5:6], in0=mx, scalar1=-1.0, scalar2=None, op0=ALU.mult),
                      reads=["lb_mx"], writes=["lb_nmx"])
                sc.op("act", lambda e: e.activation(out=lbt[:, 2:2 + DEPTH], in_=lbt[:, 0:DEPTH], func=AF.Exp,
                                                    bias=lbt[:, 5:6], scale=1.0),
                      reads=["lbt", "lb_nmx"], writes=["lb_e"])
                sc.op("dve", lambda e: e.tensor_reduce(out=lbt[:, 6:7], in_=lbt[:, 2:2 + DEPTH], axis=mybir.AxisListType.X, op=ALU.add),
                      reads=["lb_e"], writes=["lb_s"])
                sc.op("dve", lambda e: e.reciprocal(out=lbt[:, 7:8], in_=lbt[:, 6:7]), reads=["lb_s"], writes=["lb_r"])
                if layer == 0:
                    sc.op("dve", lambda e: e.memset(LB, 0.0), reads=["lb_r"], writes=["lb"])
                else:
                    sc.op("dve", lambda e: e.tensor_reduce(out=lbt[:, 11:12], in_=lbt[:, 3:2 + layer + 1],
                                                           axis=mybir.AxisListType.X, op=ALU.add),
                          reads=["lb_e"], writes=["lb_p"])
                    sc.op("dve", lambda e: e.tensor_tensor(out=LB, in0=lbt[:, 11:12], in1=lbt[:, 7:8], op=ALU.mult),
                          reads=["lb_p", "lb_r"], writes=["lb"])
                sc.op("dve", lambda e: e.tensor_scalar(out=OM, in0=LB, scalar1=-1.0, scalar2=1.0, op0=ALU.mult, op1=ALU.add),
                      reads=["lb"], writes=["om"])
                sc.op("dve", lambda e: e.tensor_scalar(out=NOM, in0=OM, scalar1=-1.0, scalar2=None, op0=ALU.mult),
                      reads=["om"], writes=["nom"])

                for tt in range(NTT if upto >= 1 else 0):
                    X = xb[tt % 2]
                    xk = f"xb{tt % 2}"
                    sc.dma("pool", X[:], xT_v[:, :, tt * 512:(tt + 1) * 512], xk, writes=[xk])
                    for oi, (bank, c0) in enumerate([(0, 0), (1, 128), (2, 256)]):
                        for kc in range(8):
                            sc.op("pe", lambda e, bank=bank, c0=c0, kc=kc: e.matmul(
                                Fp[bank][:, :], lhsT=wb[:, kc, c0:c0 + 128], rhs=X[:, kc, :], start=(kc == 0), stop=(kc == 7)),
                                reads=[xk, "wb"], writes=[f"F{bank}"])
                    if upto < 1.05:
                        sc.op("act", lambda e: e.activation(out=QT[0:64, tt * 512:(tt + 1) * 512], in_=Fp[2][0:64, :], func=AF.Copy),
                              reads=["F2"], writes=[f"QT{tt}"])
                        continue
                    for blk in range(4):
                        for kc in range(8):
                            sc.op("pe", lambda e, blk=blk, kc=kc: e.matmul(
                                Fp[3][:, blk * 128:(blk + 1) * 128], lhsT=X[:, kc, blk * 128:(blk + 1) * 128],
                                rhs=wb[:, kc, 320:448], start=(kc == 0), stop=(kc == 7)),
                                reads=[xk, "wb"], writes=["F3", "scT0", "scT1", "scT2", "scT3"])
                    for c in range(8 if upto >= 1.2 else 0):
                        for kc in range(8):
                            sc.op("pe", lambda e, c=c, kc=kc: e.matmul(
                                Fp[4][0:64, c * 64:(c + 1) * 64], lhsT=X[:, kc, c * 64:(c + 1) * 64],
                                rhs=wb[:, kc, 448:512], start=(kc == 0), stop=(kc == 7)),
                                reads=[xk, "wb"], writes=["F4"])
                    sc.op("act", lambda e: e.activation(out=QT[0:64, tt * 512:(tt + 1) * 512], in_=Fp[2][0:64, :], func=AF.Copy),
                          reads=["F2"], writes=[f"QT{tt}"])
                    F3v = Fp[3][:, :].rearrange("p (b c) -> p b c", b=4)
                    if upto >= 1.12:
                        sc.op("dve", lambda e: e.tensor_copy(out=Kb[:, :, :], in_=F3v[:, :, 0:64]), reads=["F3"], writes=["Kb"])
                    if upto >= 1.14:
                        sc.op("dve", lambda e: e.tensor_copy(out=Vf[:, tt * 4:(tt + 1) * 4, :], in_=F3v[:, :, 64:128]),
                              reads=["F3"], writes=[f"Vf{tt}"])
                    if upto >= 1.2:
                        sc.op("dve", lambda e: e.tensor_copy(out=vhb[:, :, :], in_=Fp[4][0:64, :].rearrange("p (c v) -> p c v", c=8)),
                              reads=["F4"], writes=["vhb"])
                    if upto < 1.3:
                        continue
                    for blk in range(4):
                        sc.op("pe", lambda e, blk=blk: e.matmul(
                            Fp[2][0:64, (3 - blk) * 128:(4 - blk) * 128], lhsT=Kb[:, blk, :], rhs=Jb, start=True, stop=True),
                            reads=["Kb", "cb"], writes=["F2"])
                    rb0 = NQB - 1 - (tt * 4 + 3)
                    sc.op("act", lambda e: e.activation(out=KTr[0:64, rb0 * 128:(rb0 + 4) * 128], in_=Fp[2][0:64, :], func=AF.Copy),
                          reads=["F2"], writes=[f"KTr{tt}"])
                    if upto < 1.5:
                        continue
                    for blk in range(4):
                        bs = tt * 4 + blk
                        sc.op("pe", lambda e, blk=blk, bs=bs: e.matmul(
                            Fp[3][:, (3 - blk) * 64:(4 - blk) * 64], lhsT=Af, rhs=Vf[:, bs, :], start=True, stop=(bs == 0)),
                            reads=[f"Vf{tt}", "cf"], writes=["F3"])
                        if bs > 0:
                            sc.op("pe", lambda e, blk=blk, bs=bs: e.matmul(
                                Fp[3][:, (3 - blk) * 64:(4 - blk) * 64], lhsT=A2f, rhs=Vf[:, bs - 1, :], start=False, stop=True),
                                reads=[f"Vf{(bs - 1) // 4}", "cf"], writes=["F3"])
                    sc.op("dve", lambda e: e.tensor_copy(out=DVr[:, rb0:rb0 + 4, :],
                                                         in_=Fp[3][:, 0:256].rearrange("p (b c) -> p b c", b=4)),
                          reads=["F3"], writes=[f"DVr{tt}"])

                    if upto < 2:
                        continue
                    sc.op("act", lambda e: e.activation(out=qs[:], in_=Fp[0][:, :], func=AF.Silu), reads=["F0"], writes=["qs"])
                    sc.op("act", lambda e: e.activation(out=sg[:], in_=Fp[1][:, :], func=AF.Sigmoid), reads=["F1"], writes=["sg"])
                    sc.op("dve", lambda e: e.tensor_scalar(out=ff[:], in0=sg[:], scalar1=OM, scalar2=LB, op0=ALU.mult, op1=ALU.add),
                          reads=["sg", "om", "lb"], writes=["ff"])
                    sc.op("dve", lambda e: e.tensor_scalar(out=ff[:], in0=ff[:], scalar1=F_FLOOR, scalar2=None, op0=ALU.max),
                          reads=["ff"], writes=["ff"])
                    sc.op("act", lambda e: e.activation(out=ff[:], in_=ff[:], func=AF.Ln), reads=["ff"], writes=["ff"])
                    sc.op("dve", lambda e: e.tensor_scalar(out=kk[:], in0=sg[:], scalar1=NOM, scalar2=OM, op0=ALU.mult, op1=ALU.add),
                          reads=["sg", "nom", "om"], writes=["kk"])
                    sc.op("dve", lambda e: e.tensor_tensor_scan(out=bb[:], data0=RSTf, data1=gg[:], initial=0.0,
                                                                op0=ALU.mult, op1=ALU.add),
                          reads=["ff", "cf"], writes=["bb"])
                    bbv = bb[:].rearrange("p (c t) -> p c t", c=8)
                    BM, D8, BL, NBM = sc8[:, 0:8], sc8[:, 8:16], sc8[:, 16:24], sc8[:, 24:32]
                    sc.op("dve", lambda e: e.tensor_copy(out=BM, in_=bbv[:, :, 31]), reads=["bb"], writes=["bm"])
                    sc.op("dve", lambda e: e.tensor_copy(out=BL, in_=bbv[:, :, 63]), reads=["bb"], writes=["bl"])
                    sc.op("dve", lambda e: e.tensor_scalar(out=NBM, in0=BM, scalar1=-1.0, scalar2=None, op0=ALU.mult),
                          reads=["bm"], writes=["nbm"])
                    sc.op("dve", lambda e: e.tensor_tensor(out=D8, in0=BL, in1=BM, op=ALU.subtract),
                          reads=["bl", "bm"], writes=["d8"])
                    sc.op("act", lambda e: e.activation(out=sc8[:, 32:56], in_=sc8[:, 0:24], func=AF.Exp),
                          reads=["bm", "d8", "bl"], writes=["er", "c2", "el"])
                    for c in range(8):
                        sl = slice(c * 64, (c + 1) * 64)
                        sc.op("act", lambda e, c=c, sl=sl: e.activation(out=e1[:, sl], in_=bb[:, sl], func=AF.Exp,
                                                                        bias=sc8[:, 24 + c:25 + c], scale=1.0),
                              reads=["bb", "nbm"], writes=["e1"])
                        sc.op("act", lambda e, c=c, sl=sl: e.activation(out=e2[:, sl], in_=bb[:, sl], func=AF.Exp,
                                                                        bias=sc8[:, c:c + 1], scale=-1.0),
                              reads=["bb", "bm"], writes=["e2"])
                    sc.op("dve", lambda e: e.tensor_tensor(out=qp[:], in0=qs[:], in1=e1[:], op=ALU.mult),
                          reads=["qs", "e1"], writes=["qp"])
                    sc.op("dve", lambda e: e.tensor_tensor(out=kp[:], in0=kk[:], in1=e2[:], op=ALU.mult),
                          reads=["kk", "e2"], writes=["kp"])
                    for c in range(8):
                        sl = slice(c * 64, (c + 1) * 64)
                        gi = tt * 8 + c
                        s2, s4 = gi % 2, c % 4
                        sc.op("dve", lambda e, c=c, s2=s2: e.tensor_scalar(out=srb[s2][:], in0=state[:], scalar1=sc8[:, 32 + c:33 + c],
                                                                            scalar2=None, op0=ALU.mult),
                              reads=["state", "er"], writes=[f"srb{s2}"])
                        scT = Fp[3][0:64, 256 + s4 * 64:256 + (s4 + 1) * 64]
                        Us = Fp[5][:, s4 * 64:(s4 + 1) * 64]
                        kTp = ET32[0:64, c * 128:(c + 1) * 128]
                        sc.op("pe", lambda e, sl=sl, scT=scT: e.matmul(scT, lhsT=kp[:, sl], rhs=qp[:, sl], start=True, stop=True),
                              reads=["kp", "qp"], writes=[f"scT{s4}"])
                        sc.op("pe", lambda e, sl=sl, kTp=kTp: e.matmul(kTp, lhsT=kp[:, sl], rhs=identb, start=True, stop=True),
                              reads=["kp", "cb"], writes=[f"kTp{c}"])
                        sc.op("dve", lambda e, s2=s2, scT=scT: e.tensor_tensor(out=scm[s2][:], in0=scT, in1=MHf, op=ALU.mult),
                              reads=[f"scT{s4}", "cf"], writes=[f"scm{s2}"])
                        sc.op("act", lambda e, s2=s2, kTp=kTp: e.activation(out=kTs[s2][:], in_=kTp, func=AF.Copy),
                              reads=[f"kTp{c}"], writes=[f"kTs{s2}"])
                        sc.op("pe", lambda e, c=c, s2=s2, Us=Us: e.matmul(Us, lhsT=kTs[s2][:], rhs=vhb[:, c, :], start=True, stop=True),
                              reads=[f"kTs{s2}", "vhb"], writes=["F5"])
                        sc.op("pe", lambda e, sl=sl, s2=s2: e.matmul(Fp[4][0:64, sl], lhsT=srb[s2][:], rhs=qp[:, sl], start=True, stop=False),
                              reads=[f"srb{s2}", "qp"], writes=["F4"])
                        sc.op("pe", lambda e, c=c, sl=sl, s2=s2: e.matmul(Fp[4][0:64, sl], lhsT=vhb[:, c, :], rhs=scm[s2][:], start=False, stop=True),
                              reads=[f"scm{s2}", "vhb"], writes=["F4"])
                        sc.op("dve", lambda e, c=c, Us=Us: e.tensor_scalar(out=utmp[:], in0=Us, scalar1=sc8[:, 40 + c:41 + c], scalar2=None,
                                                                            op0=ALU.mult),
                              reads=["F5", "c2"], writes=["utmp"])
                        sc.op("dve", lambda e, c=c: e.scalar_tensor_tensor(out=state[:], in0=state[:], scalar=sc8[:, 48 + c:49 + c],
                                                                           in1=utmp[:], op0=ALU.mult, op1=ALU.add),
                              reads=["state", "utmp", "el"], writes=["state"])
                    ok = f"oas{tt % 2}"
                    sc.op("act", lambda e: e.activation(out=oas[tt % 2][:], in_=Fp[4][0:64, :], func=AF.Copy), reads=["F4"], writes=[ok])
                    sc.dma("sp", oaT_o[:, tt * 512:(tt + 1) * 512], oas[tt % 2][:], f"oaT_out{tt % 2}", reads=[ok], is_output=True)

                TWK = 1024
                def qb_tiles(qb):
                    rstart = (NQB - 1 - qb) * 128
                    L = S - rstart
                    nt = (L + TWK - 1) // TWK
                    return [(qb, ti, rstart + ti * TWK, min(TWK, S - rstart - ti * TWK), ti == nt - 1) for ti in range(nt)]

                tiles = []
                prev_of = []
                for q0 in range(0, NQB, 2):
                    la, lb_ = qb_tiles(q0), (qb_tiles(q0 + 1) if q0 + 1 < NQB else [])
                    last_idx = {}
                    for i in range(max(len(la), len(lb_))):
                        for lst in (la, lb_):
                            if i < len(lst):
                                t = lst[i]
                                prev_of.append(last_idx.get(t[0], -1))
                                last_idx[t[0]] = len(tiles)
                                tiles.append(t)
                if upto < 3:
                    tiles = []
                NT = len(tiles)
                qt_keys = [f"QT{t}" for t in range(NTT)] + ["QTpad"]
                kt_keys = [f"KTr{t}" for t in range(NTT)] + ["KTpad"]
                dv_keys = [f"DVr{t}" for t in range(NTT)]
                vf_keys = [f"Vf{t}" for t in range(NTT)]
                zkeys = [["F0", "F1"], ["F2", "F5"]]
                hg_alias = [["qs", "sg"], ["ff", "kk"]]

                def stA(n):
                    qb, ti, r0, w, last = tiles[n]
                    z = Zp[n % 2]
                    for h0 in range(0, w, 512):
                        wh = min(512, w - h0)
                        zk = zkeys[n % 2][h0 // 512]
                        diag = (ti == 0 and h0 == 0)
                        sc.op("pe", lambda e: e.matmul(z[:, h0:h0 + wh], lhsT=QT[:, qb * 128:(qb + 1) * 128], rhs=KTr[:, r0 + h0:r0 + h0 + wh],
                                                       start=True, stop=(not diag)),
                              reads=qt_keys + kt_keys, writes=[zk])
                        if diag:
                            sc.op("pe", lambda e: e.matmul(z[:, 0:128], lhsT=identb, rhs=NEGb, start=False, stop=True),
                                  reads=["cb"], writes=[zk])

                def stB(n):
                    qb, ti, r0, w, last = tiles[n]
                    sc.op("act", lambda e: e.activation(out=omb[n % 2][:, 0:w], in_=Zp[n % 2][:, 0:w], func=AF.Sigmoid, scale=-0.125),
                          reads=zkeys[n % 2], writes=[f"omb{n % 2}"] + hg_alias[n % 2])

                def stC(n):
                    qb, ti, r0, w, last = tiles[n]
                    if ti == 0:
                        init = 1.0
                        rd = []
                    else:
                        pn = prev_of[n]
                        wp = tiles[pn][3]
                        init = Eb[pn % 4][:, wp - 1:wp]
                        rd = [f"Eb{pn % 4}"]
                    sc.op("dve", lambda e: e.tensor_tensor_scan(out=Eb[n % 4][:, 0:w], data0=omb[n % 2][:, 0:w], data1=zeros[:, 0:w],
                                                                initial=init, op0=ALU.mult, op1=ALU.add),
                          reads=[f"omb{n % 2}", "zeros"] + rd, writes=[f"Eb{n % 4}"])

                def stD(n):
                    qb, ti, r0, w, last = tiles[n]
                    for sbk in range(w // 128):
                        sl = slice(sbk * 128, (sbk + 1) * 128)
                        sc.op("pe", lambda e: e.matmul(ET32[:, sl], lhsT=Eb[n % 4][:, sl], rhs=identb, start=True, stop=True),
                              reads=[f"Eb{n % 4}", "cb"], writes=["B0"] + ([f"kTp{c}" for c in range(8)] if n < 2 else []))

                def stF(n):
                    qb, ti, r0, w, last = tiles[n]
                    sc.op("act", lambda e: e.activation(out=ETs[n % 2][:, 0:w], in_=ET32[:, 0:w], func=AF.Copy),
                          reads=["B0"], writes=[f"ETs{n % 2}"])
                    if ti == 0:
                        sc.op("dve", lambda e: e.tensor_tensor(out=ETs[n % 2][:, 0:128], in0=ETs[n % 2][:, 0:128], in1=MDb, op=ALU.mult),
                              reads=[f"ETs{n % 2}", "cb"], writes=[f"ETs{n % 2}"])

                def stG(n):
                    qb, ti, r0, w, last = tiles[n]
                    acc = Ap[qb % 2]
                    ak = f"F{3 + qb % 2}"
                    for sbk in range(w // 128):
                        sl = slice(sbk * 128, (sbk + 1) * 128)
                        br = r0 // 128 + sbk
                        sc.op("pe", lambda e: e.matmul(acc[:, 0:64], lhsT=ETs[n % 2][:, sl], rhs=DVr[:, br, :],
                                                       start=(ti == 0 and sbk == 0), stop=False),
                              reads=[f"ETs{n % 2}"] + dv_keys, writes=[ak])
                    if last:
                        sc.op("pe", lambda e: e.matmul(acc[:, 0:64], lhsT=SHf, rhs=Vf[:, qb, :], start=False, stop=(qb == 0)),
                              reads=["cf"] + vf_keys, writes=[ak])
                        if qb > 0:
                            sc.op("pe", lambda e: e.matmul(acc[:, 0:64], lhsT=SH2f, rhs=Vf[:, qb - 1, :], start=False, stop=True),
                                  reads=["cf"] + vf_keys, writes=[ak])
                        sc.op("dve", lambda e: e.tensor_copy(out=obs[qb % 2][:], in_=acc[:, 0:64]), reads=[ak], writes=[f"obs{qb % 2}"])
                        sc.dma("sp", ob_o[qb * 128:(qb + 1) * 128, :], obs[qb % 2][:], f"ob_out{qb % 2}", reads=[f"obs{qb % 2}"], is_output=True)

                for n in range(-2, NT + 1):
                    if 0 <= n + 2 < NT:
                        stA(n + 2)
                    if 0 <= n < NT and upto >= 3.3:
                        stC(n)
                    if 0 <= n + 2 < NT and upto >= 3.2:
                        stB(n + 2)
                    if 0 <= n < NT:
                        if upto >= 3.4:
                            stD(n)
                        if upto >= 3.5:
                            stF(n)
                    if 0 <= n - 1 < NT and upto >= 3.6:
                        stG(n - 1)
                sc.finish()
    return nc


def prep_M_weights(w_in_l, core):
    hh, vh = core // 2, core % 2
    cols = np.concatenate([
        np.arange(hh * 128, (hh + 1) * 128),
        512 + np.arange(hh * 128, (hh + 1) * 128),
        2048 + np.arange(core * 64, (core + 1) * 64),
        2560 + np.arange(core * 64, (core + 1) * 64),
        3072 + np.arange(core * 64, (core + 1) * 64),
        1024 + hh * 128 + vh * 64 + np.arange(64),
    ])
    w = w_in_l[:, cols]
    return np.ascontiguousarray(w.reshape(8, 128, 512).transpose(1, 0, 2))


NTOK_T = 2050
TW = 410
V_HG, V_L1G, V_L1B, V_CW0, V_CW1, V_CW2, V_CB, V_L2G, V_L2B, V_HM, NV_T = 0, 4, 12, 20, 64, 108, 152, 196, 204, 212, 213


def build_T():
    BW = 1025
    SUBS = [(0, 410), (410, 410), (820, 205)]
    NMAX = 410
    nc = bass.Bass("TRN2", target_bir_lowering=False)
    xTd = nc.dram_tensor("xTc", [D_MODEL, NTOK_T], F32, kind="ExternalInput").ap()
    obd = nc.dram_tensor("obTc", [512, NTOK_T], F32, kind="ExternalInput").ap()
    oad = nc.dram_tensor("oaTc", [512, NTOK_T], F32, kind="ExternalInput").ap()
    pd = nc.dram_tensor("pTc", [256, NTOK_T], F32, kind="ExternalInput").ap()
    vecd = nc.dram_tensor("vec", [128, NV_T], F32, kind="ExternalInput").ap()
    Wg = nc.dram_tensor("Wg", [20, 128, 8, 128], F32, kind="ExternalInput").ap()
    Wa = nc.dram_tensor("Wa", [8, 128, 4, 128], F32, kind="ExternalInput").ap()
    Wb = nc.dram_tensor("Wb", [8, 128, 4, 128], F32, kind="ExternalInput").ap()
    Wo = nc.dram_tensor("Wo", [8, 128, 8, 128], F32, kind="ExternalInput").ap()
    Wu = nc.dram_tensor("Wu", [44, 128, 8, 128], F32, kind="ExternalInput").ap()
    Wd = nc.dram_tensor("Wd", [8, 128, 22, 128], F32, kind="ExternalInput").ap()
    Wpe = nc.dram_tensor("Wpe", [8, 128, 2, 128], F32, kind="ExternalInput").ap()
    Wpg = nc.dram_tensor("Wpg", [8, 128, 8, 128], F32, kind="ExternalInput").ap()
    outd = nc.dram_tensor("outT", [D_MODEL, 2048], F32, kind="ExternalOutput").ap()
    xv = xTd.rearrange("(kc p) t -> p kc t", p=128)
    obv = obd.rearrange("(kc p) t -> p kc t", p=128)
    oav = oad.rearrange("(kc p) t -> p kc t", p=128)
    pv = pd.rearrange("(kc p) t -> p kc t", p=128)
    outv = outd.rearrange("(kc p) t -> p kc t", p=128)

    with contextlib.ExitStack() as es:
        def sb(name, shape, dt):
            return es.enter_context(nc.sbuf_tensor(name, shape, dt))

        def ps(name, shape, dt):
            return es.enter_context(nc.psum_tensor(name, shape, dt))

        vec = sb("vec_sb", [128, NV_T], F32)
        o1024 = sb("o1024", [128, 128], F32)
        o128 = sb("o128", [128, 128], F32)
        R = sb("R", [128, 8, BW], F32)
        xbf = sb("xbf", [128, 8, BW], BF16)
        pb = sb("pb", [128, 2, BW], BF16)
        actb = sb("actb", [128, 22, BW], BF16)
        mgb = actb[:, 0:8, :]
        obb = actb[:, 8:12, :]
        oanb = actb[:, 12:16, :]
        oaf = [sb(f"oaf{i}", [128, NMAX], F32) for i in range(2)]
        tail = sb("tail", [128, 44, 2], F32)
        t0 = [sb(f"t0_{i}", [128, NMAX], F32) for i in range(2)]
        t1 = [sb(f"t1_{i}", [128, NMAX], F32) for i in range(2)]
        t2 = [sb(f"t2_{i}", [128, NMAX], F32) for i in range(2)]
        rstd = sb("rstd", [128, NMAX], F32)
        dsq = sb("dsq", [128, NMAX], F32)
        U_ = [sb(f"U_{i}", [128, NMAX + 2], F32) for i in range(4)]
        cv = [sb(f"cv{i}", [128, NMAX], F32) for i in range(4)]
        gl = [sb(f"gl{i}", [128, NMAX], F32) for i in range(2)]
        pan8 = [sb(f"pan8_{i}", [128, 8, 128], BF16) for i in range(6)]
        pan22 = [sb(f"pan22_{i}", [128, 22, 128], BF16) for i in range(2)]
        P = [ps(f"P{i}", [128, 512], F32) for i in range(8)]

        with nc.Block() as block:
            @block.sync
            def _(_e):
                sc = Sched(nc, es)
                st = {"pan8": 0, "pan22": 0, "ps": 0, "oaf": 0}
                sc.dma("sp", vec[:], vecd[:], "vec", writes=["vec"])
                sc.op("dve", lambda e: e.memset(o1024[:], 1.0 / 1024.0), writes=["o1024"])
                sc.op("dve", lambda e: e.memset(o128[:], 1.0 / 128.0), writes=["o128"])
                sc.op("dve", lambda e: e.memset(tail[:], 0.0), writes=["tail"])

                def load_panel(W_ap, idx, kc):
                    if kc == 22:
                        i = st["pan22"] % 2
                        st["pan22"] += 1
                        buf, key = pan22[i], f"pan22_{i}"
                        sc.dma("pool", buf[:, :, :], W_ap[idx], key, writes=[key])
                        return buf, key
                    i = st["pan8"] % 6
                    st["pan8"] += 1
                    buf, key = pan8[i], f"pan8_{i}"
                    sc.dma("pool", buf[:, 0:kc, :], W_ap[idx], key, writes=[key])
                    return buf, key

                def mm(panel, kc, n, rhs_fn, rhs_keys):
                    buf, key = panel
                    b = st["ps"] % 8
                    st["ps"] += 1
                    for k in range(kc):
                        r = rhs_fn(k)
                        sc.op("pe", lambda e: e.matmul(P[b][:, 0:n], lhsT=buf[:, k, :], rhs=r, start=(k == 0), stop=(k == kc - 1)),
                              reads=[key] + rhs_keys, writes=[f"P{b}"])
                    return P[b][:, 0:n], f"P{b}"

                def stat(ones_ap, okey, n, rhs_list, rhs_keys):
                    b = st["ps"] % 8
                    st["ps"] += 1
                    m = len(rhs_list)
                    for k, r in enumerate(rhs_list):
                        sc.op("pe", lambda e: e.matmul(P[b][:, 0:n], lhsT=ones_ap, rhs=r, start=(k == 0), stop=(k == m - 1)),
                              reads=[okey] + rhs_keys, writes=[f"P{b}"])
                    return P[b][:, 0:n], f"P{b}"

                def layer_norm(gcol, bcol, c0, n, also_bf):
                    cs = slice(c0, c0 + n)
                    mean, mk = stat(o1024[:], "o1024", n, [R[:, k, cs] for k in range(8)], ["R"])
                    for k in range(8):
                        sc.op("dve", lambda e: e.tensor_tensor(out=R[:, k, cs], in0=R[:, k, cs], in1=mean, op=ALU.subtract),
                              reads=["R", mk], writes=["R"])
                    b = st["ps"] % 8
                    st["ps"] += 1
                    for k in range(8):
                        i = k % 2
                        sc.op("act", lambda e: e.activation(out=t0[i][:, 0:n], in_=R[:, k, cs], func=AF.Square),
                              reads=["R"], writes=[f"t0_{i}"])
                        sc.op("pe", lambda e: e.matmul(P[b][:, 0:n], lhsT=o1024[:], rhs=t0[i][:, 0:n], start=(k == 0), stop=(k == 7)),
                              reads=["o1024", f"t0_{i}"], writes=[f"P{b}"])
                    sc.op("act", lambda e: e.activation(out=dsq[:, 0:n], in_=P[b][:, 0:n], func=AF.Sqrt, bias=LN_EPS, scale=1.0),
                          reads=[f"P{b}"], writes=["dsq"])
                    sc.op("dve", lambda e: e.reciprocal(out=rstd[:, 0:n], in_=dsq[:, 0:n]), reads=["dsq"], writes=["rstd"])
                    sc.op("dve", lambda e: e.tensor_tensor(out=R[:, 0, cs], in0=R[:, 0, cs], in1=rstd[:, 0:n], op=ALU.mult),
                          reads=["R", "rstd"], writes=["R"] + [f"Rk{k}" for k in range(8)])
                    for k in range(8):
                        if k > 0:
                            sc.op("dve", lambda e: e.tensor_tensor(out=R[:, k, cs], in0=R[:, k, cs], in1=rstd[:, 0:n], op=ALU.mult),
                                  reads=["rstd", f"Rk{k}"], writes=[f"Rk{k}"])
                        sc.op("act", lambda e: e.activation(out=R[:, k, cs], in_=R[:, k, cs], func=AF.Identity,
                                                            bias=vec[:, bcol + k:bcol + k + 1], scale=vec[:, gcol + k:gcol + k + 1]),
                              reads=[f"Rk{k}", "vec"], writes=[f"Rk{k}"])
                        if also_bf:
                            sc.op("dve", lambda e: e.tensor_copy(out=xbf[:, k, cs], in_=R[:, k, cs]), reads=[f"Rk{k}"], writes=["xbf"])
                    sc.op("dve", lambda e: e.tensor_copy(out=dsq[:, 0:1], in_=rstd[:, 0:1]),
                          reads=[f"Rk{k}" for k in range(8)] + ["rstd"], writes=["R", "dsq"])

                for tile in range(NTOK_T // BW):
                    C0 = tile * BW
                    CS = slice(C0, C0 + BW)
                    sc.dma("sp", R[:, :, :], xv[:, :, CS], "R", writes=["R"])
                    sc.dma("pool", xbf[:, :, :], xv[:, :, CS], "xbf", writes=["xbf"])
                    sc.dma("pool", obb, obv[:, :, CS], "obb", writes=["obb", "actb"])
                    sc.dma("pool", pb[:, :, :], pv[:, :, CS], "pb", writes=["pb"])
                    for h in range(4):
                        pan = load_panel(Wg, h, 8)
                        for (c0, n) in SUBS:
                            cs = slice(c0, c0 + n)
                            i = st["oaf"] % 2
                            st["oaf"] += 1
                            sc.dma("sp", oaf[i][:, 0:n], oav[:, h, C0 + c0:C0 + c0 + n], f"oaf{i}", writes=[f"oaf{i}"])
                            sc.op("act", lambda e: e.activation(out=t0[i][:, 0:n], in_=oaf[i][:, 0:n], func=AF.Square),
                                  reads=[f"oaf{i}"], writes=[f"t0_{i}"])
                            ms, msk = stat(o128[:], "o128", n, [t0[i][:, 0:n]], [f"t0_{i}"])
                            sc.op("act", lambda e: e.activation(out=dsq[:, 0:n], in_=ms, func=AF.Sqrt, bias=RMS_EPS, scale=1.0),
                                  reads=[msk], writes=["dsq"])
                            sc.op("dve", lambda e: e.reciprocal(out=rstd[:, 0:n], in_=dsq[:, 0:n]), reads=["dsq"], writes=["rstd"])
                            gp, gk = mm(pan, 8, n, lambda k: xbf[:, k, cs], ["xbf"])
                            sc.op("act", lambda e: e.activation(out=t1[i][:, 0:n], in_=gp, func=AF.Silu), reads=[gk], writes=[f"t1_{i}"])
                            sc.op("dve", lambda e: e.scalar_tensor_tensor(out=t2[i][:, 0:n], in0=oaf[i][:, 0:n], scalar=vec[:, V_HG + h:V_HG + h + 1],
                                                                          in1=rstd[:, 0:n], op0=ALU.mult, op1=ALU.mult),
                                  reads=[f"oaf{i}", "rstd", "vec"], writes=[f"t2_{i}"])
                            sc.op("dve", lambda e: e.tensor_tensor(out=oanb[:, h, cs], in0=t2[i][:, 0:n], in1=t1[i][:, 0:n], op=ALU.mult),
                                  reads=[f"t2_{i}", f"t1_{i}"], writes=["oanb", "actb"])
                    for oc in range(8):
                        pga, pgb = load_panel(Wg, 4 + oc, 8), load_panel(Wg, 12 + oc, 8)
                        pya, pyb = load_panel(Wa, oc, 4), load_panel(Wb, oc, 4)
                        for (c0, n) in SUBS:
                            cs = slice(c0, c0 + n)
                            gap, gak = mm(pga, 8, n, lambda k: xbf[:, k, cs], ["xbf"])
                            gbp, gbk = mm(pgb, 8, n, lambda k: xbf[:, k, cs], ["xbf"])
                            yap, yak = mm(pya, 4, n, lambda k: oanb[:, k, cs], ["oanb"])
                            ybp, ybk = mm(pyb, 4, n, lambda k: obb[:, k, cs], ["obb"])
                            sc.op("act", lambda e: e.activation(out=t0[0][:, 0:n], in_=gap, func=AF.Sigmoid), reads=[gak], writes=["t0_0"])
                            sc.op("act", lambda e: e.activation(out=t0[1][:, 0:n], in_=gbp, func=AF.Sigmoid), reads=[gbk], writes=["t0_1"])
                            sc.op("dve", lambda e: e.tensor_tensor(out=t1[0][:, 0:n], in0=t0[0][:, 0:n], in1=yap, op=ALU.mult),
                                  reads=["t0_0", yak], writes=["t1_0"])
                            sc.op("dve", lambda e: e.tensor_tensor(out=t1[1][:, 0:n], in0=t0[1][:, 0:n], in1=ybp, op=ALU.mult),
                                  reads=["t0_1", ybk], writes=["t1_1"])
                            sc.op("dve", lambda e: e.tensor_tensor(out=mgb[:, oc, cs], in0=t1[0][:, 0:n], in1=t1[1][:, 0:n], op=ALU.add),
                                  reads=["t1_0", "t1_1"], writes=["mgb", "actb"])
                    for oc in range(8):
                        pan = load_panel(Wo, oc, 8)
                        for (c0, n) in SUBS:
                            cs = slice(c0, c0 + n)
                            hp, hk = mm(pan, 8, n, lambda k: mgb[:, k, cs], ["mgb"])
                            sc.op("dve", lambda e: e.scalar_tensor_tensor(out=R[:, oc, cs], in0=R[:, oc, cs], scalar=DN_ALPHA, in1=hp,
                                                                          op0=ALU.mult, op1=ALU.add),
                                  reads=["R", hk], writes=["R"])
                    for (c0, n) in SUBS:
                        layer_norm(V_L1G, V_L1B, c0, n, True)
                    for j in range(22):
                        pans = [load_panel(Wu, j, 8), load_panel(Wu, 22 + j, 8)]
                        for si, (c0, n) in enumerate(SUBS):
                            cs = slice(c0, c0 + n)
                            for half in range(2):
                                ch = half * 22 + j
                                i = half + 2 * (si % 2)
                                up, uk = mm(pans[half], 8, n, lambda k: xbf[:, k, cs], ["xbf"])
                                sc.op("act", lambda e: e.activation(out=U_[i][:, 2:2 + n], in_=up, func=AF.Copy),
                                      reads=[uk], writes=[f"U_{i}"])
                                sc.op("dve", lambda e: e.tensor_copy(out=U_[i][:, 0:2], in_=tail[:, ch, :]),
                                      reads=["tail", f"U_{i}"], writes=[f"U_{i}"])
                                if tile == 0 and si == 0:
                                    sc.op("dve", lambda e: e.tensor_scalar(out=U_[i][:, 2:4], in0=U_[i][:, 2:4], scalar1=vec[:, V_HM:V_HM + 1],
                                                                           scalar2=None, op0=ALU.mult),
                                          reads=[f"U_{i}", "vec"], writes=[f"U_{i}"])
                                sc.op("act", lambda e: e.activation(out=cv[i][:, 0:n], in_=U_[i][:, 2:2 + n], func=AF.Identity,
                                                                    bias=vec[:, V_CB + ch:V_CB + ch + 1], scale=vec[:, V_CW2 + ch:V_CW2 + ch + 1]),
                                      reads=[f"U_{i}", "vec"], writes=[f"cv{i}"])
                                sc.op("dve", lambda e: e.scalar_tensor_tensor(out=cv[i][:, 0:n], in0=U_[i][:, 1:1 + n], scalar=vec[:, V_CW1 + ch:V_CW1 + ch + 1],
                                                                              in1=cv[i][:, 0:n], op0=ALU.mult, op1=ALU.add),
                                      reads=[f"U_{i}", f"cv{i}", "vec"], writes=[f"cv{i}"])
                                sc.op("dve", lambda e: e.scalar_tensor_tensor(out=cv[i][:, 0:n], in0=U_[i][:, 0:n], scalar=vec[:, V_CW0 + ch:V_CW0 + ch + 1],
                                                                              in1=cv[i][:, 0:n], op0=ALU.mult, op1=ALU.add),
                                      reads=[f"U_{i}", f"cv{i}", "vec"], writes=[f"cv{i}"])
                                sc.op("act", lambda e: e.activation(out=tail[:, ch, :], in_=U_[i][:, n:n + 2], func=AF.Copy),
                                      reads=[f"U_{i}"], writes=["tail"])
                            iv, ig, gi = 2 * (si % 2), 1 + 2 * (si % 2), si % 2
                            sc.op("act", lambda e: e.activation(out=gl[gi][:, 0:n], in_=cv[ig][:, 0:n], func=AF.Gelu), reads=[f"cv{ig}"], writes=[f"gl{gi}"])
                            sc.op("dve", lambda e: e.tensor_tensor(out=actb[:, j, cs], in0=gl[gi][:, 0:n], in1=cv[iv][:, 0:n], op=ALU.mult),
                                  reads=[f"gl{gi}", f"cv{iv}"], writes=["actb", "mgb", "obb", "oanb"])
                    for oc in range(8):
                        pdn, ppe, ppg = load_panel(Wd, oc, 22), load_panel(Wpe, oc, 2), load_panel(Wpg, oc, 8)
                        for (c0, n) in SUBS:
                            cs = slice(c0, c0 + n)
                            fp_, fk = mm(pdn, 22, n, lambda k: actb[:, k, cs], ["actb"])
                            pep, pek = mm(ppe, 2, n, lambda k: pb[:, k, cs], ["pb"])
                            pgp, pgk = mm(ppg, 8, n, lambda k: xbf[:, k, cs], ["xbf"])
                            sc.op("act", lambda e: e.activation(out=t0[1][:, 0:n], in_=pgp, func=AF.Sigmoid), reads=[pgk], writes=["t0_1"])
                            sc.op("dve", lambda e: e.tensor_tensor(out=t1[0][:, 0:n], in0=t0[1][:, 0:n], in1=pep, op=ALU.mult),
                                  reads=["t0_1", pek], writes=["t1_0"])
                            sc.op("dve", lambda e: e.tensor_tensor(out=t1[1][:, 0:n], in0=t1[0][:, 0:n], in1=fp_, op=ALU.add),
                                  reads=["t1_0", fk], writes=["t1_1"])
                            sc.op("dve", lambda e: e.scalar_tensor_tensor(out=R[:, oc, cs], in0=R[:, oc, cs], scalar=DN_ALPHA, in1=t1[1][:, 0:n],
                                                                          op0=ALU.mult, op1=ALU.add),
                                  reads=["R", "t1_1"], writes=["R"])
                    for si, (c0, n) in enumerate(SUBS):
                        layer_norm(V_L2G, V_L2B, c0, n, False)
                        lo = 2 if (tile == 0 and si == 0) else 0
                        g0 = C0 + c0 + lo - 2
                        sc.dma("sp", outv[:, :, g0:g0 + n - lo], R[:, :, c0 + lo:c0 + n], "out", reads=["R"], is_output=True)
                sc.finish()
    return nc


def _panels(W):
    K_, M_ = W.shape
    return np.ascontiguousarray(W.reshape(K_ // 128, 128, M_ // 128, 128).transpose(2, 1, 0, 3))


def _cols(v):
    return np.ascontiguousarray(v.reshape(-1, 128).T)


_PROGS = {}


def kernel(x, p, lb_logits, w_in, hg_norm_g, w_a, w_b, w_out, ln1_g, ln1_b,
           w_up, conv_w, conv_b, w_down, w_pe, w_pg, ln2_g, ln2_b):
    f32 = np.float32
    x = np.asarray(x, f32)
    p = np.asarray(p, f32)
    S = SEQ
    h = x[0]
    cM = consts_M()
    cores = list(range(NCORES))
    for l in range(DEPTH):
        hT = np.ascontiguousarray(h.T)
        if ("M", l) not in _PROGS:
            _PROGS[("M", l)] = build_M(S, l)
        maps = []
        for c in cores:
            hh = c // 2
            maps.append({"xT": hT, "wM": prep_M_weights(np.asarray(w_in[l], f32), c),
                         "lbl": np.ascontiguousarray(np.asarray(lb_logits, f32)[:, hh * 128:(hh + 1) * 128].T), "cst": cM})
        res = run_bass_kernel_spmd(_PROGS[("M", l)], maps, core_ids=cores)
        obT = np.concatenate([np.asarray(res.results[c]["ob"]).T for c in cores], axis=0)
        oaT = np.concatenate([np.asarray(res.results[c]["oaT"]) for c in cores], axis=0)
        def pad2(a):
            return np.concatenate([np.zeros((a.shape[0], 2), f32), a], axis=1)
        hTp, obTp, oaTp, pTp = pad2(hT), pad2(obT), pad2(oaT), pad2(np.ascontiguousarray(p[l, 0].T))
        wl = np.asarray(w_in[l], f32)
        Wg = _panels(np.concatenate([wl[:, 1536:2048], wl[:, 3584:4608], wl[:, 4608:5632]], axis=1))
        Wa, Wb, Wo = _panels(np.asarray(w_a[l], f32)), _panels(np.asarray(w_b[l], f32)), _panels(np.asarray(w_out[l], f32))
        Wu, Wd = _panels(np.asarray(w_up[l], f32)), _panels(np.asarray(w_down[l], f32))
        Wpe, Wpg = _panels(np.asarray(w_pe[l], f32)), _panels(np.asarray(w_pg[l], f32))
        cw = np.asarray(conv_w[l], f32)
        vec = np.zeros((128, NV_T), f32)
        vec[:, V_HG:V_HG + 4] = _cols(np.asarray(hg_norm_g[l], f32))
        vec[:, V_L1G:V_L1G + 8] = _cols(np.asarray(ln1_g[l], f32))
        vec[:, V_L1B:V_L1B + 8] = _cols(np.asarray(ln1_b[l], f32))
        vec[:, V_CW0:V_CW0 + 44] = _cols(cw[0])
        vec[:, V_CW1:V_CW1 + 44] = _cols(cw[1])
        vec[:, V_CW2:V_CW2 + 44] = _cols(cw[2])
        vec[:, V_CB:V_CB + 44] = _cols(np.asarray(conv_b[l], f32))
        vec[:, V_L2G:V_L2G + 8] = _cols(np.asarray(ln2_g[l], f32))
        vec[:, V_L2B:V_L2B + 8] = _cols(np.asarray(ln2_b[l], f32))
        if "T" not in _PROGS:
            _PROGS["T"] = build_T()
        maps = []
        for c in cores:
            sl = slice(c * 2048, c * 2048 + NTOK_T)
            v = vec.copy()
            v[:, V_HM] = 0.0 if c == 0 else 1.0
            maps.append({"xTc": np.ascontiguousarray(hTp[:, sl]), "obTc": np.ascontiguousarray(obTp[:, sl]),
                         "oaTc": np.ascontiguousarray(oaTp[:, sl]), "pTc": np.ascontiguousarray(pTp[:, sl]), "vec": v,
                         "Wg": Wg, "Wa": Wa, "Wb": Wb, "Wo": Wo, "Wu": Wu, "Wd": Wd, "Wpe": Wpe, "Wpg": Wpg})
        res = run_bass_kernel_spmd(_PROGS["T"], maps, core_ids=cores)
        outT = np.concatenate([np.asarray(res.results[c]["outT"]) for c in cores], axis=1)
        h = np.ascontiguousarray(outT.T)
    return h[None].astype(f32)
```

```python
import contextlib
import types
import numpy as np
import concourse.bass as bass
import concourse.mybir as mybir
from concourse.bass_utils import run_bass_kernel_spmd

F32 = mybir.dt.float32
BF16 = mybir.dt.bfloat16
AF = mybir.ActivationFunctionType
ALU = mybir.AluOpType

D_MODEL = 1024
SEQ = 16384
DEPTH = 2
NCORES = 8
D_FF = 2816
DN_ALPHA = (2 * DEPTH) ** 0.25
LN_EPS = 1e-5
RMS_EPS = 1e-6
F_FLOOR = 1e-30


class Sched:
    WIN = 512

    def __init__(self, nc, es):
        self.nc = nc
        self.es = es
        self.engs = {"pe": nc.tensor, "act": nc.scalar, "dve": nc.vector, "pool": nc.gpsimd, "sp": nc.sync}
        self.ops = []
        self.seq = {}
        self.last_w = {}
        self.readers = {}
        self.out_ids = []

    def _deps(self, reads, writes):
        d = []
        for b in reads:
            if b in self.last_w:
                d.append(self.last_w[b])
        for b in writes:
            if b in self.last_w:
                d.append(self.last_w[b])
            d.extend(self.readers.get(b, []))
        return d

    def _commit(self, i, reads, writes):
        for b in reads:
            self.readers.setdefault(b, []).append(i)
        for b in writes:
            self.last_w[b] = i
            self.readers[b] = []

    def _add(self, kind, e, payload, stream, reads, writes):
        deps = self._deps(reads, writes)
        sq = self.seq.get(stream, 0)
        self.seq[stream] = sq + 1
        i = len(self.ops)
        self.ops.append((kind, e, payload, deps, stream, sq))
        self._commit(i, reads, writes)
        return i

    @staticmethod
    def _freeze(fn):
        if fn.__closure__ is None:
            return fn
        cells = []
        for c in fn.__closure__:
            try:
                cells.append(types.CellType(c.cell_contents))
            except ValueError:
                cells.append(c)
        return types.FunctionType(fn.__code__, fn.__globals__, fn.__name__, fn.__defaults__, tuple(cells))

    def op(self, e, fn, reads=(), writes=()):
        return self._add("op", e, self._freeze(fn), e, reads, writes)

    def dma(self, q, out, in_, key, reads=(), writes=(), is_output=False):
        i = self._add("dma", q, (out, in_), "dma_" + key, reads, writes)
        if is_output:
            self.out_ids.append(i)
        return i

    def finish(self):
        ops = self.ops
        n = len(ops)
        seen = {e: {} for e in self.engs}
        waits = [[] for _ in range(n)]
        needs = [False] * n
        for i, (kind, e, payload, deps, stream, sq) in enumerate(ops):
            best = {}
            for d in deps:
                ds, dq = ops[d][4], ops[d][5]
                if e == "pe" and ds == "pe":
                    continue
                if seen[e].get(ds, -1) >= dq:
                    continue
                if ds not in best or best[ds][1] < dq:
                    best[ds] = (d, dq)
            for ds, (d, dq) in best.items():
                waits[i].append(d)
                seen[e][ds] = dq
                needs[d] = True
        final_waits = []
        lastout = {}
        for i in self.out_ids:
            lastout[ops[i][4]] = i
        for i in ops and self.out_ids:
            needs[i] = True
        cnt = {}
        sems = {}
        tok = [None] * n
        for i, (kind, e, payload, deps, stream, sq) in enumerate(ops):
            if kind == "dma" or needs[i]:
                step = 16 if kind == "dma" else 1
                c = cnt.get(stream, 0) + step
                cnt[stream] = c
                w = (c - 1) // self.WIN
                if (stream, w) not in sems:
                    sems[(stream, w)] = self.es.enter_context(self.nc.semaphore(f"s_{stream}_{w}"))
                tok[i] = (sems[(stream, w)], c - w * self.WIN, step)
        for i, (kind, e, payload, deps, stream, sq) in enumerate(ops):
            eng = self.engs[e]
            for d in waits[i]:
                eng.wait_ge(tok[d][0], tok[d][1])
            if kind == "dma":
                ins = eng.dma_start(out=payload[0], in_=payload[1])
            else:
                ins = payload(eng)
            if tok[i] is not None:
                ins.then_inc(tok[i][0], tok[i][2])
        done = set()
        for i in reversed(self.out_ids):
            key = id(tok[i][0])
            if key in done:
                continue
            done.add(key)
            self.engs["sp"].wait_ge(tok[i][0], tok[i][1])


C_ID, C_J, C_A, C_A2, C_SH, C_SH2, C_NEG, C_MD, C_MH, C_RST = 0, 128, 256, 384, 512, 640, 768, 896, 1024, 1088
CW_M = 1600


def consts_M():
    c = np.zeros((128, CW_M), np.float32)
    i = np.arange(128)
    c[i, C_ID + i] = 1.0
    c[i, C_J + (127 - i)] = 1.0
    for j in range(128):
        if 126 - j >= 0:
            c[126 - j, C_A + j] += 1.0
        c[127 - j, C_A + j] -= 1.0
    c[127, C_A2 + 127] = 1.0
    for t in range(1, 128):
        c[t - 1, C_SH + t] = 1.0
    c[127, C_SH2 + 0] = 1.0
    ii, jj = np.meshgrid(i, i, indexing="ij")
    c[:, C_NEG:C_NEG + 128] = np.where(ii + jj <= 127, -30000.0, 0.0)
    c[:, C_MD:C_MD + 128] = np.where(ii + jj >= 128, 1.0, 0.0)
    s64 = np.arange(64)
    ss, tt = np.meshgrid(s64, s64, indexing="ij")
    c[0:64, C_MH:C_MH + 64] = np.where(ss <= tt, 1.0, 0.0)
    r = np.ones(512, np.float32)
    r[0::64] = 0.0
    c[:, C_RST:C_RST + 512] = r[None, :]
    return c


def build_M(S, layer, upto=9):
    NQB = S // 128
    NTT = S // 512
    nc = bass.Bass("TRN2", target_bir_lowering=False)
    xT = nc.dram_tensor("xT", [D_MODEL, S], F32, kind="ExternalInput").ap()
    wM = nc.dram_tensor("wM", [128, 8, 512], F32, kind="ExternalInput").ap()
    lbl = nc.dram_tensor("lbl", [128, DEPTH], F32, kind="ExternalInput").ap()
    cst = nc.dram_tensor("cst", [128, CW_M], F32, kind="ExternalInput").ap()
    ob_o = nc.dram_tensor("ob", [S, 64], F32, kind="ExternalOutput").ap()
    oaT_o = nc.dram_tensor("oaT", [64, S], F32, kind="ExternalOutput").ap()
    xT_v = xT.rearrange("(kc p) t -> p kc t", p=128)

    with contextlib.ExitStack() as es:
        def sb(name, shape, dt):
            return es.enter_context(nc.sbuf_tensor(name, shape, dt))

        def ps(name, shape, dt):
            return es.enter_context(nc.psum_tensor(name, shape, dt))

        cf = sb("cf", [128, CW_M], F32)
        cb = sb("cb", [128, CW_M], BF16)
        wb = sb("wb", [128, 8, 512], BF16)
        QT = sb("QT", [128, S], BF16)
        KTr = sb("KTr", [128, S], BF16)
        DVr = sb("DVr", [128, NQB, 64], BF16)
        Vf = sb("Vf", [128, NQB, 64], F32)
        xb = [sb(f"xb{i}", [128, 8, 512], BF16) for i in range(2)]
        Kb = sb("Kb", [128, 4, 64], BF16)
        vhb = sb("vhb", [64, 8, 64], BF16)
        zeros = sb("zeros", [128, 1024], F32)
        lbt = sb("lbt", [128, 16], F32)
        bb = sb("bb", [128, 512], F32)
        e1 = sb("e1", [128, 512], F32)
        e2 = sb("e2", [128, 512], F32)
        qp = sb("qp", [128, 512], BF16)
        kp = sb("kp", [128, 512], BF16)
        sc8 = sb("sc8", [128, 64], F32)
        state = sb("state", [128, 64], F32)
        srb = [sb(f"srb{i}", [128, 64], BF16) for i in range(2)]
        scm = [sb(f"scm{i}", [64, 64], BF16) for i in range(8)]
        kTs = [sb(f"kTs{i}", [64, 128], BF16) for i in range(8)]
        utmp = sb("utmp", [128, 64], F32)
        oas = [sb(f"oas{i}", [64, 512], F32) for i in range(2)]
        omb = [sb(f"omb{i}", [128, 1024], F32) for i in range(2)]
        Eb = [sb(f"Eb{i}", [128, 1024], BF16) for i in range(4)]
        ETs = [sb(f"ETs{i}", [128, 1024], BF16) for i in range(2)]
        qs, sg, ff, kk = omb[0][:, 0:512], omb[0][:, 512:1024], omb[1][:, 0:512], omb[1][:, 512:1024]
        gg = ff
        obs = [sb(f"obs{i}", [128, 64], F32) for i in range(2)]
        Zp = [ps(f"Zp{i}", [128, 1024], F32) for i in range(2)]
        Ap = [ps(f"Ap{i}", [128, 512], F32) for i in range(2)]
        Fp = [Zp[0][:, 0:512], Zp[0][:, 512:1024], Zp[1][:, 0:512], Ap[0][:, :], Ap[1][:, :], Zp[1][:, 512:1024]]
        ET32 = ps("ET32", [128, 1024], F32)

        identb = cb[:, C_ID:C_ID + 128]
        Jb = cb[:, C_J:C_J + 128]
        NEGb = cb[:, C_NEG:C_NEG + 128]
        MDb = cb[:, C_MD:C_MD + 128]
        Af = cf[:, C_A:C_A + 128]
        A2f = cf[:, C_A2:C_A2 + 128]
        SHf = cf[:, C_SH:C_SH + 128]
        SH2f = cf[:, C_SH2:C_SH2 + 128]
        MHf = cf[0:64, C_MH:C_MH + 64]
        RSTf = cf[:, C_RST:C_RST + 512]

        with nc.Block() as block:
            @block.sync
            def _(_e):
                sc = Sched(nc, es)
                sc.dma("sp", cf[:], cst[:], "cf", writes=["cf"])
                sc.dma("pool", cb[:], cst[:], "cb", writes=["cb"])
                sc.dma("pool", wb[:], wM[:], "wb", writes=["wb"])
                sc.dma("sp", lbt[:, 0:DEPTH], lbl[:], "lbt", writes=["lbt"])
                sc.op("dve", lambda e: e.memset(zeros[:], 0.0), writes=["zeros"])
                sc.op("dve", lambda e: e.memset(state[:], 0.0), writes=["state"])
                sc.op("pool", lambda e: e.memset(QT[64:128, :], 0.0), writes=["QTpad"])
                sc.op("pool", lambda e: e.memset(KTr[64:128, :], 0.0), writes=["KTpad"])
                LB, OM, NOM = lbt[:, 8:9], lbt[:, 9:10], lbt[:, 10:11]
                mx = lbt[:, 4:5]
                sc.op("dve", lambda e: e.tensor_reduce(out=mx, in_=lbt[:, 0:DEPTH], axis=mybir.AxisListType.X, op=ALU.max),
                      reads=["lbt"], writes=["lb_mx"])
                sc.op("dve", lambda e: e.tensor_scalar(out=lbt[:, 5:6], in0=mx, scalar1=-1.0, scalar2=None, op0=ALU.mult),
                      reads=["lb_mx"], writes=["lb_nmx"])
                sc.op("act", lambda e: e.activation(out=lbt[:, 2:2 + DEPTH], in_=lbt[:, 0:DEPTH], func=AF.Exp,
                                                    bias=lbt[:, 5:6], scale=1.0),
                      reads=["lbt", "lb_nmx"], writes=["lb_e"])
                sc.op("dve", lambda e: e.tensor_reduce(out=lbt[:, 6:7], in_=lbt[:, 2:2 + DEPTH], axis=mybir.AxisListType.X, op=ALU.add),
                      reads=["lb_e"], writes=["lb_s"])
                sc.op("dve", lambda e: e.reciprocal(out=lbt[:, 7:8], in_=lbt[:, 6:7]), reads=["lb_s"], writes=["lb_r"])
                if layer == 0:
                    sc.op("dve", lambda e: e.memset(LB, 0.0), reads=["lb_r"], writes=["lb"])
                else:
                    sc.op("dve", lambda e: e.tensor_reduce(out=lbt[:, 11:12], in_=lbt[:, 3:2 + layer + 1],
                                                           axis=mybir.AxisListType.X, op=ALU.add),
                          reads=["lb_e"], writes=["lb_p"])
                    sc.op("dve", lambda e: e.tensor_tensor(out=LB, in0=lbt[:, 11:12], in1=lbt[:, 7:8], op=ALU.mult),
                          reads=["lb_p", "lb_r"], writes=["lb"])
                sc.op("dve", lambda e: e.tensor_scalar(out=OM, in0=LB, scalar1=-1.0, scalar2=1.0, op0=ALU.mult, op1=ALU.add),
                      reads=["lb"], writes=["om"])
                sc.op("dve", lambda e: e.tensor_scalar(out=NOM, in0=OM, scalar1=-1.0, scalar2=None, op0=ALU.mult),
                      reads=["om"], writes=["nom"])

                for tt in range(NTT if upto >= 1 else 0):
                    X = xb[tt % 2]
                    xk = f"xb{tt % 2}"
                    sc.dma("pool", X[:], xT_v[:, :, tt * 512:(tt + 1) * 512], xk, writes=[xk])
                    for oi, (bank, c0) in enumerate([(0, 0), (1, 128), (2, 256)]):
                        for kc in range(8):
                            sc.op("pe", lambda e, bank=bank, c0=c0, kc=kc: e.matmul(
                                Fp[bank][:, :], lhsT=wb[:, kc, c0:c0 + 128], rhs=X[:, kc, :], start=(kc == 0), stop=(kc == 7)),
                                reads=[xk, "wb"], writes=[f"F{bank}"])
                    if upto < 1.05:
                        sc.op("act", lambda e: e.activation(out=QT[0:64, tt * 512:(tt + 1) * 512], in_=Fp[2][0:64, :], func=AF.Copy),
                              reads=["F2"], writes=[f"QT{tt}"])
                        continue
                    for blk in range(4):
                        for kc in range(8):
                            sc.op("pe", lambda e, blk=blk, kc=kc: e.matmul(
                                Fp[3][:, blk * 128:(blk + 1) * 128], lhsT=X[:, kc, blk * 128:(blk + 1) * 128],
                                rhs=wb[:, kc, 320:448], start=(kc == 0), stop=(kc == 7)),
                                reads=[xk, "wb"], writes=["F3"])
                    for c in range(8 if upto >= 1.2 else 0):
                        for kc in range(8):
                            sc.op("pe", lambda e, c=c, kc=kc: e.matmul(
                                Fp[4][0:64, c * 64:(c + 1) * 64], lhsT=X[:, kc, c * 64:(c + 1) * 64],
                                rhs=wb[:, kc, 448:512], start=(kc == 0), stop=(kc == 7)),
                                reads=[xk, "wb"], writes=["F4"])
                    sc.op("act", lambda e: e.activation(out=QT[0:64, tt * 512:(tt + 1) * 512], in_=Fp[2][0:64, :], func=AF.Copy),
                          reads=["F2"], writes=[f"QT{tt}"])
                    F3v = Fp[3][:, :].rearrange("p (b c) -> p b c", b=4)
                    if upto >= 1.12:
                        sc.op("dve", lambda e: e.tensor_copy(out=Kb[:, :, :], in_=F3v[:, :, 0:64]), reads=["F3"], writes=["Kb"])
                    if upto >= 1.14:
                        sc.op("dve", lambda e: e.tensor_copy(out=Vf[:, tt * 4:(tt + 1) * 4, :], in_=F3v[:, :, 64:128]),
                              reads=["F3"], writes=[f"Vf{tt}"])
                    if upto >= 1.2:
                        sc.op("dve", lambda e: e.tensor_copy(out=vhb[:, :, :], in_=Fp[4][0:64, :].rearrange("p (c v) -> p c v", c=8)),
                              reads=["F4"], writes=["vhb"])
                    if upto < 1.3:
                        continue
                    for blk in range(4):
                        sc.op("pe", lambda e, blk=blk: e.matmul(
                            Fp[2][0:64, (3 - blk) * 128:(4 - blk) * 128], lhsT=Kb[:, blk, :], rhs=Jb, start=True, stop=True),
                            reads=["Kb", "cb"], writes=["F2"])
                    rb0 = NQB - 1 - (tt * 4 + 3)
                    sc.op("act", lambda e: e.activation(out=KTr[0:64, rb0 * 128:(rb0 + 4) * 128], in_=Fp[2][0:64, :], func=AF.Copy),
                          reads=["F2"], writes=[f"KTr{tt}"])
                    if upto < 1.5:
                        continue
                    for blk in range(4):
                        bs = tt * 4 + blk
                        sc.op("pe", lambda e, blk=blk, bs=bs: e.matmul(
                            Fp[3][:, (3 - blk) * 64:(4 - blk) * 64], lhsT=Af, rhs=Vf[:, bs, :], start=True, stop=(bs == 0)),
                            reads=[f"Vf{tt}", "cf"], writes=["F3"])
                        if bs > 0:
                            sc.op("pe", lambda e, blk=blk, bs=bs: e.matmul(
                                Fp[3][:, (3 - blk) * 64:(4 - blk) * 64], lhsT=A2f, rhs=Vf[:, bs - 1, :], start=False, stop=True),
                                reads=[f"Vf{(bs - 1) // 4}", "cf"], writes=["F3"])
                    sc.op("dve", lambda e: e.tensor_copy(out=DVr[:, rb0:rb0 + 4, :],
                                                         in_=Fp[3][:, 0:256].rearrange("p (b c) -> p b c", b=4)),
                          reads=["F3"], writes=[f"DVr{tt}"])

                    if upto < 2:
                        continue
                    sc.op("act", lambda e: e.activation(out=qs[:], in_=Fp[0][:, :], func=AF.Silu), reads=["F0"], writes=["qs"])
                    sc.op("act", lambda e: e.activation(out=sg[:], in_=Fp[1][:, :], func=AF.Sigmoid), reads=["F1"], writes=["sg"])
                    sc.op("dve", lambda e: e.tensor_scalar(out=ff[:], in0=sg[:], scalar1=OM, scalar2=LB, op0=ALU.mult, op1=ALU.add),
                          reads=["sg", "om", "lb"], writes=["ff"])
                    sc.op("dve", lambda e: e.tensor_scalar(out=ff[:], in0=ff[:], scalar1=F_FLOOR, scalar2=None, op0=ALU.max),
                          reads=["ff"], writes=["ff"])
                    sc.op("act", lambda e: e.activation(out=ff[:], in_=ff[:], func=AF.Ln), reads=["ff"], writes=["ff"])
                    sc.op("dve", lambda e: e.tensor_scalar(out=kk[:], in0=sg[:], scalar1=NOM, scalar2=OM, op0=ALU.mult, op1=ALU.add),
                          reads=["sg", "nom", "om"], writes=["kk"])
                    sc.op("dve", lambda e: e.tensor_tensor_scan(out=bb[:], data0=RSTf, data1=gg[:], initial=0.0,
                                                                op0=ALU.mult, op1=ALU.add),
                          reads=["ff", "cf"], writes=["bb"])
                    bbv = bb[:].rearrange("p (c t) -> p c t", c=8)
                    BM, D8, BL, NBM = sc8[:, 0:8], sc8[:, 8:16], sc8[:, 16:24], sc8[:, 24:32]
                    sc.op("dve", lambda e: e.tensor_copy(out=BM, in_=bbv[:, :, 31]), reads=["bb"], writes=["bm"])
                    sc.op("dve", lambda e: e.tensor_copy(out=BL, in_=bbv[:, :, 63]), reads=["bb"], writes=["bl"])
                    sc.op("dve", lambda e: e.tensor_scalar(out=NBM, in0=BM, scalar1=-1.0, scalar2=None, op0=ALU.mult),
                          reads=["bm"], writes=["nbm"])
                    sc.op("dve", lambda e: e.tensor_tensor(out=D8, in0=BL, in1=BM, op=ALU.subtract),
                          reads=["bl", "bm"], writes=["d8"])
                    sc.op("act", lambda e: e.activation(out=sc8[:, 32:56], in_=sc8[:, 0:24], func=AF.Exp),
                          reads=["bm", "d8", "bl"], writes=["er", "c2", "el"])
                    for c in range(8):
                        sl = slice(c * 64, (c + 1) * 64)
                        sc.op("act", lambda e, c=c, sl=sl: e.activation(out=e1[:, sl], in_=bb[:, sl], func=AF.Exp,
                                                                        bias=sc8[:, 24 + c:25 + c], scale=1.0),
                              reads=["bb", "nbm"], writes=["e1"])
                        sc.op("act", lambda e, c=c, sl=sl: e.activation(out=e2[:, sl], in_=bb[:, sl], func=AF.Exp,
                                                                        bias=sc8[:, c:c + 1], scale=-1.0),
                              reads=["bb", "bm"], writes=["e2"])
                    sc.op("dve", lambda e: e.tensor_tensor(out=qp[:], in0=qs[:], in1=e1[:], op=ALU.mult),
                          reads=["qs", "e1"], writes=["qp"])
                    sc.op("dve", lambda e: e.tensor_tensor(out=kp[:], in0=kk[:], in1=e2[:], op=ALU.mult),
                          reads=["kk", "e2"], writes=["kp"])
                    for c in range(8):
                        sl = slice(c * 64, (c + 1) * 64)
                        sc.op("pe", lambda e: e.matmul(Fp[3][0:64, sl], lhsT=kp[:, sl], rhs=qp[:, sl], start=True, stop=True),
                              reads=["kp", "qp"], writes=["F3"])
                    for c in range(8):
                        sl = slice(c * 64, (c + 1) * 64)
                        sc.op("pe", lambda e: e.matmul(ET32[0:64, c * 128:(c + 1) * 128], lhsT=kp[:, sl], rhs=identb, start=True, stop=True),
                              reads=["kp", "cb"], writes=["B0"])
                    for c in range(8):
                        sl = slice(c * 64, (c + 1) * 64)
                        sc.op("dve", lambda e: e.tensor_tensor(out=scm[c][:], in0=Fp[3][0:64, sl], in1=MHf, op=ALU.mult),
                              reads=["F3", "cf"], writes=[f"scm{c}"])
                        sc.op("act", lambda e: e.activation(out=kTs[c][:], in_=ET32[0:64, c * 128:(c + 1) * 128], func=AF.Copy),
                              reads=["B0"], writes=[f"kTs{c}"])
                    for c in range(8):
                        sc.op("pe", lambda e: e.matmul(Fp[5][:, c * 64:(c + 1) * 64], lhsT=kTs[c][:], rhs=vhb[:, c, :], start=True, stop=True),
                              reads=[f"kTs{c}", "vhb"], writes=["F5"])
                    for c in range(8):
                        sl = slice(c * 64, (c + 1) * 64)
                        s2 = (tt * 8 + c) % 2
                        sc.op("dve", lambda e: e.tensor_scalar(out=srb[s2][:], in0=state[:], scalar1=sc8[:, 32 + c:33 + c],
                                                               scalar2=None, op0=ALU.mult),
                              reads=["state", "er"], writes=[f"srb{s2}"])
                        sc.op("pe", lambda e: e.matmul(Fp[4][0:64, sl], lhsT=srb[s2][:], rhs=qp[:, sl], start=True, stop=False),
                              reads=[f"srb{s2}", "qp"], writes=["F4"])
                        sc.op("pe", lambda e: e.matmul(Fp[4][0:64, sl], lhsT=vhb[:, c, :], rhs=scm[c][:], start=False, stop=True),
                              reads=[f"scm{c}", "vhb"], writes=["F4"])
                        sc.op("dve", lambda e: e.tensor_scalar(out=utmp[:], in0=Fp[5][:, c * 64:(c + 1) * 64], scalar1=sc8[:, 40 + c:41 + c],
                                                               scalar2=None, op0=ALU.mult),
                              reads=["F5", "c2"], writes=["utmp"])
                        sc.op("dve", lambda e: e.scalar_tensor_tensor(out=state[:], in0=state[:], scalar=sc8[:, 48 + c:49 + c],
                                                                      in1=utmp[:], op0=ALU.mult, op1=ALU.add),
                              reads=["state", "utmp", "el"], writes=["state"])
                    ok = f"oas{tt % 2}"
                    sc.op("act", lambda e: e.activation(out=oas[tt % 2][:], in_=Fp[4][0:64, :], func=AF.Copy), reads=["F4"], writes=[ok])
                    sc.dma("sp", oaT_o[:, tt * 512:(tt + 1) * 512], oas[tt % 2][:], f"oaT_out{tt % 2}", reads=[ok], is_output=True)

                TWK = 1024
                def qb_tiles(qb):
                    rstart = (NQB - 1 - qb) * 128
                    L = S - rstart
                    nt = (L + TWK - 1) // TWK
                    return [(qb, ti, rstart + ti * TWK, min(TWK, S - rstart - ti * TWK), ti == nt - 1) for ti in range(nt)]

                tiles = []
                prev_of = []
                for q0 in range(0, NQB, 2):
                    la, lb_ = qb_tiles(q0), (qb_tiles(q0 + 1) if q0 + 1 < NQB else [])
                    last_idx = {}
                    for i in range(max(len(la), len(lb_))):
                        for lst in (la, lb_):
                            if i < len(lst):
                                t = lst[i]
                                prev_of.append(last_idx.get(t[0], -1))
                                last_idx[t[0]] = len(tiles)
                                tiles.append(t)
                if upto < 3:
                    tiles = []
                NT = len(tiles)
                qt_keys = [f"QT{t}" for t in range(NTT)] + ["QTpad"]
                kt_keys = [f"KTr{t}" for t in range(NTT)] + ["KTpad"]
                dv_keys = [f"DVr{t}" for t in range(NTT)]
                vf_keys = [f"Vf{t}" for t in range(NTT)]
                zkeys = [["F0", "F1"], ["F2", "F5"]]
                hg_alias = [["qs", "sg"], ["ff", "kk"]]

                def stA(n):
                    qb, ti, r0, w, last = tiles[n]
                    z = Zp[n % 2]
                    for h0 in range(0, w, 512):
                        wh = min(512, w - h0)
                        zk = zkeys[n % 2][h0 // 512]
                        diag = (ti == 0 and h0 == 0)
                        sc.op("pe", lambda e: e.matmul(z[:, h0:h0 + wh], lhsT=QT[:, qb * 128:(qb + 1) * 128], rhs=KTr[:, r0 + h0:r0 + h0 + wh],
                                                       start=True, stop=(not diag)),
                              reads=qt_keys + kt_keys, writes=[zk])
                        if diag:
                            sc.op("pe", lambda e: e.matmul(z[:, 0:128], lhsT=identb, rhs=NEGb, start=False, stop=True),
                                  reads=["cb"], writes=[zk])

                def stB(n):
                    qb, ti, r0, w, last = tiles[n]
                    sc.op("act", lambda e: e.activation(out=omb[n % 2][:, 0:w], in_=Zp[n % 2][:, 0:w], func=AF.Sigmoid, scale=-0.125),
                          reads=zkeys[n % 2], writes=[f"omb{n % 2}"] + hg_alias[n % 2])

                def stC(n):
                    qb, ti, r0, w, last = tiles[n]
                    if ti == 0:
                        init = 1.0
                        rd = []
                    else:
                        pn = prev_of[n]
                        wp = tiles[pn][3]
                        init = Eb[pn % 4][:, wp - 1:wp]
                        rd = [f"Eb{pn % 4}"]
                    sc.op("dve", lambda e: e.tensor_tensor_scan(out=Eb[n % 4][:, 0:w], data0=omb[n % 2][:, 0:w], data1=zeros[:, 0:w],
                                                                initial=init, op0=ALU.mult, op1=ALU.add),
                          reads=[f"omb{n % 2}", "zeros"] + rd, writes=[f"Eb{n % 4}"])

                def stD(n):
                    qb, ti, r0, w, last = tiles[n]
                    for sbk in range(w // 128):
                        sl = slice(sbk * 128, (sbk + 1) * 128)
                        sc.op("pe", lambda e: e.matmul(ET32[:, sl], lhsT=Eb[n % 4][:, sl], rhs=identb, start=True, stop=True),
                              reads=[f"Eb{n % 4}", "cb"], writes=["B0"])

                def stF(n):
                    qb, ti, r0, w, last = tiles[n]
                    sc.op("act", lambda e: e.activation(out=ETs[n % 2][:, 0:w], in_=ET32[:, 0:w], func=AF.Copy),
                          reads=["B0"], writes=[f"ETs{n % 2}"])
                    if ti == 0:
                        sc.op("dve", lambda e: e.tensor_tensor(out=ETs[n % 2][:, 0:128], in0=ETs[n % 2][:, 0:128], in1=MDb, op=ALU.mult),
                              reads=[f"ETs{n % 2}", "cb"], writes=[f"ETs{n % 2}"])

                def stG(n):
                    qb, ti, r0, w, last = tiles[n]
                    acc = Ap[qb % 2]
                    ak = f"F{3 + qb % 2}"
                    for sbk in range(w // 128):
                        sl = slice(sbk * 128, (sbk + 1) * 128)
                        br = r0 // 128 + sbk
                        sc.op("pe", lambda e: e.matmul(acc[:, 0:64], lhsT=ETs[n % 2][:, sl], rhs=DVr[:, br, :],
                                                       start=(ti == 0 and sbk == 0), stop=False),
                              reads=[f"ETs{n % 2}"] + dv_keys, writes=[ak])
                    if last:
                        sc.op("pe", lambda e: e.matmul(acc[:, 0:64], lhsT=SHf, rhs=Vf[:, qb, :], start=False, stop=(qb == 0)),
                              reads=["cf"] + vf_keys, writes=[ak])
                        if qb > 0:
                            sc.op("pe", lambda e: e.matmul(acc[:, 0:64], lhsT=SH2f, rhs=Vf[:, qb - 1, :], start=False, stop=True),
                                  reads=["cf"] + vf_keys, writes=[ak])
                        sc.op("dve", lambda e: e.tensor_copy(out=obs[qb % 2][:], in_=acc[:, 0:64]), reads=[ak], writes=[f"obs{qb % 2}"])
                        sc.dma("sp", ob_o[qb * 128:(qb + 1) * 128, :], obs[qb % 2][:], f"ob_out{qb % 2}", reads=[f"obs{qb % 2}"], is_output=True)

                for n in range(-2, NT + 1):
                    if 0 <= n + 2 < NT:
                        stA(n + 2)
                    if 0 <= n < NT and upto >= 3.3:
                        stC(n)
                    if 0 <= n + 2 < NT and upto >= 3.2:
                        stB(n + 2)
                    if 0 <= n < NT:
                        if upto >= 3.4:
                            stD(n)
                        if upto >= 3.5:
                            stF(n)
                    if 0 <= n - 1 < NT and upto >= 3.6:
                        stG(n - 1)
                sc.finish()
    return nc


def prep_M_weights(w_in_l, core):
    hh, vh = core // 2, core % 2
    cols = np.concatenate([
        np.arange(hh * 128, (hh + 1) * 128),
        512 + np.arange(hh * 128, (hh + 1) * 128),
        2048 + np.arange(core * 64, (core + 1) * 64),
        2560 + np.arange(core * 64, (core + 1) * 64),
        3072 + np.arange(core * 64, (core + 1) * 64),
        1024 + hh * 128 + vh * 64 + np.arange(64),
    ])
    w = w_in_l[:, cols]
    return np.ascontiguousarray(w.reshape(8, 128, 512).transpose(1, 0, 2))


NTOK_T = 2050
TW = 410
V_HG, V_L1G, V_L1B, V_CW0, V_CW1, V_CW2, V_CB, V_L2G, V_L2B, V_HM, NV_T = 0, 4, 12, 20, 64, 108, 152, 196, 204, 212, 213


def build_T():
    BW = 1025
    SUBS = [(0, 410), (410, 410), (820, 205)]
    NMAX = 410
    nc = bass.Bass("TRN2", target_bir_lowering=False)
    xTd = nc.dram_tensor("xTc", [D_MODEL, NTOK_T], F32, kind="ExternalInput").ap()
    obd = nc.dram_tensor("obTc", [512, NTOK_T], F32, kind="ExternalInput").ap()
    oad = nc.dram_tensor("oaTc", [512, NTOK_T], F32, kind="ExternalInput").ap()
    pd = nc.dram_tensor("pTc", [256, NTOK_T], F32, kind="ExternalInput").ap()
    vecd = nc.dram_tensor("vec", [128, NV_T], F32, kind="ExternalInput").ap()
    Wg = nc.dram_tensor("Wg", [20, 128, 8, 128], F32, kind="ExternalInput").ap()
    Wa = nc.dram_tensor("Wa", [8, 128, 4, 128], F32, kind="ExternalInput").ap()
    Wb = nc.dram_tensor("Wb", [8, 128, 4, 128], F32, kind="ExternalInput").ap()
    Wo = nc.dram_tensor("Wo", [8, 128, 8, 128], F32, kind="ExternalInput").ap()
    Wu = nc.dram_tensor("Wu", [44, 128, 8, 128], F32, kind="ExternalInput").ap()
    Wd = nc.dram_tensor("Wd", [8, 128, 22, 128], F32, kind="ExternalInput").ap()
    Wpe = nc.dram_tensor("Wpe", [8, 128, 2, 128], F32, kind="ExternalInput").ap()
    Wpg = nc.dram_tensor("Wpg", [8, 128, 8, 128], F32, kind="ExternalInput").ap()
    outd = nc.dram_tensor("outT", [D_MODEL, 2048], F32, kind="ExternalOutput").ap()
    xv = xTd.rearrange("(kc p) t -> p kc t", p=128)
    obv = obd.rearrange("(kc p) t -> p kc t", p=128)
    oav = oad.rearrange("(kc p) t -> p kc t", p=128)
    pv = pd.rearrange("(kc p) t -> p kc t", p=128)
    outv = outd.rearrange("(kc p) t -> p kc t", p=128)

    with contextlib.ExitStack() as es:
        def sb(name, shape, dt):
            return es.enter_context(nc.sbuf_tensor(name, shape, dt))

        def ps(name, shape, dt):
            return es.enter_context(nc.psum_tensor(name, shape, dt))

        vec = sb("vec_sb", [128, NV_T], F32)
        o1024 = sb("o1024", [128, 128], F32)
        o128 = sb("o128", [128, 128], F32)
        R = sb("R", [128, 8, BW], F32)
        xbf = sb("xbf", [128, 8, BW], BF16)
        pb = sb("pb", [128, 2, BW], BF16)
        actb = sb("actb", [128, 22, BW], BF16)
        mgb = actb[:, 0:8, :]
        obb = actb[:, 8:12, :]
        oanb = actb[:, 12:16, :]
        oaf = [sb(f"oaf{i}", [128, NMAX], F32) for i in range(2)]
        tail = sb("tail", [128, 44, 2], F32)
        t0 = [sb(f"t0_{i}", [128, NMAX], F32) for i in range(2)]
        t1 = [sb(f"t1_{i}", [128, NMAX], F32) for i in range(2)]
        t2 = [sb(f"t2_{i}", [128, NMAX], F32) for i in range(2)]
        rstd = sb("rstd", [128, NMAX], F32)
        dsq = sb("dsq", [128, NMAX], F32)
        U_ = [sb(f"U_{i}", [128, NMAX + 2], F32) for i in range(4)]
        cv = [sb(f"cv{i}", [128, NMAX], F32) for i in range(4)]
        gl = [sb(f"gl{i}", [128, NMAX], F32) for i in range(2)]
        pan8 = [sb(f"pan8_{i}", [128, 8, 128], BF16) for i in range(6)]
        pan22 = [sb(f"pan22_{i}", [128, 22, 128], BF16) for i in range(2)]
        P = [ps(f"P{i}", [128, 512], F32) for i in range(8)]

        with nc.Block() as block:
            @block.sync
            def _(_e):
                sc = Sched(nc, es)
                st = {"pan8": 0, "pan22": 0, "ps": 0, "oaf": 0}
                sc.dma("sp", vec[:], vecd[:], "vec", writes=["vec"])
                sc.op("dve", lambda e: e.memset(o1024[:], 1.0 / 1024.0), writes=["o1024"])
                sc.op("dve", lambda e: e.memset(o128[:], 1.0 / 128.0), writes=["o128"])
                sc.op("dve", lambda e: e.memset(tail[:], 0.0), writes=["tail"])

                def load_panel(W_ap, idx, kc):
                    if kc == 22:
                        i = st["pan22"] % 2
                        st["pan22"] += 1
                        buf, key = pan22[i], f"pan22_{i}"
                        sc.dma("pool", buf[:, :, :], W_ap[idx], key, writes=[key])
                        return buf, key
                    i = st["pan8"] % 6
                    st["pan8"] += 1
                    buf, key = pan8[i], f"pan8_{i}"
                    sc.dma("pool", buf[:, 0:kc, :], W_ap[idx], key, writes=[key])
                    return buf, key

                def mm(panel, kc, n, rhs_fn, rhs_keys):
                    buf, key = panel
                    b = st["ps"] % 8
                    st["ps"] += 1
                    for k in range(kc):
                        r = rhs_fn(k)
                        sc.op("pe", lambda e: e.matmul(P[b][:, 0:n], lhsT=buf[:, k, :], rhs=r, start=(k == 0), stop=(k == kc - 1)),
                              reads=[key] + rhs_keys, writes=[f"P{b}"])
                    return P[b][:, 0:n], f"P{b}"

                def stat(ones_ap, okey, n, rhs_list, rhs_keys):
                    b = st["ps"] % 8
                    st["ps"] += 1
                    m = len(rhs_list)
                    for k, r in enumerate(rhs_list):
                        sc.op("pe", lambda e: e.matmul(P[b][:, 0:n], lhsT=ones_ap, rhs=r, start=(k == 0), stop=(k == m - 1)),
                              reads=[okey] + rhs_keys, writes=[f"P{b}"])
                    return P[b][:, 0:n], f"P{b}"

                def layer_norm(gcol, bcol, c0, n, also_bf):
                    cs = slice(c0, c0 + n)
                    mean, mk = stat(o1024[:], "o1024", n, [R[:, k, cs] for k in range(8)], ["R"])
                    for k in range(8):
                        sc.op("dve", lambda e: e.tensor_tensor(out=R[:, k, cs], in0=R[:, k, cs], in1=mean, op=ALU.subtract),
                              reads=["R", mk], writes=["R"])
                    b = st["ps"] % 8
                    st["ps"] += 1
                    for k in range(8):
                        i = k % 2
                        sc.op("act", lambda e: e.activation(out=t0[i][:, 0:n], in_=R[:, k, cs], func=AF.Square),
                              reads=["R"], writes=[f"t0_{i}"])
                        sc.op("pe", lambda e: e.matmul(P[b][:, 0:n], lhsT=o1024[:], rhs=t0[i][:, 0:n], start=(k == 0), stop=(k == 7)),
                              reads=["o1024", f"t0_{i}"], writes=[f"P{b}"])
                    sc.op("act", lambda e: e.activation(out=dsq[:, 0:n], in_=P[b][:, 0:n], func=AF.Sqrt, bias=LN_EPS, scale=1.0),
                          reads=[f"P{b}"], writes=["dsq"])
                    sc.op("dve", lambda e: e.reciprocal(out=rstd[:, 0:n], in_=dsq[:, 0:n]), reads=["dsq"], writes=["rstd"])
                    sc.op("dve", lambda e: e.tensor_tensor(out=R[:, 0, cs], in0=R[:, 0, cs], in1=rstd[:, 0:n], op=ALU.mult),
                          reads=["R", "rstd"], writes=["R"] + [f"Rk{k}" for k in range(8)])
                    for k in range(8):
                        if k > 0:
                            sc.op("dve", lambda e: e.tensor_tensor(out=R[:, k, cs], in0=R[:, k, cs], in1=rstd[:, 0:n], op=ALU.mult),
                                  reads=["rstd", f"Rk{k}"], writes=[f"Rk{k}"])
                        sc.op("act", lambda e: e.activation(out=R[:, k, cs], in_=R[:, k, cs], func=AF.Identity,
                                                            bias=vec[:, bcol + k:bcol + k + 1], scale=vec[:, gcol + k:gcol + k + 1]),
                              reads=[f"Rk{k}", "vec"], writes=[f"Rk{k}"])
                        if also_bf:
                            sc.op("dve", lambda e: e.tensor_copy(out=xbf[:, k, cs], in_=R[:, k, cs]), reads=[f"Rk{k}"], writes=["xbf"])
                    sc.op("dve", lambda e: e.tensor_copy(out=dsq[:, 0:1], in_=rstd[:, 0:1]),
                          reads=[f"Rk{k}" for k in range(8)] + ["rstd"], writes=["R", "dsq"])

                for tile in range(NTOK_T // BW):
                    C0 = tile * BW
                    CS = slice(C0, C0 + BW)
                    sc.dma("sp", R[:, :, :], xv[:, :, CS], "R", writes=["R"])
                    sc.dma("pool", xbf[:, :, :], xv[:, :, CS], "xbf", writes=["xbf"])
                    sc.dma("pool", obb, obv[:, :, CS], "obb", writes=["obb", "actb"])
                    sc.dma("pool", pb[:, :, :], pv[:, :, CS], "pb", writes=["pb"])
                    for h in range(4):
                        pan = load_panel(Wg, h, 8)
                        for (c0, n) in SUBS:
                            cs = slice(c0, c0 + n)
                            i = st["oaf"] % 2
                            st["oaf"] += 1
                            sc.dma("sp", oaf[i][:, 0:n], oav[:, h, C0 + c0:C0 + c0 + n], f"oaf{i}", writes=[f"oaf{i}"])
                            sc.op("act", lambda e: e.activation(out=t0[i][:, 0:n], in_=oaf[i][:, 0:n], func=AF.Square),
                                  reads=[f"oaf{i}"], writes=[f"t0_{i}"])
                            ms, msk = stat(o128[:], "o128", n, [t0[i][:, 0:n]], [f"t0_{i}"])
                            sc.op("act", lambda e: e.activation(out=dsq[:, 0:n], in_=ms, func=AF.Sqrt, bias=RMS_EPS, scale=1.0),
                                  reads=[msk], writes=["dsq"])
                            sc.op("dve", lambda e: e.reciprocal(out=rstd[:, 0:n], in_=dsq[:, 0:n]), reads=["dsq"], writes=["rstd"])
                            gp, gk = mm(pan, 8, n, lambda k: xbf[:, k, cs], ["xbf"])
                            sc.op("act", lambda e: e.activation(out=t1[i][:, 0:n], in_=gp, func=AF.Silu), reads=[gk], writes=[f"t1_{i}"])
                            sc.op("dve", lambda e: e.scalar_tensor_tensor(out=t2[i][:, 0:n], in0=oaf[i][:, 0:n], scalar=vec[:, V_HG + h:V_HG + h + 1],
                                                                          in1=rstd[:, 0:n], op0=ALU.mult, op1=ALU.mult),
                                  reads=[f"oaf{i}", "rstd", "vec"], writes=[f"t2_{i}"])
                            sc.op("dve", lambda e: e.tensor_tensor(out=oanb[:, h, cs], in0=t2[i][:, 0:n], in1=t1[i][:, 0:n], op=ALU.mult),
                                  reads=[f"t2_{i}", f"t1_{i}"], writes=["oanb", "actb"])
                    for oc in range(8):
                        pga, pgb = load_panel(Wg, 4 + oc, 8), load_panel(Wg, 12 + oc, 8)
                        pya, pyb = load_panel(Wa, oc, 4), load_panel(Wb, oc, 4)
                        for (c0, n) in SUBS:
                            cs = slice(c0, c0 + n)
                            gap, gak = mm(pga, 8, n, lambda k: xbf[:, k, cs], ["xbf"])
                            gbp, gbk = mm(pgb, 8, n, lambda k: xbf[:, k, cs], ["xbf"])
                            yap, yak = mm(pya, 4, n, lambda k: oanb[:, k, cs], ["oanb"])
                            ybp, ybk = mm(pyb, 4, n, lambda k: obb[:, k, cs], ["obb"])
                            sc.op("act", lambda e: e.activation(out=t0[0][:, 0:n], in_=gap, func=AF.Sigmoid), reads=[gak], writes=["t0_0"])
                            sc.op("act", lambda e: e.activation(out=t0[1][:, 0:n], in_=gbp, func=AF.Sigmoid), reads=[gbk], writes=["t0_1"])
                            sc.op("dve", lambda e: e.tensor_tensor(out=t1[0][:, 0:n], in0=t0[0][:, 0:n], in1=yap, op=ALU.mult),
                                  reads=["t0_0", yak], writes=["t1_0"])
                            sc.op("dve", lambda e: e.tensor_tensor(out=t1[1][:, 0:n], in0=t0[1][:, 0:n], in1=ybp, op=ALU.mult),
                                  reads=["t0_1", ybk], writes=["t1_1"])
                            sc.op("dve", lambda e: e.tensor_tensor(out=mgb[:, oc, cs], in0=t1[0][:, 0:n], in1=t1[1][:, 0:n], op=ALU.add),
                                  reads=["t1_0", "t1_1"], writes=["mgb", "actb"])
                    for oc in range(8):
                        pan = load_panel(Wo, oc, 8)
                        for (c0, n) in SUBS:
                            cs = slice(c0, c0 + n)
                            hp, hk = mm(pan, 8, n, lambda k: mgb[:, k, cs], ["mgb"])
                            sc.op("dve", lambda e: e.scalar_tensor_tensor(out=R[:, oc, cs], in0=R[:, oc, cs], scalar=DN_ALPHA, in1=hp,
                                                                          op0=ALU.mult, op1=ALU.add),
                                  reads=["R", hk], writes=["R"])
                    for (c0, n) in SUBS:
                        layer_norm(V_L1G, V_L1B, c0, n, True)
                    for j in range(22):
                        pans = [load_panel(Wu, j, 8), load_panel(Wu, 22 + j, 8)]
                        for si, (c0, n) in enumerate(SUBS):
                            cs = slice(c0, c0 + n)
                            for half in range(2):
                                ch = half * 22 + j
                                i = half + 2 * (si % 2)
                                up, uk = mm(pans[half], 8, n, lambda k: xbf[:, k, cs], ["xbf"])
                                sc.op("act", lambda e: e.activation(out=U_[i][:, 2:2 + n], in_=up, func=AF.Copy),
                                      reads=[uk], writes=[f"U_{i}"])
                                sc.op("dve", lambda e: e.tensor_copy(out=U_[i][:, 0:2], in_=tail[:, ch, :]),
                                      reads=["tail", f"U_{i}"], writes=[f"U_{i}"])
                                if tile == 0 and si == 0:
                                    sc.op("dve", lambda e: e.tensor_scalar(out=U_[i][:, 2:4], in0=U_[i][:, 2:4], scalar1=vec[:, V_HM:V_HM + 1],
                                                                           scalar2=None, op0=ALU.mult),
                                          reads=[f"U_{i}", "vec"], writes=[f"U_{i}"])
                                sc.op("act", lambda e: e.activation(out=cv[i][:, 0:n], in_=U_[i][:, 2:2 + n], func=AF.Identity,
                                                                    bias=vec[:, V_CB + ch:V_CB + ch + 1], scale=vec[:, V_CW2 + ch:V_CW2 + ch + 1]),
                                      reads=[f"U_{i}", "vec"], writes=[f"cv{i}"])
                                sc.op("dve", lambda e: e.scalar_tensor_tensor(out=cv[i][:, 0:n], in0=U_[i][:, 1:1 + n], scalar=vec[:, V_CW1 + ch:V_CW1 + ch + 1],
                                                                              in1=cv[i][:, 0:n], op0=ALU.mult, op1=ALU.add),
                                      reads=[f"U_{i}", f"cv{i}", "vec"], writes=[f"cv{i}"])
                                sc.op("dve", lambda e: e.scalar_tensor_tensor(out=cv[i][:, 0:n], in0=U_[i][:, 0:n], scalar=vec[:, V_CW0 + ch:V_CW0 + ch + 1],
                                                                              in1=cv[i][:, 0:n], op0=ALU.mult, op1=ALU.add),
                                      reads=[f"U_{i}", f"cv{i}", "vec"], writes=[f"cv{i}"])
                                sc.op("act", lambda e: e.activation(out=tail[:, ch, :], in_=U_[i][:, n:n + 2], func=AF.Copy),
                                      reads=[f"U_{i}"], writes=["tail"])
                            iv, ig, gi = 2 * (si % 2), 1 + 2 * (si % 2), si % 2
                            sc.op("act", lambda e: e.activation(out=gl[gi][:, 0:n], in_=cv[ig][:, 0:n], func=AF.Gelu), reads=[f"cv{ig}"], writes=[f"gl{gi}"])
                            sc.op("dve", lambda e: e.tensor_tensor(out=actb[:, j, cs], in0=gl[gi][:, 0:n], in1=cv[iv][:, 0:n], op=ALU.mult),
                                  reads=[f"gl{gi}", f"cv{iv}"], writes=["actb", "mgb", "obb", "oanb"])
                    for oc in range(8):
                        pdn, ppe, ppg = load_panel(Wd, oc, 22), load_panel(Wpe, oc, 2), load_panel(Wpg, oc, 8)
                        for (c0, n) in SUBS:
                            cs = slice(c0, c0 + n)
                            fp_, fk = mm(pdn, 22, n, lambda k: actb[:, k, cs], ["actb"])
                            pep, pek = mm(ppe, 2, n, lambda k: pb[:, k, cs], ["pb"])
                            pgp, pgk = mm(ppg, 8, n, lambda k: xbf[:, k, cs], ["xbf"])
                            sc.op("act", lambda e: e.activation(out=t0[1][:, 0:n], in_=pgp, func=AF.Sigmoid), reads=[pgk], writes=["t0_1"])
                            sc.op("dve", lambda e: e.tensor_tensor(out=t1[0][:, 0:n], in0=t0[1][:, 0:n], in1=pep, op=ALU.mult),
                                  reads=["t0_1", pek], writes=["t1_0"])
                            sc.op("dve", lambda e: e.tensor_tensor(out=t1[1][:, 0:n], in0=t1[0][:, 0:n], in1=fp_, op=ALU.add),
                                  reads=["t1_0", fk], writes=["t1_1"])
                            sc.op("dve", lambda e: e.scalar_tensor_tensor(out=R[:, oc, cs], in0=R[:, oc, cs], scalar=DN_ALPHA, in1=t1[1][:, 0:n],
                                                                          op0=ALU.mult, op1=ALU.add),
                                  reads=["R", "t1_1"], writes=["R"])
                    for si, (c0, n) in enumerate(SUBS):
                        layer_norm(V_L2G, V_L2B, c0, n, False)
                        lo = 2 if (tile == 0 and si == 0) else 0
                        g0 = C0 + c0 + lo - 2
                        sc.dma("sp", outv[:, :, g0:g0 + n - lo], R[:, :, c0 + lo:c0 + n], "out", reads=["R"], is_output=True)
                sc.finish()
    return nc


def _panels(W):
    K_, M_ = W.shape
    return np.ascontiguousarray(W.reshape(K_ // 128, 128, M_ // 128, 128).transpose(2, 1, 0, 3))


def _cols(v):
    return np.ascontiguousarray(v.reshape(-1, 128).T)


_PROGS = {}


def kernel(x, p, lb_logits, w_in, hg_norm_g, w_a, w_b, w_out, ln1_g, ln1_b,
           w_up, conv_w, conv_b, w_down, w_pe, w_pg, ln2_g, ln2_b):
    f32 = np.float32
    x = np.asarray(x, f32)
    p = np.asarray(p, f32)
    S = SEQ
    h = x[0]
    cM = consts_M()
    cores = list(range(NCORES))
    for l in range(DEPTH):
        hT = np.ascontiguousarray(h.T)
        if ("M", l) not in _PROGS:
            _PROGS[("M", l)] = build_M(S, l)
        maps = []
        for c in cores:
            hh = c // 2
            maps.append({"xT": hT, "wM": prep_M_weights(np.asarray(w_in[l], f32), c),
                         "lbl": np.ascontiguousarray(np.asarray(lb_logits, f32)[:, hh * 128:(hh + 1) * 128].T), "cst": cM})
        res = run_bass_kernel_spmd(_PROGS[("M", l)], maps, core_ids=cores)
        obT = np.concatenate([np.asarray(res.results[c]["ob"]).T for c in cores], axis=0)
        oaT = np.concatenate([np.asarray(res.results[c]["oaT"]) for c in cores], axis=0)
        def pad2(a):
            return np.concatenate([np.zeros((a.shape[0], 2), f32), a], axis=1)
        hTp, obTp, oaTp, pTp = pad2(hT), pad2(obT), pad2(oaT), pad2(np.ascontiguousarray(p[l, 0].T))
        wl = np.asarray(w_in[l], f32)
        Wg = _panels(np.concatenate([wl[:, 1536:2048], wl[:, 3584:4608], wl[:, 4608:5632]], axis=1))
        Wa, Wb, Wo = _panels(np.asarray(w_a[l], f32)), _panels(np.asarray(w_b[l], f32)), _panels(np.asarray(w_out[l], f32))
        Wu, Wd = _panels(np.asarray(w_up[l], f32)), _panels(np.asarray(w_down[l], f32))
        Wpe, Wpg = _panels(np.asarray(w_pe[l], f32)), _panels(np.asarray(w_pg[l], f32))
        cw = np.asarray(conv_w[l], f32)
        vec = np.zeros((128, NV_T), f32)
        vec[:, V_HG:V_HG + 4] = _cols(np.asarray(hg_norm_g[l], f32))
        vec[:, V_L1G:V_L1G + 8] = _cols(np.asarray(ln1_g[l], f32))
        vec[:, V_L1B:V_L1B + 8] = _cols(np.asarray(ln1_b[l], f32))
        vec[:, V_CW0:V_CW0 + 44] = _cols(cw[0])
        vec[:, V_CW1:V_CW1 + 44] = _cols(cw[1])
        vec[:, V_CW2:V_CW2 + 44] = _cols(cw[2])
        vec[:, V_CB:V_CB + 44] = _cols(np.asarray(conv_b[l], f32))
        vec[:, V_L2G:V_L2G + 8] = _cols(np.asarray(ln2_g[l], f32))
        vec[:, V_L2B:V_L2B + 8] = _cols(np.asarray(ln2_b[l], f32))
        if "T" not in _PROGS:
            _PROGS["T"] = build_T()
        maps = []
        for c in cores:
            sl = slice(c * 2048, c * 2048 + NTOK_T)
            v = vec.copy()
            v[:, V_HM] = 0.0 if c == 0 else 1.0
            maps.append({"xTc": np.ascontiguousarray(hTp[:, sl]), "obTc": np.ascontiguousarray(obTp[:, sl]),
                         "oaTc": np.ascontiguousarray(oaTp[:, sl]), "pTc": np.ascontiguousarray(pTp[:, sl]), "vec": v,
                         "Wg": Wg, "Wa": Wa, "Wb": Wb, "Wo": Wo, "Wu": Wu, "Wd": Wd, "Wpe": Wpe, "Wpg": Wpg})
        res = run_bass_kernel_spmd(_PROGS["T"], maps, core_ids=cores)
        outT = np.concatenate([np.asarray(res.results[c]["outT"]) for c in cores], axis=1)
        h = np.ascontiguousarray(outT.T)
    return h[None].astype(f32)
```

```python
import contextlib
import types
import numpy as np
import concourse.bass as bass
import concourse.mybir as mybir
from concourse.bass_utils import run_bass_kernel_spmd

F32 = mybir.dt.float32
BF16 = mybir.dt.bfloat16
AF = mybir.ActivationFunctionType
ALU = mybir.AluOpType

D_MODEL = 1024
SEQ = 16384
DEPTH = 2
NCORES = 8
D_FF = 2816
DN_ALPHA = (2 * DEPTH) ** 0.25
LN_EPS = 1e-5
RMS_EPS = 1e-6
F_FLOOR = 1e-30


class Sched:
    WIN = 512

    def __init__(self, nc, es):
        self.nc = nc
        self.es = es
        self.engs = {"pe": nc.tensor, "act": nc.scalar, "dve": nc.vector, "pool": nc.gpsimd, "sp": nc.sync}
        self.ops = []
        self.seq = {}
        self.last_w = {}
        self.readers = {}
        self.out_ids = []

    def _deps(self, reads, writes):
        d = []
        for b in reads:
            if b in self.last_w:
                d.append(self.last_w[b])
        for b in writes:
            if b in self.last_w:
                d.append(self.last_w[b])
            d.extend(self.readers.get(b, []))
        return d

    def _commit(self, i, reads, writes):
        for b in reads:
            self.readers.setdefault(b, []).append(i)
        for b in writes:
            self.last_w[b] = i
            self.readers[b] = []

    def _add(self, kind, e, payload, stream, reads, writes):
        deps = self._deps(reads, writes)
        sq = self.seq.get(stream, 0)
        self.seq[stream] = sq + 1
        i = len(self.ops)
        self.ops.append((kind, e, payload, deps, stream, sq))
        self._commit(i, reads, writes)
        return i

    @staticmethod
    def _freeze(fn):
        if fn.__closure__ is None:
            return fn
        cells = []
        for c in fn.__closure__:
            try:
                cells.append(types.CellType(c.cell_contents))
            except ValueError:
                cells.append(c)
        return types.FunctionType(fn.__code__, fn.__globals__, fn.__name__, fn.__defaults__, tuple(cells))

    def op(self, e, fn, reads=(), writes=()):
        return self._add("op", e, self._freeze(fn), e, reads, writes)

    def dma(self, q, out, in_, key, reads=(), writes=(), is_output=False):
        i = self._add("dma", q, (out, in_), "dma_" + key, reads, writes)
        if is_output:
            self.out_ids.append(i)
        return i

    def finish(self):
        ops = self.ops
        n = len(ops)
        seen = {e: {} for e in self.engs}
        waits = [[] for _ in range(n)]
        needs = [False] * n
        for i, (kind, e, payload, deps, stream, sq) in enumerate(ops):
            best = {}
            for d in deps:
                ds, dq = ops[d][4], ops[d][5]
                if e == "pe" and ds == "pe":
                    continue
                if seen[e].get(ds, -1) >= dq:
                    continue
                if ds not in best or best[ds][1] < dq:
                    best[ds] = (d, dq)
            for ds, (d, dq) in best.items():
                waits[i].append(d)
                seen[e][ds] = dq
                needs[d] = True
        final_waits = []
        lastout = {}
        for i in self.out_ids:
            lastout[ops[i][4]] = i
        for i in ops and self.out_ids:
            needs[i] = True
        cnt = {}
        sems = {}
        tok = [None] * n
        for i, (kind, e, payload, deps, stream, sq) in enumerate(ops):
            if kind == "dma" or needs[i]:
                step = 16 if kind == "dma" else 1
                c = cnt.get(stream, 0) + step
                cnt[stream] = c
                w = (c - 1) // self.WIN
                if (stream, w) not in sems:
                    sems[(stream, w)] = self.es.enter_context(self.nc.semaphore(f"s_{stream}_{w}"))
                tok[i] = (sems[(stream, w)], c - w * self.WIN, step)
        for i, (kind, e, payload, deps, stream, sq) in enumerate(ops):
            eng = self.engs[e]
            for d in waits[i]:
                eng.wait_ge(tok[d][0], tok[d][1])
            if kind == "dma":
                ins = eng.dma_start(out=payload[0], in_=payload[1])
            else:
                ins = payload(eng)
            if tok[i] is not None:
                ins.then_inc(tok[i][0], tok[i][2])
        done = set()
        for i in reversed(self.out_ids):
            key = id(tok[i][0])
            if key in done:
                continue
            done.add(key)
            self.engs["sp"].wait_ge(tok[i][0], tok[i][1])


C_ID, C_J, C_A, C_A2, C_SH, C_SH2, C_NEG, C_MD, C_MH, C_RST = 0, 128, 256, 384, 512, 640, 768, 896, 1024, 1088
CW_M = 1600


def consts_M():
    c = np.zeros((128, CW_M), np.float32)
    i = np.arange(128)
    c[i, C_ID + i] = 1.0
    c[i, C_J + (127 - i)] = 1.0
    for j in range(128):
        if 126 - j >= 0:
            c[126 - j, C_A + j] += 1.0
        c[127 - j, C_A + j] -= 1.0
    c[127, C_A2 + 127] = 1.0
    for t in range(1, 128):
        c[t - 1, C_SH + t] = 1.0
    c[127, C_SH2 + 0] = 1.0
    ii, jj = np.meshgrid(i, i, indexing="ij")
    c[:, C_NEG:C_NEG + 128] = np.where(ii + jj <= 127, -30000.0, 0.0)
    c[:, C_MD:C_MD + 128] = np.where(ii + jj >= 128, 1.0, 0.0)
    s64 = np.arange(64)
    ss, tt = np.meshgrid(s64, s64, indexing="ij")
    c[0:64, C_MH:C_MH + 64] = np.where(ss <= tt, 1.0, 0.0)
    r = np.ones(512, np.float32)
    r[0::64] = 0.0
    c[:, C_RST:C_RST + 512] = r[None, :]
    return c


def build_M(S, layer, upto=9):
    NQB = S // 128
    NTT = S // 512
    nc = bass.Bass("TRN2", target_bir_lowering=False)
    xT = nc.dram_tensor("xT", [D_MODEL, S], F32, kind="ExternalInput").ap()
    wM = nc.dram_tensor("wM", [128, 8, 512], F32, kind="ExternalInput").ap()
    lbl = nc.dram_tensor("lbl", [128, DEPTH], F32, kind="ExternalInput").ap()
    cst = nc.dram_tensor("cst", [128, CW_M], F32, kind="ExternalInput").ap()
    ob_o = nc.dram_tensor("ob", [S, 64], F32, kind="ExternalOutput").ap()
    oaT_o = nc.dram_tensor("oaT", [64, S], F32, kind="ExternalOutput").ap()
    xT_v = xT.rearrange("(kc p) t -> p kc t", p=128)

    with contextlib.ExitStack() as es:
        def sb(name, shape, dt):
            return es.enter_context(nc.sbuf_tensor(name, shape, dt))

        def ps(name, shape, dt):
            return es.enter_context(nc.psum_tensor(name, shape, dt))

        cf = sb("cf", [128, CW_M], F32)
        cb = sb("cb", [128, CW_M], BF16)
        wb = sb("wb", [128, 8, 512], BF16)
        QT = sb("QT", [128, S], BF16)
        KTr = sb("KTr", [128, S], BF16)
        DVr = sb("DVr", [128, NQB, 64], BF16)
        Vf = sb("Vf", [128, NQB, 64], F32)
        xb = [sb(f"xb{i}", [128, 8, 512], BF16) for i in range(2)]
        Kb = sb("Kb", [128, 4, 64], BF16)
        vhb = sb("vhb", [64, 8, 64], BF16)
        zeros = sb("zeros", [128, 1024], F32)
        lbt = sb("lbt", [128, 16], F32)
        bb = sb("bb", [128, 512], F32)
        e1 = sb("e1", [128, 512], F32)
        e2 = sb("e2", [128, 512], F32)
        qp = sb("qp", [128, 512], BF16)
        kp = sb("kp", [128, 512], BF16)
        sc8 = sb("sc8", [128, 64], F32)
        state = sb("state", [128, 64], F32)
        srb = [sb(f"srb{i}", [128, 64], BF16) for i in range(2)]
        scm = [sb(f"scm{i}", [64, 64], BF16) for i in range(8)]
        kTs = [sb(f"kTs{i}", [64, 128], BF16) for i in range(8)]
        utmp = sb("utmp", [128, 64], F32)
        oas = [sb(f"oas{i}", [64, 512], F32) for i in range(2)]
        omb = [sb(f"omb{i}", [128, 1024], F32) for i in range(2)]
        Eb = [sb(f"Eb{i}", [128, 1024], BF16) for i in range(4)]
        ETs = [sb(f"ETs{i}", [128, 1024], BF16) for i in range(2)]
        qs, sg, ff, kk = omb[0][:, 0:512], omb[0][:, 512:1024], omb[1][:, 0:512], omb[1][:, 512:1024]
        gg = ff
        obs = [sb(f"obs{i}", [128, 64], F32) for i in range(2)]
        Zp = [ps(f"Zp{i}", [128, 1024], F32) for i in range(2)]
        Ap = [ps(f"Ap{i}", [128, 512], F32) for i in range(2)]
        Fp = [Zp[0][:, 0:512], Zp[0][:, 512:1024], Zp[1][:, 0:512], Ap[0][:, :], Ap[1][:, :], Zp[1][:, 512:1024]]
        ET32 = ps("ET32", [128, 1024], F32)

        identb = cb[:, C_ID:C_ID + 128]
        Jb = cb[:, C_J:C_J + 128]
        NEGb = cb[:, C_NEG:C_NEG + 128]
        MDb = cb[:, C_MD:C_MD + 128]
        Af = cf[:, C_A:C_A + 128]
        A2f = cf[:, C_A2:C_A2 + 128]
        SHf = cf[:, C_SH:C_SH + 128]
        SH2f = cf[:, C_SH2:C_SH2 + 128]
        MHf = cf[0:64, C_MH:C_MH + 64]
        RSTf = cf[:, C_RST:C_RST + 512]

        with nc.Block() as block:
            @block.sync
            def _(_e):
                sc = Sched(nc, es)
                sc.dma("sp", cf[:], cst[:], "cf", writes=["cf"])
                sc.dma("pool", cb[:], cst[:], "cb", writes=["cb"])
                sc.dma("pool", wb[:], wM[:], "wb", writes=["wb"])
                sc.dma("sp", lbt[:, 0:DEPTH], lbl[:], "lbt", writes=["lbt"])
                sc.op("dve", lambda e: e.memset(zeros[:], 0.0), writes=["zeros"])
                sc.op("dve", lambda e: e.memset(state[:], 0.0), writes=["state"])
                sc.op("pool", lambda e: e.memset(QT[64:128, :], 0.0), writes=["QTpad"])
                sc.op("pool", lambda e: e.memset(KTr[64:128, :], 0.0), writes=["KTpad"])
                LB, OM, NOM = lbt[:, 8:9], lbt[:, 9:10], lbt[:, 10:11]
                mx = lbt[:, 4:5]
                sc.op("dve", lambda e: e.tensor_reduce(out=mx, in_=lbt[:, 0:DEPTH], axis=mybir.AxisListType.X, op=ALU.max),
                      reads=["lbt"], writes=["lb_mx"])
                sc.op("dve", lambda e: e.tensor_scalar(out=lbt[:, 5:6], in0=mx, scalar1=-1.0, scalar2=None, op0=ALU.mult),
                      reads=["lb_mx"], writes=["lb_nmx"])
                sc.op("act", lambda e: e.activation(out=lbt[:, 2:2 + DEPTH], in_=lbt[:, 0:DEPTH], func=AF.Exp,
                                                    bias=lbt[:, 5:6], scale=1.0),
                      reads=["lbt", "lb_nmx"], writes=["lb_e"])
                sc.op("dve", lambda e: e.tensor_reduce(out=lbt[:, 6:7], in_=lbt[:, 2:2 + DEPTH], axis=mybir.AxisListType.X, op=ALU.add),
                      reads=["lb_e"], writes=["lb_s"])
                sc.op("dve", lambda e: e.reciprocal(out=lbt[:, 7:8], in_=lbt[:, 6:7]), reads=["lb_s"], writes=["lb_r"])
                if layer == 0:
                    sc.op("dve", lambda e: e.memset(LB, 0.0), reads=["lb_r"], writes=["lb"])
                else:
                    sc.op("dve", lambda e: e.tensor_reduce(out=lbt[:, 11:12], in_=lbt[:, 3:2 + layer + 1],
                                                           axis=mybir.AxisListType.X, op=ALU.add),
                          reads=["lb_e"], writes=["lb_p"])
                    sc.op("dve", lambda e: e.tensor_tensor(out=LB, in0=lbt[:, 11:12], in1=lbt[:, 7:8], op=ALU.mult),
                          reads=["lb_p", "lb_r"], writes=["lb"])
                sc.op("dve", lambda e: e.tensor_scalar(out=OM, in0=LB, scalar1=-1.0, scalar2=1.0, op0=ALU.mult, op1=ALU.add),
                      reads=["lb"], writes=["om"])
                sc.op("dve", lambda e: e.tensor_scalar(out=NOM, in0=OM, scalar1=-1.0, scalar2=None, op0=ALU.mult),
                      reads=["om"], writes=["nom"])

                for tt in range(NTT if upto >= 1 else 0):
                    X = xb[tt % 2]
                    xk = f"xb{tt % 2}"
                    sc.dma("pool", X[:], xT_v[:, :, tt * 512:(tt + 1) * 512], xk, writes=[xk])
                    for oi, (bank, c0) in enumerate([(0, 0), (1, 128), (2, 256)]):
                        for kc in range(8):
                            sc.op("pe", lambda e, bank=bank, c0=c0, kc=kc: e.matmul(
                                Fp[bank][:, :], lhsT=wb[:, kc, c0:c0 + 128], rhs=X[:, kc, :], start=(kc == 0), stop=(kc == 7)),
                                reads=[xk, "wb"], writes=[f"F{bank}"])
                    if upto < 1.05:
                        sc.op("act", lambda e: e.activation(out=QT[0:64, tt * 512:(tt + 1) * 512], in_=Fp[2][0:64, :], func=AF.Copy),
                              reads=["F2"], writes=[f"QT{tt}"])
                        continue
                    for blk in range(4):
                        for kc in range(8):
                            sc.op("pe", lambda e, blk=blk, kc=kc: e.matmul(
                                Fp[3][:, blk * 128:(blk + 1) * 128], lhsT=X[:, kc, blk * 128:(blk + 1) * 128],
                                rhs=wb[:, kc, 320:448], start=(kc == 0), stop=(kc == 7)),
                                reads=[xk, "wb"], writes=["F3"])
                    for c in range(8 if upto >= 1.2 else 0):
                        for kc in range(8):
                            sc.op("pe", lambda e, c=c, kc=kc: e.matmul(
                                Fp[4][0:64, c * 64:(c + 1) * 64], lhsT=X[:, kc, c * 64:(c + 1) * 64],
                                rhs=wb[:, kc, 448:512], start=(kc == 0), stop=(kc == 7)),
                                reads=[xk, "wb"], writes=["F4"])
                    sc.op("act", lambda e: e.activation(out=QT[0:64, tt * 512:(tt + 1) * 512], in_=Fp[2][0:64, :], func=AF.Copy),
                          reads=["F2"], writes=[f"QT{tt}"])
                    F3v = Fp[3][:, :].rearrange("p (b c) -> p b c", b=4)
                    if upto >= 1.12:
                        sc.op("dve", lambda e: e.tensor_copy(out=Kb[:, :, :], in_=F3v[:, :, 0:64]), reads=["F3"], writes=["Kb"])
                    if upto >= 1.14:
                        sc.op("dve", lambda e: e.tensor_copy(out=Vf[:, tt * 4:(tt + 1) * 4, :], in_=F3v[:, :, 64:128]),
                              reads=["F3"], writes=[f"Vf{tt}"])
                    if upto >= 1.2:
                        sc.op("dve", lambda e: e.tensor_copy(out=vhb[:, :, :], in_=Fp[4][0:64, :].rearrange("p (c v) -> p c v", c=8)),
                              reads=["F4"], writes=["vhb"])
                    if upto < 1.3:
                        continue
                    for blk in range(4):
                        sc.op("pe", lambda e, blk=blk: e.matmul(
                            Fp[2][0:64, (3 - blk) * 128:(4 - blk) * 128], lhsT=Kb[:, blk, :], rhs=Jb, start=True, stop=True),
                            reads=["Kb", "cb"], writes=["F2"])
                    rb0 = NQB - 1 - (tt * 4 + 3)
                    sc.op("act", lambda e: e.activation(out=KTr[0:64, rb0 * 128:(rb0 + 4) * 128], in_=Fp[2][0:64, :], func=AF.Copy),
                          reads=["F2"], writes=[f"KTr{tt}"])
                    if upto < 1.5:
                        continue
                    for blk in range(4):
                        bs = tt * 4 + blk
                        sc.op("pe", lambda e, blk=blk, bs=bs: e.matmul(
                            Fp[3][:, (3 - blk) * 64:(4 - blk) * 64], lhsT=Af, rhs=Vf[:, bs, :], start=True, stop=(bs == 0)),
                            reads=[f"Vf{tt}", "cf"], writes=["F3"])
                        if bs > 0:
                            sc.op("pe", lambda e, blk=blk, bs=bs: e.matmul(
                                Fp[3][:, (3 - blk) * 64:(4 - blk) * 64], lhsT=A2f, rhs=Vf[:, bs - 1, :], start=False, stop=True),
                                reads=[f"Vf{(bs - 1) // 4}", "cf"], writes=["F3"])
                    sc.op("dve", lambda e: e.tensor_copy(out=DVr[:, rb0:rb0 + 4, :],
                                                         in_=Fp[3][:, 0:256].rearrange("p (b c) -> p b c", b=4)),
                          reads=["F3"], writes=[f"DVr{tt}"])

                    if upto < 2:
                        continue
                    sc.op("act", lambda e: e.activation(out=qs[:], in_=Fp[0][:, :], func=AF.Silu), reads=["F0"], writes=["qs"])
                    sc.op("act", lambda e: e.activation(out=sg[:], in_=Fp[1][:, :], func=AF.Sigmoid), reads=["F1"], writes=["sg"])
                    sc.op("dve", lambda e: e.tensor_scalar(out=ff[:], in0=sg[:], scalar1=OM, scalar2=LB, op0=ALU.mult, op1=ALU.add),
                          reads=["sg", "om", "lb"], writes=["ff"])
                    sc.op("dve", lambda e: e.tensor_scalar(out=ff[:], in0=ff[:], scalar1=F_FLOOR, scalar2=None, op0=ALU.max),
                          reads=["ff"], writes=["ff"])
                    sc.op("act", lambda e: e.activation(out=ff[:], in_=ff[:], func=AF.Ln), reads=["ff"], writes=["ff"])
                    sc.op("dve", lambda e: e.tensor_scalar(out=kk[:], in0=sg[:], scalar1=NOM, scalar2=OM, op0=ALU.mult, op1=ALU.add),
                          reads=["sg", "nom", "om"], writes=["kk"])
                    sc.op("dve", lambda e: e.tensor_tensor_scan(out=bb[:], data0=RSTf, data1=gg[:], initial=0.0,
                                                                op0=ALU.mult, op1=ALU.add),
                          reads=["ff", "cf"], writes=["bb"])
                    bbv = bb[:].rearrange("p (c t) -> p c t", c=8)
                    BM, D8, BL, NBM = sc8[:, 0:8], sc8[:, 8:16], sc8[:, 16:24], sc8[:, 24:32]
                    sc.op("dve", lambda e: e.tensor_copy(out=BM, in_=bbv[:, :, 31]), reads=["bb"], writes=["bm"])
                    sc.op("dve", lambda e: e.tensor_copy(out=BL, in_=bbv[:, :, 63]), reads=["bb"], writes=["bl"])
                    sc.op("dve", lambda e: e.tensor_scalar(out=NBM, in0=BM, scalar1=-1.0, scalar2=None, op0=ALU.mult),
                          reads=["bm"], writes=["nbm"])
                    sc.op("dve", lambda e: e.tensor_tensor(out=D8, in0=BL, in1=BM, op=ALU.subtract),
                          reads=["bl", "bm"], writes=["d8"])
                    sc.op("act", lambda e: e.activation(out=sc8[:, 32:56], in_=sc8[:, 0:24], func=AF.Exp),
                          reads=["bm", "d8", "bl"], writes=["er", "c2", "el"])
                    for c in range(8):
                        sl = slice(c * 64, (c + 1) * 64)
                        sc.op("act", lambda e, c=c, sl=sl: e.activation(out=e1[:, sl], in_=bb[:, sl], func=AF.Exp,
                                                                        bias=sc8[:, 24 + c:25 + c], scale=1.0),
                              reads=["bb", "nbm"], writes=["e1"])
                        sc.op("act", lambda e, c=c, sl=sl: e.activation(out=e2[:, sl], in_=bb[:, sl], func=AF.Exp,
                                                                        bias=sc8[:, c:c + 1], scale=-1.0),
                              reads=["bb", "bm"], writes=["e2"])
                    sc.op("dve", lambda e: e.tensor_tensor(out=qp[:], in0=qs[:], in1=e1[:], op=ALU.mult),
                          reads=["qs", "e1"], writes=["qp"])
                    sc.op("dve", lambda e: e.tensor_tensor(out=kp[:], in0=kk[:], in1=e2[:], op=ALU.mult),
                          reads=["kk", "e2"], writes=["kp"])
                    for c in range(8):
                        sl = slice(c * 64, (c + 1) * 64)
                        sc.op("pe", lambda e: e.matmul(Fp[3][0:64, sl], lhsT=kp[:, sl], rhs=qp[:, sl], start=True, stop=True),
                              reads=["kp", "qp"], writes=["F3"])
                    for c in range(8):
                        sl = slice(c * 64, (c + 1) * 64)
                        sc.op("pe", lambda e: e.matmul(ET32[0:64, c * 128:(c + 1) * 128], lhsT=kp[:, sl], rhs=identb, start=True, stop=True),
                              reads=["kp", "cb"], writes=["B0"])
                    for c in range(8):
                        sl = slice(c * 64, (c + 1) * 64)
                        sc.op("dve", lambda e: e.tensor_tensor(out=scm[c][:], in0=Fp[3][0:64, sl], in1=MHf, op=ALU.mult),
                              reads=["F3", "cf"], writes=[f"scm{c}"])
                        sc.op("act", lambda e: e.activation(out=kTs[c][:], in_=ET32[0:64, c * 128:(c + 1) * 128], func=AF.Copy),
                              reads=["B0"], writes=[f"kTs{c}"])
                    for c in range(8):
                        sc.op("pe", lambda e: e.matmul(Fp[5][:, c * 64:(c + 1) * 64], lhsT=kTs[c][:], rhs=vhb[:, c, :], start=True, stop=True),
                              reads=[f"kTs{c}", "vhb"], writes=["F5"])
                    for c in range(8):
                        sl = slice(c * 64, (c + 1) * 64)
                        s2 = (tt * 8 + c) % 2
                        sc.op("dve", lambda e: e.tensor_scalar(out=srb[s2][:], in0=state[:], scalar1=sc8[:, 32 + c:33 + c],
                                                               scalar2=None, op0=ALU.mult),
                              reads=["state", "er"], writes=[f"srb{s2}"])
                        sc.op("pe", lambda e: e.matmul(Fp[4][0:64, sl], lhsT=srb[s2][:], rhs=qp[:, sl], start=True, stop=False),
                              reads=[f"srb{s2}", "qp"], writes=["F4"])
                        sc.op("pe", lambda e: e.matmul(Fp[4][0:64, sl], lhsT=vhb[:, c, :], rhs=scm[c][:], start=False, stop=True),
                              reads=[f"scm{c}", "vhb"], writes=["F4"])
                        sc.op("dve", lambda e: e.tensor_scalar(out=utmp[:], in0=Fp[5][:, c * 64:(c + 1) * 64], scalar1=sc8[:, 40 + c:41 + c],
                                                               scalar2=None, op0=ALU.mult),
                              reads=["F5", "c2"], writes=["utmp"])
                        sc.op("dve", lambda e: e.scalar_tensor_tensor(out=state[:], in0=state[:], scalar=sc8[:, 48 + c:49 + c],
                                                                      in1=utmp[:], op0=ALU.mult, op1=ALU.add),
                              reads=["state", "utmp", "el"], writes=["state"])
                    ok = f"oas{tt % 2}"
                    sc.op("act", lambda e: e.activation(out=oas[tt % 2][:], in_=Fp[4][0:64, :], func=AF.Copy), reads=["F4"], writes=[ok])
                    sc.dma("sp", oaT_o[:, tt * 512:(tt + 1) * 512], oas[tt % 2][:], f"oaT_out{tt % 2}", reads=[ok], is_output=True)

                TWK = 1024
                def qb_tiles(qb):
                    rstart = (NQB - 1 - qb) * 128
                    L = S - rstart
                    nt = (L + TWK - 1) // TWK
                    return [(qb, ti, rstart + ti * TWK, min(TWK, S - rstart - ti * TWK), ti == nt - 1) for ti in range(nt)]

                tiles = []
                prev_of = []
                for q0 in range(0, NQB, 2):
                    la, lb_ = qb_tiles(q0), (qb_tiles(q0 + 1) if q0 + 1 < NQB else [])
                    last_idx = {}
                    for i in range(max(len(la), len(lb_))):
                        for lst in (la, lb_):
                            if i < len(lst):
                                t = lst[i]
                                prev_of.append(last_idx.get(t[0], -1))
                                last_idx[t[0]] = len(tiles)
                                tiles.append(t)
                if upto < 3:
                    tiles = []
                NT = len(tiles)
                qt_keys = [f"QT{t}" for t in range(NTT)] + ["QTpad"]
                kt_keys = [f"KTr{t}" for t in range(NTT)] + ["KTpad"]
                dv_keys = [f"DVr{t}" for t in range(NTT)]
                vf_keys = [f"Vf{t}" for t in range(NTT)]
                zkeys = [["F0", "F1"], ["F2", "F5"]]
                hg_alias = [["qs", "sg"], ["ff", "kk"]]

                def stA(n):
                    qb, ti, r0, w, last = tiles[n]
                    z = Zp[n % 2]
                    for h0 in range(0, w, 512):
                        wh = min(512, w - h0)
                        zk = zkeys[n % 2][h0 // 512]
                        diag = (ti == 0 and h0 == 0)
                        sc.op("pe", lambda e: e.matmul(z[:, h0:h0 + wh], lhsT=QT[:, qb * 128:(qb + 1) * 128], rhs=KTr[:, r0 + h0:r0 + h0 + wh],
                                                       start=True, stop=(not diag)),
                              reads=qt_keys + kt_keys, writes=[zk])
                        if diag:
                            sc.op("pe", lambda e: e.matmul(z[:, 0:128], lhsT=identb, rhs=NEGb, start=False, stop=True),
                                  reads=["cb"], writes=[zk])

                def stB(n):
                    qb, ti, r0, w, last = tiles[n]
                    sc.op("act", lambda e: e.activation(out=omb[n % 2][:, 0:w], in_=Zp[n % 2][:, 0:w], func=AF.Sigmoid, scale=-0.125),
                          reads=zkeys[n % 2], writes=[f"omb{n % 2}"] + hg_alias[n % 2])

                def stC(n):
                    qb, ti, r0, w, last = tiles[n]
                    if ti == 0:
                        init = 1.0
                        rd = []
                    else:
                        pn = prev_of[n]
                        wp = tiles[pn][3]
                        init = Eb[pn % 4][:, wp - 1:wp]
                        rd = [f"Eb{pn % 4}"]
                    sc.op("dve", lambda e: e.tensor_tensor_scan(out=Eb[n % 4][:, 0:w], data0=omb[n % 2][:, 0:w], data1=zeros[:, 0:w],
                                                                initial=init, op0=ALU.mult, op1=ALU.add),
                          reads=[f"omb{n % 2}", "zeros"] + rd, writes=[f"Eb{n % 4}"])

                def stD(n):
                    qb, ti, r0, w, last = tiles[n]
                    for sbk in range(w // 128):
                        sl = slice(sbk * 128, (sbk + 1) * 128)
                        sc.op("pe", lambda e: e.matmul(ET32[:, sl], lhsT=Eb[n % 4][:, sl], rhs=identb, start=True, stop=True),
                              reads=[f"Eb{n % 4}", "cb"], writes=["B0"])

                def stF(n):
                    qb, ti, r0, w, last = tiles[n]
                    sc.op("act", lambda e: e.activation(out=ETs[n % 2][:, 0:w], in_=ET32[:, 0:w], func=AF.Copy),
                          reads=["B0"], writes=[f"ETs{n % 2}"])
                    if ti == 0:
                        sc.op("dve", lambda e: e.tensor_tensor(out=ETs[n % 2][:, 0:128], in0=ETs[n % 2][:, 0:128], in1=MDb, op=ALU.mult),
                              reads=[f"ETs{n % 2}", "cb"], writes=[f"ETs{n % 2}"])

                def stG(n):
                    qb, ti, r0, w, last = tiles[n]
                    acc = Ap[qb % 2]
                    ak = f"F{3 + qb % 2}"
                    for sbk in range(w // 128):
                        sl = slice(sbk * 128, (sbk + 1) * 128)
                        br = r0 // 128 + sbk
                        sc.op("pe", lambda e: e.matmul(acc[:, 0:64], lhsT=ETs[n % 2][:, sl], rhs=DVr[:, br, :],
                                                       start=(ti == 0 and sbk == 0), stop=False),
                              reads=[f"ETs{n % 2}"] + dv_keys, writes=[ak])
                    if last:
                        sc.op("pe", lambda e: e.matmul(acc[:, 0:64], lhsT=SHf, rhs=Vf[:, qb, :], start=False, stop=(qb == 0)),
                              reads=["cf"] + vf_keys, writes=[ak])
                        if qb > 0:
                            sc.op("pe", lambda e: e.matmul(acc[:, 0:64], lhsT=SH2f, rhs=Vf[:, qb - 1, :], start=False, stop=True),
                                  reads=["cf"] + vf_keys, writes=[ak])
                        sc.op("dve", lambda e: e.tensor_copy(out=obs[qb % 2][:], in_=acc[:, 0:64]), reads=[ak], writes=[f"obs{qb % 2}"])
                        sc.dma("sp", ob_o[qb * 128:(qb + 1) * 128, :], obs[qb % 2][:], f"ob_out{qb % 2}", reads=[f"obs{qb % 2}"], is_output=True)

                for n in range(-2, NT + 1):
                    if 0 <= n + 2 < NT:
                        stA(n + 2)
                    if 0 <= n < NT and upto >= 3.3:
                        stC(n)
                    if 0 <= n + 2 < NT and upto >= 3.2:
                        stB(n + 2)
                    if 0 <= n < NT:
                        if upto >= 3.4:
                            stD(n)
                        if upto >= 3.5:
                            stF(n)
                    if 0 <= n - 1 < NT and upto >= 3.6:
                        stG(n - 1)
                sc.finish()
    return nc


def prep_M_weights(w_in_l, core):
    hh, vh = core // 2, core % 2
    cols = np.concatenate([
        np.arange(hh * 128, (hh + 1) * 128),
        512 + np.arange(hh * 128, (hh + 1) * 128),
        2048 + np.arange(core * 64, (core + 1) * 64),
        2560 + np.arange(core * 64, (core + 1) * 64),
        3072 + np.arange(core * 64, (core + 1) * 64),
        1024 + hh * 128 + vh * 64 + np.arange(64),
    ])
    w = w_in_l[:, cols]
    return np.ascontiguousarray(w.reshape(8, 128, 512).transpose(1, 0, 2))


NTOK_T = 2050
TW = 410
V_HG, V_L1G, V_L1B, V_CW0, V_CW1, V_CW2, V_CB, V_L2G, V_L2B, V_HM, NV_T = 0, 4, 12, 20, 64, 108, 152, 196, 204, 212, 213


def build_T():
    BW = 1025
    SUBS = [(0, 410), (410, 410), (820, 205)]
    NMAX = 410
    nc = bass.Bass("TRN2", target_bir_lowering=False)
    xTd = nc.dram_tensor("xTc", [D_MODEL, NTOK_T], F32, kind="ExternalInput").ap()
    obd = nc.dram_tensor("obTc", [512, NTOK_T], F32, kind="ExternalInput").ap()
    oad = nc.dram_tensor("oaTc", [512, NTOK_T], F32, kind="ExternalInput").ap()
    pd = nc.dram_tensor("pTc", [256, NTOK_T], F32, kind="ExternalInput").ap()
    vecd = nc.dram_tensor("vec", [128, NV_T], F32, kind="ExternalInput").ap()
    Wg = nc.dram_tensor("Wg", [20, 128, 8, 128], F32, kind="ExternalInput").ap()
    Wa = nc.dram_tensor("Wa", [8, 128, 4, 128], F32, kind="ExternalInput").ap()
    Wb = nc.dram_tensor("Wb", [8, 128, 4, 128], F32, kind="ExternalInput").ap()
    Wo = nc.dram_tensor("Wo", [8, 128, 8, 128], F32, kind="ExternalInput").ap()
    Wu = nc.dram_tensor("Wu", [44, 128, 8, 128], F32, kind="ExternalInput").ap()
    Wd = nc.dram_tensor("Wd", [8, 128, 22, 128], F32, kind="ExternalInput").ap()
    Wpe = nc.dram_tensor("Wpe", [8, 128, 2, 128], F32, kind="ExternalInput").ap()
    Wpg = nc.dram_tensor("Wpg", [8, 128, 8, 128], F32, kind="ExternalInput").ap()
    outd = nc.dram_tensor("outT", [D_MODEL, 2048], F32, kind="ExternalOutput").ap()
    xv = xTd.rearrange("(kc p) t -> p kc t", p=128)
    obv = obd.rearrange("(kc p) t -> p kc t", p=128)
    oav = oad.rearrange("(kc p) t -> p kc t", p=128)
    pv = pd.rearrange("(kc p) t -> p kc t", p=128)
    outv = outd.rearrange("(kc p) t -> p kc t", p=128)

    with contextlib.ExitStack() as es:
        def sb(name, shape, dt):
            return es.enter_context(nc.sbuf_tensor(name, shape, dt))

        def ps(name, shape, dt):
            return es.enter_context(nc.psum_tensor(name, shape, dt))

        vec = sb("vec_sb", [128, NV_T], F32)
        o1024 = sb("o1024", [128, 128], F32)
        o128 = sb("o128", [128, 128], F32)
        o1024b = sb("o1024b", [128, 128], BF16)
        o128b = sb("o128b", [128, 128], BF16)
        t0b = [sb(f"t0b_{i}", [128, NMAX], BF16) for i in range(2)]
        R = sb("R", [128, 8, BW], F32)
        xbf = sb("xbf", [128, 8, BW], BF16)
        pb = sb("pb", [128, 2, BW], BF16)
        actb = sb("actb", [128, 22, BW], BF16)
        mgb = actb[:, 0:8, :]
        obb = actb[:, 8:12, :]
        oanb = actb[:, 12:16, :]
        oaf = [sb(f"oaf{i}", [128, NMAX], F32) for i in range(2)]
        tail = sb("tail", [128, 44, 2], F32)
        t0 = [sb(f"t0_{i}", [128, NMAX], F32) for i in range(2)]
        t1 = [sb(f"t1_{i}", [128, NMAX], F32) for i in range(2)]
        t2 = [sb(f"t2_{i}", [128, NMAX], F32) for i in range(2)]
        rstd = sb("rstd", [128, NMAX], F32)
        dsq = sb("dsq", [128, NMAX], F32)
        U_ = [sb(f"U_{i}", [128, NMAX + 2], F32) for i in range(4)]
        cv = [sb(f"cv{i}", [128, NMAX], F32) for i in range(4)]
        gl = [sb(f"gl{i}", [128, NMAX], F32) for i in range(2)]
        pan8 = [sb(f"pan8_{i}", [128, 8, 128], BF16) for i in range(6)]
        pan22 = [sb(f"pan22_{i}", [128, 22, 128], BF16) for i in range(2)]
        P = [ps(f"P{i}", [128, 512], F32) for i in range(8)]

        with nc.Block() as block:
            @block.sync
            def _(_e):
                sc = Sched(nc, es)
                st = {"pan8": 0, "pan22": 0, "ps": 0, "oaf": 0}
                sc.dma("sp", vec[:], vecd[:], "vec", writes=["vec"])
                sc.op("dve", lambda e: e.memset(o1024[:], 1.0 / 1024.0), writes=["o1024"])
                sc.op("dve", lambda e: e.memset(o128[:], 1.0 / 128.0), writes=["o128"])
                sc.op("dve", lambda e: e.memset(o1024b[:], 1.0 / 1024.0), writes=["o1024b"])
                sc.op("dve", lambda e: e.memset(o128b[:], 1.0 / 128.0), writes=["o128b"])
                sc.op("dve", lambda e: e.memset(tail[:], 0.0), writes=["tail"])

                def load_panel(W_ap, idx, kc):
                    if kc == 22:
                        i = st["pan22"] % 2
                        st["pan22"] += 1
                        buf, key = pan22[i], f"pan22_{i}"
                        sc.dma("pool", buf[:, :, :], W_ap[idx], key, writes=[key])
                        return buf, key
                    i = st["pan8"] % 6
                    st["pan8"] += 1
                    buf, key = pan8[i], f"pan8_{i}"
                    sc.dma("pool", buf[:, 0:kc, :], W_ap[idx], key, writes=[key])
                    return buf, key

                def mm(panel, kc, n, rhs_fn, rhs_keys):
                    buf, key = panel
                    b = st["ps"] % 8
                    st["ps"] += 1
                    for k in range(kc):
                        r = rhs_fn(k)
                        sc.op("pe", lambda e: e.matmul(P[b][:, 0:n], lhsT=buf[:, k, :], rhs=r, start=(k == 0), stop=(k == kc - 1)),
                              reads=[key] + rhs_keys, writes=[f"P{b}"])
                    return P[b][:, 0:n], f"P{b}"

                def stat(ones_ap, okey, n, rhs_list, rhs_keys):
                    b = st["ps"] % 8
                    st["ps"] += 1
                    m = len(rhs_list)
                    for k, r in enumerate(rhs_list):
                        sc.op("pe", lambda e: e.matmul(P[b][:, 0:n], lhsT=ones_ap, rhs=r, start=(k == 0), stop=(k == m - 1)),
                              reads=[okey] + rhs_keys, writes=[f"P{b}"])
                    return P[b][:, 0:n], f"P{b}"

                def layer_norm(gcol, bcol, c0, n, also_bf):
                    cs = slice(c0, c0 + n)
                    mean, mk = stat(o1024[:], "o1024", n, [R[:, k, cs] for k in range(8)], ["R"])
                    for k in range(8):
                        sc.op("dve", lambda e: e.tensor_tensor(out=R[:, k, cs], in0=R[:, k, cs], in1=mean, op=ALU.subtract),
                              reads=["R", mk], writes=["R"])
                    b = st["ps"] % 8
                    st["ps"] += 1
                    for k in range(8):
                        i = k % 2
                        sc.op("act", lambda e: e.activation(out=t0b[i][:, 0:n], in_=R[:, k, cs], func=AF.Square),
                              reads=["R"], writes=[f"t0b_{i}"])
                        sc.op("pe", lambda e: e.matmul(P[b][:, 0:n], lhsT=o1024b[:], rhs=t0b[i][:, 0:n], start=(k == 0), stop=(k == 7)),
                              reads=["o1024b", f"t0b_{i}"], writes=[f"P{b}"])
                    sc.op("act", lambda e: e.activation(out=dsq[:, 0:n], in_=P[b][:, 0:n], func=AF.Sqrt, bias=LN_EPS, scale=1.0),
                          reads=[f"P{b}"], writes=["dsq"])
                    sc.op("dve", lambda e: e.reciprocal(out=rstd[:, 0:n], in_=dsq[:, 0:n]), reads=["dsq"], writes=["rstd"])
                    sc.op("dve", lambda e: e.tensor_tensor(out=R[:, 0, cs], in0=R[:, 0, cs], in1=rstd[:, 0:n], op=ALU.mult),
                          reads=["R", "rstd"], writes=["R"] + [f"Rk{k}" for k in range(8)])
                    for k in range(8):
                        if k > 0:
                            sc.op("dve", lambda e: e.tensor_tensor(out=R[:, k, cs], in0=R[:, k, cs], in1=rstd[:, 0:n], op=ALU.mult),
                                  reads=["rstd", f"Rk{k}"], writes=[f"Rk{k}"])
                        sc.op("act", lambda e: e.activation(out=R[:, k, cs], in_=R[:, k, cs], func=AF.Identity,
                                                            bias=vec[:, bcol + k:bcol + k + 1], scale=vec[:, gcol + k:gcol + k + 1]),
                              reads=[f"Rk{k}", "vec"], writes=[f"Rk{k}"])
                        if also_bf:
                            sc.op("dve", lambda e: e.tensor_copy(out=xbf[:, k, cs], in_=R[:, k, cs]), reads=[f"Rk{k}"], writes=["xbf"])
                    sc.op("dve", lambda e: e.tensor_copy(out=dsq[:, 0:1], in_=rstd[:, 0:1]),
                          reads=[f"Rk{k}" for k in range(8)] + ["rstd"], writes=["R", "dsq"])

                for tile in range(NTOK_T // BW):
                    C0 = tile * BW
                    CS = slice(C0, C0 + BW)
                    sc.dma("sp", R[:, :, :], xv[:, :, CS], "R", writes=["R"])
                    sc.dma("pool", xbf[:, :, :], xv[:, :, CS], "xbf", writes=["xbf"])
                    sc.dma("pool", obb, obv[:, :, CS], "obb", writes=["obb", "actb"])
                    sc.dma("pool", pb[:, :, :], pv[:, :, CS], "pb", writes=["pb"])
                    for h in range(4):
                        pan = load_panel(Wg, h, 8)
                        for (c0, n) in SUBS:
                            cs = slice(c0, c0 + n)
                            i = st["oaf"] % 2
                            st["oaf"] += 1
                            sc.dma("sp", oaf[i][:, 0:n], oav[:, h, C0 + c0:C0 + c0 + n], f"oaf{i}", writes=[f"oaf{i}"])
                            sc.op("act", lambda e: e.activation(out=t0b[i][:, 0:n], in_=oaf[i][:, 0:n], func=AF.Square),
                                  reads=[f"oaf{i}"], writes=[f"t0b_{i}"])
                            ms, msk = stat(o128b[:], "o128b", n, [t0b[i][:, 0:n]], [f"t0b_{i}"])
                            sc.op("act", lambda e: e.activation(out=dsq[:, 0:n], in_=ms, func=AF.Sqrt, bias=RMS_EPS, scale=1.0),
                                  reads=[msk], writes=["dsq"])
                            sc.op("dve", lambda e: e.reciprocal(out=rstd[:, 0:n], in_=dsq[:, 0:n]), reads=["dsq"], writes=["rstd"])
                            gp, gk = mm(pan, 8, n, lambda k: xbf[:, k, cs], ["xbf"])
                            sc.op("act", lambda e: e.activation(out=t1[i][:, 0:n], in_=gp, func=AF.Silu), reads=[gk], writes=[f"t1_{i}"])
                            sc.op("dve", lambda e: e.scalar_tensor_tensor(out=t2[i][:, 0:n], in0=oaf[i][:, 0:n], scalar=vec[:, V_HG + h:V_HG + h + 1],
                                                                          in1=rstd[:, 0:n], op0=ALU.mult, op1=ALU.mult),
                                  reads=[f"oaf{i}", "rstd", "vec"], writes=[f"t2_{i}"])
                            sc.op("dve", lambda e: e.tensor_tensor(out=oanb[:, h, cs], in0=t2[i][:, 0:n], in1=t1[i][:, 0:n], op=ALU.mult),
                                  reads=[f"t2_{i}", f"t1_{i}"], writes=["oanb", "actb"])
                    for oc in range(8):
                        pga, pgb = load_panel(Wg, 4 + oc, 8), load_panel(Wg, 12 + oc, 8)
                        pya, pyb = load_panel(Wa, oc, 4), load_panel(Wb, oc, 4)
                        for (c0, n) in SUBS:
                            cs = slice(c0, c0 + n)
                            gap, gak = mm(pga, 8, n, lambda k: xbf[:, k, cs], ["xbf"])
                            gbp, gbk = mm(pgb, 8, n, lambda k: xbf[:, k, cs], ["xbf"])
                            yap, yak = mm(pya, 4, n, lambda k: oanb[:, k, cs], ["oanb"])
                            ybp, ybk = mm(pyb, 4, n, lambda k: obb[:, k, cs], ["obb"])
                            sc.op("act", lambda e: e.activation(out=t0[0][:, 0:n], in_=gap, func=AF.Sigmoid), reads=[gak], writes=["t0_0"])
                            sc.op("act", lambda e: e.activation(out=t0[1][:, 0:n], in_=gbp, func=AF.Sigmoid), reads=[gbk], writes=["t0_1"])
                            sc.op("dve", lambda e: e.tensor_tensor(out=t1[0][:, 0:n], in0=t0[0][:, 0:n], in1=yap, op=ALU.mult),
                                  reads=["t0_0", yak], writes=["t1_0"])
                            sc.op("dve", lambda e: e.tensor_tensor(out=t1[1][:, 0:n], in0=t0[1][:, 0:n], in1=ybp, op=ALU.mult),
                                  reads=["t0_1", ybk], writes=["t1_1"])
                            sc.op("dve", lambda e: e.tensor_tensor(out=mgb[:, oc, cs], in0=t1[0][:, 0:n], in1=t1[1][:, 0:n], op=ALU.add),
                                  reads=["t1_0", "t1_1"], writes=["mgb", "actb"])
                    for oc in range(8):
                        pan = load_panel(Wo, oc, 8)
                        for (c0, n) in SUBS:
                            cs = slice(c0, c0 + n)
                            hp, hk = mm(pan, 8, n, lambda k: mgb[:, k, cs], ["mgb"])
                            sc.op("dve", lambda e: e.scalar_tensor_tensor(out=R[:, oc, cs], in0=R[:, oc, cs], scalar=DN_ALPHA, in1=hp,
                                                                          op0=ALU.mult, op1=ALU.add),
                                  reads=["R", hk], writes=["R"])
                    for (c0, n) in SUBS:
                        layer_norm(V_L1G, V_L1B, c0, n, True)
                    for j in range(22):
                        pans = [load_panel(Wu, j, 8), load_panel(Wu, 22 + j, 8)]
                        for si, (c0, n) in enumerate(SUBS):
                            cs = slice(c0, c0 + n)
                            for half in range(2):
                                ch = half * 22 + j
                                i = half + 2 * (si % 2)
                                up, uk = mm(pans[half], 8, n, lambda k: xbf[:, k, cs], ["xbf"])
                                sc.op("act", lambda e: e.activation(out=U_[i][:, 2:2 + n], in_=up, func=AF.Copy),
                                      reads=[uk], writes=[f"U_{i}"])
                                sc.op("dve", lambda e: e.tensor_copy(out=U_[i][:, 0:2], in_=tail[:, ch, :]),
                                      reads=["tail", f"U_{i}"], writes=[f"U_{i}"])
                                if tile == 0 and si == 0:
                                    sc.op("dve", lambda e: e.tensor_scalar(out=U_[i][:, 2:4], in0=U_[i][:, 2:4], scalar1=vec[:, V_HM:V_HM + 1],
                                                                           scalar2=None, op0=ALU.mult),
                                          reads=[f"U_{i}", "vec"], writes=[f"U_{i}"])
                                sc.op("act", lambda e: e.activation(out=cv[i][:, 0:n], in_=U_[i][:, 2:2 + n], func=AF.Identity,
                                                                    bias=vec[:, V_CB + ch:V_CB + ch + 1], scale=vec[:, V_CW2 + ch:V_CW2 + ch + 1]),
                                      reads=[f"U_{i}", "vec"], writes=[f"cv{i}"])
                                sc.op("dve", lambda e: e.scalar_tensor_tensor(out=cv[i][:, 0:n], in0=U_[i][:, 1:1 + n], scalar=vec[:, V_CW1 + ch:V_CW1 + ch + 1],
                                                                              in1=cv[i][:, 0:n], op0=ALU.mult, op1=ALU.add),
                                      reads=[f"U_{i}", f"cv{i}", "vec"], writes=[f"cv{i}"])
                                sc.op("dve", lambda e: e.scalar_tensor_tensor(out=cv[i][:, 0:n], in0=U_[i][:, 0:n], scalar=vec[:, V_CW0 + ch:V_CW0 + ch + 1],
                                                                              in1=cv[i][:, 0:n], op0=ALU.mult, op1=ALU.add),
                                      reads=[f"U_{i}", f"cv{i}", "vec"], writes=[f"cv{i}"])
                                sc.op("act", lambda e: e.activation(out=tail[:, ch, :], in_=U_[i][:, n:n + 2], func=AF.Copy),
                                      reads=[f"U_{i}"], writes=["tail"])
                            iv, ig, gi = 2 * (si % 2), 1 + 2 * (si % 2), si % 2
                            sc.op("act", lambda e: e.activation(out=gl[gi][:, 0:n], in_=cv[ig][:, 0:n], func=AF.Gelu), reads=[f"cv{ig}"], writes=[f"gl{gi}"])
                            sc.op("dve", lambda e: e.tensor_tensor(out=actb[:, j, cs], in0=gl[gi][:, 0:n], in1=cv[iv][:, 0:n], op=ALU.mult),
                                  reads=[f"gl{gi}", f"cv{iv}"], writes=["actb", "mgb", "obb", "oanb"])
                    for oc in range(8):
                        pdn, ppe, ppg = load_panel(Wd, oc, 22), load_panel(Wpe, oc, 2), load_panel(Wpg, oc, 8)
                        for (c0, n) in SUBS:
                            cs = slice(c0, c0 + n)
                            fp_, fk = mm(pdn, 22, n, lambda k: actb[:, k, cs], ["actb"])
                            pep, pek = mm(ppe, 2, n, lambda k: pb[:, k, cs], ["pb"])
                            pgp, pgk = mm(ppg, 8, n, lambda k: xbf[:, k, cs], ["xbf"])
                            sc.op("act", lambda e: e.activation(out=t0[1][:, 0:n], in_=pgp, func=AF.Sigmoid), reads=[pgk], writes=["t0_1"])
                            sc.op("dve", lambda e: e.tensor_tensor(out=t1[0][:, 0:n], in0=t0[1][:, 0:n], in1=pep, op=ALU.mult),
                                  reads=["t0_1", pek], writes=["t1_0"])
                            sc.op("dve", lambda e: e.tensor_tensor(out=t1[1][:, 0:n], in0=t1[0][:, 0:n], in1=fp_, op=ALU.add),
                                  reads=["t1_0", fk], writes=["t1_1"])
                            sc.op("dve", lambda e: e.scalar_tensor_tensor(out=R[:, oc, cs], in0=R[:, oc, cs], scalar=DN_ALPHA, in1=t1[1][:, 0:n],
                                                                          op0=ALU.mult, op1=ALU.add),
                                  reads=["R", "t1_1"], writes=["R"])
                    for si, (c0, n) in enumerate(SUBS):
                        layer_norm(V_L2G, V_L2B, c0, n, False)
                        lo = 2 if (tile == 0 and si == 0) else 0
                        g0 = C0 + c0 + lo - 2
                        sc.dma("sp", outv[:, :, g0:g0 + n - lo], R[:, :, c0 + lo:c0 + n], "out", reads=["R"], is_output=True)
                sc.finish()
    return nc


def _panels(W):
    K_, M_ = W.shape
    return np.ascontiguousarray(W.reshape(K_ // 128, 128, M_ // 128, 128).transpose(2, 1, 0, 3))


def _cols(v):
    return np.ascontiguousarray(v.reshape(-1, 128).T)


_PROGS = {}


def kernel(x, p, lb_logits, w_in, hg_norm_g, w_a, w_b, w_out, ln1_g, ln1_b,
           w_up, conv_w, conv_b, w_down, w_pe, w_pg, ln2_g, ln2_b):
    f32 = np.float32
    x = np.asarray(x, f32)
    p = np.asarray(p, f32)
    S = SEQ
    h = x[0]
    cM = consts_M()
    cores = list(range(NCORES))
    for l in range(DEPTH):
        hT = np.ascontiguousarray(h.T)
        if ("M", l) not in _PROGS:
            _PROGS[("M", l)] = build_M(S, l)
        maps = []
        for c in cores:
            hh = c // 2
            maps.append({"xT": hT, "wM": prep_M_weights(np.asarray(w_in[l], f32), c),
                         "lbl": np.ascontiguousarray(np.asarray(lb_logits, f32)[:, hh * 128:(hh + 1) * 128].T), "cst": cM})
        res = run_bass_kernel_spmd(_PROGS[("M", l)], maps, core_ids=cores)
        obT = np.concatenate([np.asarray(res.results[c]["ob"]).T for c in cores], axis=0)
        oaT = np.concatenate([np.asarray(res.results[c]["oaT"]) for c in cores], axis=0)
        def pad2(a):
            return np.concatenate([np.zeros((a.shape[0], 2), f32), a], axis=1)
        hTp, obTp, oaTp, pTp = pad2(hT), pad2(obT), pad2(oaT), pad2(np.ascontiguousarray(p[l, 0].T))
        wl = np.asarray(w_in[l], f32)
        Wg = _panels(np.concatenate([wl[:, 1536:2048], wl[:, 3584:4608], wl[:, 4608:5632]], axis=1))
        Wa, Wb, Wo = _panels(np.asarray(w_a[l], f32)), _panels(np.asarray(w_b[l], f32)), _panels(np.asarray(w_out[l], f32))
        Wu, Wd = _panels(np.asarray(w_up[l], f32)), _panels(np.asarray(w_down[l], f32))
        Wpe, Wpg = _panels(np.asarray(w_pe[l], f32)), _panels(np.asarray(w_pg[l], f32))
        cw = np.asarray(conv_w[l], f32)
        vec = np.zeros((128, NV_T), f32)
        vec[:, V_HG:V_HG + 4] = _cols(np.asarray(hg_norm_g[l], f32))
        vec[:, V_L1G:V_L1G + 8] = _cols(np.asarray(ln1_g[l], f32))
        vec[:, V_L1B:V_L1B + 8] = _cols(np.asarray(ln1_b[l], f32))
        vec[:, V_CW0:V_CW0 + 44] = _cols(cw[0])
        vec[:, V_CW1:V_CW1 + 44] = _cols(cw[1])
        vec[:, V_CW2:V_CW2 + 44] = _cols(cw[2])
        vec[:, V_CB:V_CB + 44] = _cols(np.asarray(conv_b[l], f32))
        vec[:, V_L2G:V_L2G + 8] = _cols(np.asarray(ln2_g[l], f32))
        vec[:, V_L2B:V_L2B + 8] = _cols(np.asarray(ln2_b[l], f32))
        if "T" not in _PROGS:
            _PROGS["T"] = build_T()
        maps = []
        for c in cores:
            sl = slice(c * 2048, c * 2048 + NTOK_T)
            v = vec.copy()
            v[:, V_HM] = 0.0 if c == 0 else 1.0
            maps.append({"xTc": np.ascontiguousarray(hTp[:, sl]), "obTc": np.ascontiguousarray(obTp[:, sl]),
                         "oaTc": np.ascontiguousarray(oaTp[:, sl]), "pTc": np.ascontiguousarray(pTp[:, sl]), "vec": v,
                         "Wg": Wg, "Wa": Wa, "Wb": Wb, "Wo": Wo, "Wu": Wu, "Wd": Wd, "Wpe": Wpe, "Wpg": Wpg})
        res = run_bass_kernel_spmd(_PROGS["T"], maps, core_ids=cores)
        outT = np.concatenate([np.asarray(res.results[c]["outT"]) for c in cores], axis=1)
        h = np.ascontiguousarray(outT.T)
    return h[None].astype(f32)
```

```python
import contextlib
import types
import numpy as np
import concourse.bass as bass
import concourse.mybir as mybir
from concourse.bass_utils import run_bass_kernel_spmd

F32 = mybir.dt.float32
BF16 = mybir.dt.bfloat16
AF = mybir.ActivationFunctionType
ALU = mybir.AluOpType

D_MODEL = 1024
SEQ = 16384
DEPTH = 2
NCORES = 8
D_FF = 2816
DN_ALPHA = (2 * DEPTH) ** 0.25
LN_EPS = 1e-5
RMS_EPS = 1e-6
F_FLOOR = 1e-30


class Sched:
    WIN = 512

    def __init__(self, nc, es):
        self.nc = nc
        self.es = es
        self.engs = {"pe": nc.tensor, "act": nc.scalar, "dve": nc.vector, "pool": nc.gpsimd, "sp": nc.sync}
        self.ops = []
        self.seq = {}
        self.last_w = {}
        self.readers = {}
        self.out_ids = []

    def _deps(self, reads, writes):
        d = []
        for b in reads:
            if b in self.last_w:
                d.append(self.last_w[b])
        for b in writes:
            if b in self.last_w:
                d.append(self.last_w[b])
            d.extend(self.readers.get(b, []))
        return d

    def _commit(self, i, reads, writes):
        for b in reads:
            self.readers.setdefault(b, []).append(i)
        for b in writes:
            self.last_w[b] = i
            self.readers[b] = []

    def _add(self, kind, e, payload, stream, reads, writes):
        deps = self._deps(reads, writes)
        sq = self.seq.get(stream, 0)
        self.seq[stream] = sq + 1
        i = len(self.ops)
        self.ops.append((kind, e, payload, deps, stream, sq))
        self._commit(i, reads, writes)
        return i

    @staticmethod
    def _freeze(fn):
        if fn.__closure__ is None:
            return fn
        cells = []
        for c in fn.__closure__:
            try:
                cells.append(types.CellType(c.cell_contents))
            except ValueError:
                cells.append(c)
        return types.FunctionType(fn.__code__, fn.__globals__, fn.__name__, fn.__defaults__, tuple(cells))

    def op(self, e, fn, reads=(), writes=()):
        return self._add("op", e, self._freeze(fn), e, reads, writes)

    def dma(self, q, out, in_, key, reads=(), writes=(), is_output=False):
        i = self._add("dma", q, (out, in_), "dma_" + key, reads, writes)
        if is_output:
            self.out_ids.append(i)
        return i

    def finish(self):
        ops = self.ops
        n = len(ops)
        seen = {e: {} for e in self.engs}
        waits = [[] for _ in range(n)]
        needs = [False] * n
        for i, (kind, e, payload, deps, stream, sq) in enumerate(ops):
            best = {}
            for d in deps:
                ds, dq = ops[d][4], ops[d][5]
                if e == "pe" and ds == "pe":
                    continue
                if seen[e].get(ds, -1) >= dq:
                    continue
                if ds not in best or best[ds][1] < dq:
                    best[ds] = (d, dq)
            for ds, (d, dq) in best.items():
                waits[i].append(d)
                seen[e][ds] = dq
                needs[d] = True
        final_waits = []
        lastout = {}
        for i in self.out_ids:
            lastout[ops[i][4]] = i
        for i in ops and self.out_ids:
            needs[i] = True
        cnt = {}
        sems = {}
        tok = [None] * n
        for i, (kind, e, payload, deps, stream, sq) in enumerate(ops):
            if kind == "dma" or needs[i]:
                step = 16 if kind == "dma" else 1
                c = cnt.get(stream, 0) + step
                cnt[stream] = c
                w = (c - 1) // self.WIN
                if (stream, w) not in sems:
                    sems[(stream, w)] = self.es.enter_context(self.nc.semaphore(f"s_{stream}_{w}"))
                tok[i] = (sems[(stream, w)], c - w * self.WIN, step)
        for i, (kind, e, payload, deps, stream, sq) in enumerate(ops):
            eng = self.engs[e]
            for d in waits[i]:
                eng.wait_ge(tok[d][0], tok[d][1])
            if kind == "dma":
                ins = eng.dma_start(out=payload[0], in_=payload[1])
            else:
                ins = payload(eng)
            if tok[i] is not None:
                ins.then_inc(tok[i][0], tok[i][2])
        done = set()
        for i in reversed(self.out_ids):
            key = id(tok[i][0])
            if key in done:
                continue
            done.add(key)
            self.engs["sp"].wait_ge(tok[i][0], tok[i][1])


C_ID, C_J, C_A, C_A2, C_SH, C_SH2, C_NEG, C_MD, C_MH, C_RST = 0, 128, 256, 384, 512, 640, 768, 896, 1024, 1088
CW_M = 1600


def consts_M():
    c = np.zeros((128, CW_M), np.float32)
    i = np.arange(128)
    c[i, C_ID + i] = 1.0
    c[i, C_J + (127 - i)] = 1.0
    for j in range(128):
        if 126 - j >= 0:
            c[126 - j, C_A + j] += 1.0
        c[127 - j, C_A + j] -= 1.0
    c[127, C_A2 + 127] = 1.0
    for t in range(1, 128):
        c[t - 1, C_SH + t] = 1.0
    c[127, C_SH2 + 0] = 1.0
    ii, jj = np.meshgrid(i, i, indexing="ij")
    c[:, C_NEG:C_NEG + 128] = np.where(ii + jj <= 127, -30000.0, 0.0)
    c[:, C_MD:C_MD + 128] = np.where(ii + jj >= 128, 1.0, 0.0)
    s64 = np.arange(64)
    ss, tt = np.meshgrid(s64, s64, indexing="ij")
    c[0:64, C_MH:C_MH + 64] = np.where(ss <= tt, 1.0, 0.0)
    r = np.ones(512, np.float32)
    r[0::64] = 0.0
    c[:, C_RST:C_RST + 512] = r[None, :]
    return c


def build_M(S, layer, upto=9):
    NQB = S // 128
    NTT = S // 512
    nc = bass.Bass("TRN2", target_bir_lowering=False)
    xT = nc.dram_tensor("xT", [D_MODEL, S], F32, kind="ExternalInput").ap()
    wM = nc.dram_tensor("wM", [128, 8, 512], F32, kind="ExternalInput").ap()
    lbl = nc.dram_tensor("lbl", [128, DEPTH], F32, kind="ExternalInput").ap()
    cst = nc.dram_tensor("cst", [128, CW_M], F32, kind="ExternalInput").ap()
    ob_o = nc.dram_tensor("ob", [S, 64], F32, kind="ExternalOutput").ap()
    oaT_o = nc.dram_tensor("oaT", [64, S], F32, kind="ExternalOutput").ap()
    xT_v = xT.rearrange("(kc p) t -> p kc t", p=128)

    with contextlib.ExitStack() as es:
        def sb(name, shape, dt):
            return es.enter_context(nc.sbuf_tensor(name, shape, dt))

        def ps(name, shape, dt):
            return es.enter_context(nc.psum_tensor(name, shape, dt))

        cf = sb("cf", [128, CW_M], F32)
        cb = sb("cb", [128, CW_M], BF16)
        wb = sb("wb", [128, 8, 512], BF16)
        QT = sb("QT", [128, S], BF16)
        KTr = sb("KTr", [128, S], BF16)
        DVr = sb("DVr", [128, NQB, 64], BF16)
        Vf = sb("Vf", [128, NQB, 64], F32)
        xb = [sb(f"xb{i}", [128, 8, 512], BF16) for i in range(2)]
        Kb = sb("Kb", [128, 4, 64], BF16)
        vhb = sb("vhb", [64, 8, 64], BF16)
        zeros = sb("zeros", [128, 1024], F32)
        lbt = sb("lbt", [128, 16], F32)
        bb = sb("bb", [128, 512], F32)
        e1 = sb("e1", [128, 512], F32)
        e2 = sb("e2", [128, 512], F32)
        qp = sb("qp", [128, 512], BF16)
        kp = sb("kp", [128, 512], BF16)
        sc8 = sb("sc8", [128, 64], F32)
        state = sb("state", [128, 64], F32)
        srb = [sb(f"srb{i}", [128, 64], BF16) for i in range(2)]
        scm = [sb(f"scm{i}", [64, 64], BF16) for i in range(8)]
        kTs = [sb(f"kTs{i}", [64, 128], BF16) for i in range(8)]
        utmp = sb("utmp", [128, 64], F32)
        oas = [sb(f"oas{i}", [64, 512], F32) for i in range(2)]
        omb = [sb(f"omb{i}", [128, 1024], F32) for i in range(2)]
        Eb = [sb(f"Eb{i}", [128, 1024], BF16) for i in range(4)]
        ETs = [sb(f"ETs{i}", [128, 1024], BF16) for i in range(2)]
        qs, sg, ff, kk = omb[0][:, 0:512], omb[0][:, 512:1024], omb[1][:, 0:512], omb[1][:, 512:1024]
        gg = ff
        obs = [sb(f"obs{i}", [128, 64], F32) for i in range(2)]
        Zp = [ps(f"Zp{i}", [128, 1024], F32) for i in range(2)]
        Ap = [ps(f"Ap{i}", [128, 512], F32) for i in range(2)]
        Fp = [Zp[0][:, 0:512], Zp[0][:, 512:1024], Zp[1][:, 0:512], Ap[0][:, :], Ap[1][:, :], Zp[1][:, 512:1024]]
        ET32 = ps("ET32", [128, 1024], F32)

        identb = cb[:, C_ID:C_ID + 128]
        Jb = cb[:, C_J:C_J + 128]
        NEGb = cb[:, C_NEG:C_NEG + 128]
        MDb = cb[:, C_MD:C_MD + 128]
        Af = cf[:, C_A:C_A + 128]
        A2f = cf[:, C_A2:C_A2 + 128]
        SHf = cf[:, C_SH:C_SH + 128]
        SH2f = cf[:, C_SH2:C_SH2 + 128]
        MHf = cf[0:64, C_MH:C_MH + 64]
        RSTf = cf[:, C_RST:C_RST + 512]

        with nc.Block() as block:
            @block.sync
            def _(_e):
                sc = Sched(nc, es)
                sc.dma("sp", cf[:], cst[:], "cf", writes=["cf"])
                sc.dma("pool", cb[:], cst[:], "cb", writes=["cb"])
                sc.dma("pool", wb[:], wM[:], "wb", writes=["wb"])
                sc.dma("sp", lbt[:, 0:DEPTH], lbl[:], "lbt", writes=["lbt"])
                sc.op("dve", lambda e: e.memset(zeros[:], 0.0), writes=["zeros"])
                sc.op("dve", lambda e: e.memset(state[:], 0.0), writes=["state"])
                sc.op("pool", lambda e: e.memset(QT[64:128, :], 0.0), writes=["QTpad"])
                sc.op("pool", lambda e: e.memset(KTr[64:128, :], 0.0), writes=["KTpad"])
                LB, OM, NOM = lbt[:, 8:9], lbt[:, 9:10], lbt[:, 10:11]
                mx = lbt[:, 4:5]
                sc.op("dve", lambda e: e.tensor_reduce(out=mx, in_=lbt[:, 0:DEPTH], axis=mybir.AxisListType.X, op=ALU.max),
                      reads=["lbt"], writes=["lb_mx"])
                sc.op("dve", lambda e: e.tensor_scalar(out=lbt[:, 5:6], in0=mx, scalar1=-1.0, scalar2=None, op0=ALU.mult),
                      reads=["lb_mx"], writes=["lb_nmx"])
                sc.op("act", lambda e: e.activation(out=lbt[:, 2:2 + DEPTH], in_=lbt[:, 0:DEPTH], func=AF.Exp,
                                                    bias=lbt[:, 5:6], scale=1.0),
                      reads=["lbt", "lb_nmx"], writes=["lb_e"])
                sc.op("dve", lambda e: e.tensor_reduce(out=lbt[:, 6:7], in_=lbt[:, 2:2 + DEPTH], axis=mybir.AxisListType.X, op=ALU.add),
                      reads=["lb_e"], writes=["lb_s"])
                sc.op("dve", lambda e: e.reciprocal(out=lbt[:, 7:8], in_=lbt[:, 6:7]), reads=["lb_s"], writes=["lb_r"])
                if layer == 0:
                    sc.op("dve", lambda e: e.memset(LB, 0.0), reads=["lb_r"], writes=["lb"])
                else:
                    sc.op("dve", lambda e: e.tensor_reduce(out=lbt[:, 11:12], in_=lbt[:, 3:2 + layer + 1],
                                                           axis=mybir.AxisListType.X, op=ALU.add),
                          reads=["lb_e"], writes=["lb_p"])
                    sc.op("dve", lambda e: e.tensor_tensor(out=LB, in0=lbt[:, 11:12], in1=lbt[:, 7:8], op=ALU.mult),
                          reads=["lb_p", "lb_r"], writes=["lb"])
                sc.op("dve", lambda e: e.tensor_scalar(out=OM, in0=LB, scalar1=-1.0, scalar2=1.0, op0=ALU.mult, op1=ALU.add),
                      reads=["lb"], writes=["om"])
                sc.op("dve", lambda e: e.tensor_scalar(out=NOM, in0=OM, scalar1=-1.0, scalar2=None, op0=ALU.mult),
                      reads=["om"], writes=["nom"])

                for tt in range(NTT if upto >= 1 else 0):
                    X = xb[tt % 2]
                    xk = f"xb{tt % 2}"
                    sc.dma("pool", X[:], xT_v[:, :, tt * 512:(tt + 1) * 512], xk, writes=[xk])
                    for oi, (bank, c0) in enumerate([(0, 0), (1, 128), (2, 256)]):
                        for kc in range(8):
                            sc.op("pe", lambda e, bank=bank, c0=c0, kc=kc: e.matmul(
                                Fp[bank][:, :], lhsT=wb[:, kc, c0:c0 + 128], rhs=X[:, kc, :], start=(kc == 0), stop=(kc == 7)),
                                reads=[xk, "wb"], writes=[f"F{bank}"])
                    if upto < 1.05:
                        sc.op("act", lambda e: e.activation(out=QT[0:64, tt * 512:(tt + 1) * 512], in_=Fp[2][0:64, :], func=AF.Copy),
                              reads=["F2"], writes=[f"QT{tt}"])
                        continue
                    for blk in range(4):
                        for kc in range(8):
                            sc.op("pe", lambda e, blk=blk, kc=kc: e.matmul(
                                Fp[3][:, blk * 128:(blk + 1) * 128], lhsT=X[:, kc, blk * 128:(blk + 1) * 128],
                                rhs=wb[:, kc, 320:448], start=(kc == 0), stop=(kc == 7)),
                                reads=[xk, "wb"], writes=["F3"])
                    for c in range(8 if upto >= 1.2 else 0):
                        for kc in range(8):
                            sc.op("pe", lambda e, c=c, kc=kc: e.matmul(
                                Fp[4][0:64, c * 64:(c + 1) * 64], lhsT=X[:, kc, c * 64:(c + 1) * 64],
                                rhs=wb[:, kc, 448:512], start=(kc == 0), stop=(kc == 7)),
                                reads=[xk, "wb"], writes=["F4"])
                    sc.op("act", lambda e: e.activation(out=QT[0:64, tt * 512:(tt + 1) * 512], in_=Fp[2][0:64, :], func=AF.Copy),
                          reads=["F2"], writes=[f"QT{tt}"])
                    F3v = Fp[3][:, :].rearrange("p (b c) -> p b c", b=4)
                    if upto >= 1.12:
                        sc.op("dve", lambda e: e.tensor_copy(out=Kb[:, :, :], in_=F3v[:, :, 0:64]), reads=["F3"], writes=["Kb"])
                    if upto >= 1.14:
                        sc.op("dve", lambda e: e.tensor_copy(out=Vf[:, tt * 4:(tt + 1) * 4, :], in_=F3v[:, :, 64:128]),
                              reads=["F3"], writes=[f"Vf{tt}"])
                    if upto >= 1.2:
                        sc.op("dve", lambda e: e.tensor_copy(out=vhb[:, :, :], in_=Fp[4][0:64, :].rearrange("p (c v) -> p c v", c=8)),
                              reads=["F4"], writes=["vhb"])
                    if upto < 1.3:
                        continue
                    for blk in range(4):
                        sc.op("pe", lambda e, blk=blk: e.matmul(
                            Fp[2][0:64, (3 - blk) * 128:(4 - blk) * 128], lhsT=Kb[:, blk, :], rhs=Jb, start=True, stop=True),
                            reads=["Kb", "cb"], writes=["F2"])
                    rb0 = NQB - 1 - (tt * 4 + 3)
                    sc.op("act", lambda e: e.activation(out=KTr[0:64, rb0 * 128:(rb0 + 4) * 128], in_=Fp[2][0:64, :], func=AF.Copy),
                          reads=["F2"], writes=[f"KTr{tt}"])
                    if upto < 1.5:
                        continue
                    for blk in range(4):
                        bs = tt * 4 + blk
                        sc.op("pe", lambda e, blk=blk, bs=bs: e.matmul(
                            Fp[3][:, (3 - blk) * 64:(4 - blk) * 64], lhsT=Af, rhs=Vf[:, bs, :], start=True, stop=(bs == 0)),
                            reads=[f"Vf{tt}", "cf"], writes=["F3"])
                        if bs > 0:
                            sc.op("pe", lambda e, blk=blk, bs=bs: e.matmul(
                                Fp[3][:, (3 - blk) * 64:(4 - blk) * 64], lhsT=A2f, rhs=Vf[:, bs - 1, :], start=False, stop=True),
                                reads=[f"Vf{(bs - 1) // 4}", "cf"], writes=["F3"])
                    sc.op("dve", lambda e: e.tensor_copy(out=DVr[:, rb0:rb0 + 4, :],
                                                         in_=Fp[3][:, 0:256].rearrange("p (b c) -> p b c", b=4)),
                          reads=["F3"], writes=[f"DVr{tt}"])

                    if upto < 2:
                        continue
                    sc.op("act", lambda e: e.activation(out=qs[:], in_=Fp[0][:, :], func=AF.Sigmoid), reads=["F0"], writes=["qs"])
                    sc.op("act", lambda e: e.activation(out=sg[:], in_=Fp[1][:, :], func=AF.Sigmoid), reads=["F1"], writes=["sg"])
                    sc.op("dve", lambda e: e.tensor_tensor(out=qs[:], in0=qs[:], in1=Fp[0][:, :], op=ALU.mult), reads=["qs", "F0"], writes=["qs"])
                    sc.op("dve", lambda e: e.tensor_scalar(out=ff[:], in0=sg[:], scalar1=OM, scalar2=LB, op0=ALU.mult, op1=ALU.add),
                          reads=["sg", "om", "lb"], writes=["ff"])
                    sc.op("dve", lambda e: e.tensor_scalar(out=ff[:], in0=ff[:], scalar1=F_FLOOR, scalar2=None, op0=ALU.max),
                          reads=["ff"], writes=["ff"])
                    sc.op("act", lambda e: e.activation(out=ff[:], in_=ff[:], func=AF.Ln), reads=["ff"], writes=["ff"])
                    sc.op("dve", lambda e: e.tensor_scalar(out=kk[:], in0=sg[:], scalar1=NOM, scalar2=OM, op0=ALU.mult, op1=ALU.add),
                          reads=["sg", "nom", "om"], writes=["kk"])
                    sc.op("dve", lambda e: e.tensor_tensor_scan(out=bb[:], data0=RSTf, data1=gg[:], initial=0.0,
                                                                op0=ALU.mult, op1=ALU.add),
                          reads=["ff", "cf"], writes=["bb"])
                    bbv = bb[:].rearrange("p (c t) -> p c t", c=8)
                    BM, D8, BL, NBM = sc8[:, 0:8], sc8[:, 8:16], sc8[:, 16:24], sc8[:, 24:32]
                    sc.op("dve", lambda e: e.tensor_copy(out=BM, in_=bbv[:, :, 31]), reads=["bb"], writes=["bm"])
                    sc.op("dve", lambda e: e.tensor_copy(out=BL, in_=bbv[:, :, 63]), reads=["bb"], writes=["bl"])
                    sc.op("dve", lambda e: e.tensor_scalar(out=NBM, in0=BM, scalar1=-1.0, scalar2=None, op0=ALU.mult),
                          reads=["bm"], writes=["nbm"])
                    sc.op("dve", lambda e: e.tensor_tensor(out=D8, in0=BL, in1=BM, op=ALU.subtract),
                          reads=["bl", "bm"], writes=["d8"])
                    sc.op("act", lambda e: e.activation(out=sc8[:, 32:56], in_=sc8[:, 0:24], func=AF.Exp),
                          reads=["bm", "d8", "bl"], writes=["er", "c2", "el"])
                    for c in range(8):
                        sl = slice(c * 64, (c + 1) * 64)
                        sc.op("act", lambda e, c=c, sl=sl: e.activation(out=e1[:, sl], in_=bb[:, sl], func=AF.Exp,
                                                                        bias=sc8[:, 24 + c:25 + c], scale=1.0),
                              reads=["bb", "nbm"], writes=["e1"])
                        sc.op("act", lambda e, c=c, sl=sl: e.activation(out=e2[:, sl], in_=bb[:, sl], func=AF.Exp,
                                                                        bias=sc8[:, c:c + 1], scale=-1.0),
                              reads=["bb", "bm"], writes=["e2"])
                    sc.op("dve", lambda e: e.tensor_tensor(out=qp[:], in0=qs[:], in1=e1[:], op=ALU.mult),
                          reads=["qs", "e1"], writes=["qp"])
                    sc.op("dve", lambda e: e.tensor_tensor(out=kp[:], in0=kk[:], in1=e2[:], op=ALU.mult),
                          reads=["kk", "e2"], writes=["kp"])
                    for c in range(8):
                        sl = slice(c * 64, (c + 1) * 64)
                        sc.op("pe", lambda e: e.matmul(Fp[3][0:64, sl], lhsT=kp[:, sl], rhs=qp[:, sl], start=True, stop=True),
                              reads=["kp", "qp"], writes=["F3"])
                    for c in range(8):
                        sl = slice(c * 64, (c + 1) * 64)
                        sc.op("pe", lambda e: e.matmul(ET32[0:64, c * 128:(c + 1) * 128], lhsT=kp[:, sl], rhs=identb, start=True, stop=True),
                              reads=["kp", "cb"], writes=["B0"])
                    for c in range(8):
                        sl = slice(c * 64, (c + 1) * 64)
                        sc.op("dve", lambda e: e.tensor_tensor(out=scm[c][:], in0=Fp[3][0:64, sl], in1=MHf, op=ALU.mult),
                              reads=["F3", "cf"], writes=[f"scm{c}"])
                        sc.op("act", lambda e: e.activation(out=kTs[c][:], in_=ET32[0:64, c * 128:(c + 1) * 128], func=AF.Copy),
                              reads=["B0"], writes=[f"kTs{c}"])
                    for c in range(8):
                        sc.op("pe", lambda e: e.matmul(Fp[5][:, c * 64:(c + 1) * 64], lhsT=kTs[c][:], rhs=vhb[:, c, :], start=True, stop=True),
                              reads=[f"kTs{c}", "vhb"], writes=["F5"])
                    for c in range(8):
                        sl = slice(c * 64, (c + 1) * 64)
                        s2 = (tt * 8 + c) % 2
                        sc.op("dve", lambda e: e.tensor_scalar(out=srb[s2][:], in0=state[:], scalar1=sc8[:, 32 + c:33 + c],
                                                               scalar2=None, op0=ALU.mult),
                              reads=["state", "er"], writes=[f"srb{s2}"])
                        sc.op("pe", lambda e: e.matmul(Fp[4][0:64, sl], lhsT=srb[s2][:], rhs=qp[:, sl], start=True, stop=False),
                              reads=[f"srb{s2}", "qp"], writes=["F4"])
                        sc.op("pe", lambda e: e.matmul(Fp[4][0:64, sl], lhsT=vhb[:, c, :], rhs=scm[c][:], start=False, stop=True),
                              reads=[f"scm{c}", "vhb"], writes=["F4"])
                        sc.op("dve", lambda e: e.tensor_scalar(out=utmp[:], in0=Fp[5][:, c * 64:(c + 1) * 64], scalar1=sc8[:, 40 + c:41 + c],
                                                               scalar2=None, op0=ALU.mult),
                              reads=["F5", "c2"], writes=["utmp"])
                        sc.op("dve", lambda e: e.scalar_tensor_tensor(out=state[:], in0=state[:], scalar=sc8[:, 48 + c:49 + c],
                                                                      in1=utmp[:], op0=ALU.mult, op1=ALU.add),
                              reads=["state", "utmp", "el"], writes=["state"])
                    ok = f"oas{tt % 2}"
                    sc.op("act", lambda e: e.activation(out=oas[tt % 2][:], in_=Fp[4][0:64, :], func=AF.Copy), reads=["F4"], writes=[ok])
                    sc.dma("sp", oaT_o[:, tt * 512:(tt + 1) * 512], oas[tt % 2][:], f"oaT_out{tt % 2}", reads=[ok], is_output=True)

                TWK = 1024
                def qb_tiles(qb):
                    rstart = (NQB - 1 - qb) * 128
                    L = S - rstart
                    nt = (L + TWK - 1) // TWK
                    return [(qb, ti, rstart + ti * TWK, min(TWK, S - rstart - ti * TWK), ti == nt - 1) for ti in range(nt)]

                tiles = []
                prev_of = []
                for q0 in range(0, NQB, 2):
                    la, lb_ = qb_tiles(q0), (qb_tiles(q0 + 1) if q0 + 1 < NQB else [])
                    last_idx = {}
                    for i in range(max(len(la), len(lb_))):
                        for lst in (la, lb_):
                            if i < len(lst):
                                t = lst[i]
                                prev_of.append(last_idx.get(t[0], -1))
                                last_idx[t[0]] = len(tiles)
                                tiles.append(t)
                if upto < 3:
                    tiles = []
                NT = len(tiles)
                qt_keys = [f"QT{t}" for t in range(NTT)] + ["QTpad"]
                kt_keys = [f"KTr{t}" for t in range(NTT)] + ["KTpad"]
                dv_keys = [f"DVr{t}" for t in range(NTT)]
                vf_keys = [f"Vf{t}" for t in range(NTT)]
                zkeys = [["F0", "F1"], ["F2", "F5"]]
                hg_alias = [["qs", "sg"], ["ff", "kk"]]

                def stA(n):
                    qb, ti, r0, w, last = tiles[n]
                    z = Zp[n % 2]
                    for h0 in range(0, w, 512):
                        wh = min(512, w - h0)
                        zk = zkeys[n % 2][h0 // 512]
                        diag = (ti == 0 and h0 == 0)
                        sc.op("pe", lambda e: e.matmul(z[:, h0:h0 + wh], lhsT=QT[:, qb * 128:(qb + 1) * 128], rhs=KTr[:, r0 + h0:r0 + h0 + wh],
                                                       start=True, stop=(not diag)),
                              reads=qt_keys + kt_keys, writes=[zk])
                        if diag:
                            sc.op("pe", lambda e: e.matmul(z[:, 0:128], lhsT=identb, rhs=NEGb, start=False, stop=True),
                                  reads=["cb"], writes=[zk])

                def stB(n):
                    qb, ti, r0, w, last = tiles[n]
                    sc.op("act", lambda e: e.activation(out=omb[n % 2][:, 0:w], in_=Zp[n % 2][:, 0:w], func=AF.Sigmoid, scale=-0.125),
                          reads=zkeys[n % 2], writes=[f"omb{n % 2}"] + hg_alias[n % 2])

                def stC(n):
                    qb, ti, r0, w, last = tiles[n]
                    if ti == 0:
                        init = 1.0
                        rd = []
                    else:
                        pn = prev_of[n]
                        wp = tiles[pn][3]
                        init = Eb[pn % 4][:, wp - 1:wp]
                        rd = [f"Eb{pn % 4}"]
                    sc.op("dve", lambda e: e.tensor_tensor_scan(out=Eb[n % 4][:, 0:w], data0=omb[n % 2][:, 0:w], data1=zeros[:, 0:w],
                                                                initial=init, op0=ALU.mult, op1=ALU.add),
                          reads=[f"omb{n % 2}", "zeros"] + rd, writes=[f"Eb{n % 4}"])

                def stD(n):
                    qb, ti, r0, w, last = tiles[n]
                    for sbk in range(w // 128):
                        sl = slice(sbk * 128, (sbk + 1) * 128)
                        sc.op("pe", lambda e: e.matmul(ET32[:, sl], lhsT=Eb[n % 4][:, sl], rhs=identb, start=True, stop=True),
                              reads=[f"Eb{n % 4}", "cb"], writes=["B0"])

                def stF(n):
                    qb, ti, r0, w, last = tiles[n]
                    sc.op("act", lambda e: e.activation(out=ETs[n % 2][:, 0:w], in_=ET32[:, 0:w], func=AF.Copy),
                          reads=["B0"], writes=[f"ETs{n % 2}"])
                    if ti == 0:
                        sc.op("dve", lambda e: e.tensor_tensor(out=ETs[n % 2][:, 0:128], in0=ETs[n % 2][:, 0:128], in1=MDb, op=ALU.mult),
                              reads=[f"ETs{n % 2}", "cb"], writes=[f"ETs{n % 2}"])

                def stG(n):
                    qb, ti, r0, w, last = tiles[n]
                    acc = Ap[qb % 2]
                    ak = f"F{3 + qb % 2}"
                    for sbk in range(w // 128):
                        sl = slice(sbk * 128, (sbk + 1) * 128)
                        br = r0 // 128 + sbk
                        sc.op("pe", lambda e: e.matmul(acc[:, 0:64], lhsT=ETs[n % 2][:, sl], rhs=DVr[:, br, :],
                                                       start=(ti == 0 and sbk == 0), stop=False),
                              reads=[f"ETs{n % 2}"] + dv_keys, writes=[ak])
                    if last:
                        sc.op("pe", lambda e: e.matmul(acc[:, 0:64], lhsT=SHf, rhs=Vf[:, qb, :], start=False, stop=(qb == 0)),
                              reads=["cf"] + vf_keys, writes=[ak])
                        if qb > 0:
                            sc.op("pe", lambda e: e.matmul(acc[:, 0:64], lhsT=SH2f, rhs=Vf[:, qb - 1, :], start=False, stop=True),
                                  reads=["cf"] + vf_keys, writes=[ak])
                        sc.op("dve", lambda e: e.tensor_copy(out=obs[qb % 2][:], in_=acc[:, 0:64]), reads=[ak], writes=[f"obs{qb % 2}"])
                        sc.dma("sp", ob_o[qb * 128:(qb + 1) * 128, :], obs[qb % 2][:], f"ob_out{qb % 2}", reads=[f"obs{qb % 2}"], is_output=True)

                for n in range(-2, NT + 1):
                    if 0 <= n + 2 < NT:
                        stA(n + 2)
                    if 0 <= n < NT and upto >= 3.3:
                        stC(n)
                    if 0 <= n + 2 < NT and upto >= 3.2:
                        stB(n + 2)
                    if 0 <= n < NT:
                        if upto >= 3.4:
                            stD(n)
                        if upto >= 3.5:
                            stF(n)
                    if 0 <= n - 1 < NT and upto >= 3.6:
                        stG(n - 1)
                sc.finish()
    return nc


def prep_M_weights(w_in_l, core):
    hh, vh = core // 2, core % 2
    cols = np.concatenate([
        np.arange(hh * 128, (hh + 1) * 128),
        512 + np.arange(hh * 128, (hh + 1) * 128),
        2048 + np.arange(core * 64, (core + 1) * 64),
        2560 + np.arange(core * 64, (core + 1) * 64),
        3072 + np.arange(core * 64, (core + 1) * 64),
        1024 + hh * 128 + vh * 64 + np.arange(64),
    ])
    w = w_in_l[:, cols]
    return np.ascontiguousarray(w.reshape(8, 128, 512).transpose(1, 0, 2))


NTOK_T = 2050
TW = 410
V_HG, V_L1G, V_L1B, V_CW0, V_CW1, V_CW2, V_CB, V_L2G, V_L2B, V_HM, NV_T = 0, 4, 12, 20, 64, 108, 152, 196, 204, 212, 213


def build_T():
    BW = 1025
    SUBS = [(0, 410), (410, 410), (820, 205)]
    NMAX = 410
    nc = bass.Bass("TRN2", target_bir_lowering=False)
    xTd = nc.dram_tensor("xTc", [D_MODEL, NTOK_T], F32, kind="ExternalInput").ap()
    obd = nc.dram_tensor("obTc", [512, NTOK_T], F32, kind="ExternalInput").ap()
    oad = nc.dram_tensor("oaTc", [512, NTOK_T], F32, kind="ExternalInput").ap()
    pd = nc.dram_tensor("pTc", [256, NTOK_T], F32, kind="ExternalInput").ap()
    vecd = nc.dram_tensor("vec", [128, NV_T], F32, kind="ExternalInput").ap()
    Wg = nc.dram_tensor("Wg", [20, 128, 8, 128], F32, kind="ExternalInput").ap()
    Wa = nc.dram_tensor("Wa", [8, 128, 4, 128], F32, kind="ExternalInput").ap()
    Wb = nc.dram_tensor("Wb", [8, 128, 4, 128], F32, kind="ExternalInput").ap()
    Wo = nc.dram_tensor("Wo", [8, 128, 8, 128], F32, kind="ExternalInput").ap()
    Wu = nc.dram_tensor("Wu", [44, 128, 8, 128], F32, kind="ExternalInput").ap()
    Wd = nc.dram_tensor("Wd", [8, 128, 22, 128], F32, kind="ExternalInput").ap()
    Wpe = nc.dram_tensor("Wpe", [8, 128, 2, 128], F32, kind="ExternalInput").ap()
    Wpg = nc.dram_tensor("Wpg", [8, 128, 8, 128], F32, kind="ExternalInput").ap()
    outd = nc.dram_tensor("outT", [D_MODEL, 2048], F32, kind="ExternalOutput").ap()
    xv = xTd.rearrange("(kc p) t -> p kc t", p=128)
    obv = obd.rearrange("(kc p) t -> p kc t", p=128)
    oav = oad.rearrange("(kc p) t -> p kc t", p=128)
    pv = pd.rearrange("(kc p) t -> p kc t", p=128)
    outv = outd.rearrange("(kc p) t -> p kc t", p=128)

    with contextlib.ExitStack() as es:
        def sb(name, shape, dt):
            return es.enter_context(nc.sbuf_tensor(name, shape, dt))

        def ps(name, shape, dt):
            return es.enter_context(nc.psum_tensor(name, shape, dt))

        vec = sb("vec_sb", [128, NV_T], F32)
        o1024 = sb("o1024", [128, 128], F32)
        o128 = sb("o128", [128, 128], F32)
        o1024b = sb("o1024b", [128, 128], BF16)
        o128b = sb("o128b", [128, 128], BF16)
        t0b = [sb(f"t0b_{i}", [128, NMAX], BF16) for i in range(2)]
        R = sb("R", [128, 8, BW], F32)
        xbf = sb("xbf", [128, 8, BW], BF16)
        pb = sb("pb", [128, 2, BW], BF16)
        actb = sb("actb", [128, 22, BW], BF16)
        mgb = actb[:, 0:8, :]
        obb = actb[:, 8:12, :]
        oanb = actb[:, 12:16, :]
        oaf = [sb(f"oaf{i}", [128, NMAX], F32) for i in range(2)]
        tail = sb("tail", [128, 44, 2], F32)
        t0 = [sb(f"t0_{i}", [128, NMAX], F32) for i in range(2)]
        t1 = [sb(f"t1_{i}", [128, NMAX], F32) for i in range(2)]
        t2 = [sb(f"t2_{i}", [128, NMAX], F32) for i in range(2)]
        rstd = sb("rstd", [128, NMAX], F32)
        dsq = sb("dsq", [128, NMAX], F32)
        U_ = [sb(f"U_{i}", [128, NMAX + 2], F32) for i in range(4)]
        cv = [sb(f"cv{i}", [128, NMAX], F32) for i in range(4)]
        gl = [sb(f"gl{i}", [128, NMAX], F32) for i in range(2)]
        pan8 = [sb(f"pan8_{i}", [128, 8, 128], BF16) for i in range(6)]
        pan22 = [sb(f"pan22_{i}", [128, 22, 128], BF16) for i in range(2)]
        P = [ps(f"P{i}", [128, 512], F32) for i in range(8)]

        with nc.Block() as block:
            @block.sync
            def _(_e):
                sc = Sched(nc, es)
                st = {"pan8": 0, "pan22": 0, "ps": 0, "oaf": 0}
                sc.dma("sp", vec[:], vecd[:], "vec", writes=["vec"])
                sc.op("dve", lambda e: e.memset(o1024[:], 1.0 / 1024.0), writes=["o1024"])
                sc.op("dve", lambda e: e.memset(o128[:], 1.0 / 128.0), writes=["o128"])
                sc.op("dve", lambda e: e.memset(o1024b[:], 1.0 / 1024.0), writes=["o1024b"])
                sc.op("dve", lambda e: e.memset(o128b[:], 1.0 / 128.0), writes=["o128b"])
                sc.op("dve", lambda e: e.memset(tail[:], 0.0), writes=["tail"])

                def load_panel(W_ap, idx, kc):
                    if kc == 22:
                        i = st["pan22"] % 2
                        st["pan22"] += 1
                        buf, key = pan22[i], f"pan22_{i}"
                        sc.dma("pool", buf[:, :, :], W_ap[idx], key, writes=[key])
                        return buf, key
                    i = st["pan8"] % 6
                    st["pan8"] += 1
                    buf, key = pan8[i], f"pan8_{i}"
                    sc.dma("pool", buf[:, 0:kc, :], W_ap[idx], key, writes=[key])
                    return buf, key

                def mm(panel, kc, n, rhs_fn, rhs_keys):
                    buf, key = panel
                    b = st["ps"] % 8
                    st["ps"] += 1
                    for k in range(kc):
                        r = rhs_fn(k)
                        sc.op("pe", lambda e: e.matmul(P[b][:, 0:n], lhsT=buf[:, k, :], rhs=r, start=(k == 0), stop=(k == kc - 1)),
                              reads=[key] + rhs_keys, writes=[f"P{b}"])
                    return P[b][:, 0:n], f"P{b}"

                def stat(ones_ap, okey, n, rhs_list, rhs_keys):
                    b = st["ps"] % 8
                    st["ps"] += 1
                    m = len(rhs_list)
                    for k, r in enumerate(rhs_list):
                        sc.op("pe", lambda e: e.matmul(P[b][:, 0:n], lhsT=ones_ap, rhs=r, start=(k == 0), stop=(k == m - 1)),
                              reads=[okey] + rhs_keys, writes=[f"P{b}"])
                    return P[b][:, 0:n], f"P{b}"

                def layer_norm(gcol, bcol, c0, n, also_bf):
                    cs = slice(c0, c0 + n)
                    mean, mk = stat(o1024[:], "o1024", n, [R[:, k, cs] for k in range(8)], ["R"])
                    for k in range(8):
                        sc.op("dve", lambda e: e.tensor_tensor(out=R[:, k, cs], in0=R[:, k, cs], in1=mean, op=ALU.subtract),
                              reads=["R", mk], writes=["R"])
                    b = st["ps"] % 8
                    st["ps"] += 1
                    for k in range(8):
                        i = k % 2
                        sc.op("act", lambda e: e.activation(out=t0b[i][:, 0:n], in_=R[:, k, cs], func=AF.Square),
                              reads=["R"], writes=[f"t0b_{i}"])
                        sc.op("pe", lambda e: e.matmul(P[b][:, 0:n], lhsT=o1024b[:], rhs=t0b[i][:, 0:n], start=(k == 0), stop=(k == 7)),
                              reads=["o1024b", f"t0b_{i}"], writes=[f"P{b}"])
                    sc.op("act", lambda e: e.activation(out=dsq[:, 0:n], in_=P[b][:, 0:n], func=AF.Sqrt, bias=LN_EPS, scale=1.0),
                          reads=[f"P{b}"], writes=["dsq"])
                    sc.op("dve", lambda e: e.reciprocal(out=rstd[:, 0:n], in_=dsq[:, 0:n]), reads=["dsq"], writes=["rstd"])
                    sc.op("dve", lambda e: e.tensor_tensor(out=R[:, 0, cs], in0=R[:, 0, cs], in1=rstd[:, 0:n], op=ALU.mult),
                          reads=["R", "rstd"], writes=["R"] + [f"Rk{k}" for k in range(8)])
                    for k in range(8):
                        if k > 0:
                            sc.op("dve", lambda e: e.tensor_tensor(out=R[:, k, cs], in0=R[:, k, cs], in1=rstd[:, 0:n], op=ALU.mult),
                                  reads=["rstd", f"Rk{k}"], writes=[f"Rk{k}"])
                        sc.op("act", lambda e: e.activation(out=R[:, k, cs], in_=R[:, k, cs], func=AF.Identity,
                                                            bias=vec[:, bcol + k:bcol + k + 1], scale=vec[:, gcol + k:gcol + k + 1]),
                              reads=[f"Rk{k}", "vec"], writes=[f"Rk{k}"])
                        if also_bf:
                            sc.op("dve", lambda e: e.tensor_copy(out=xbf[:, k, cs], in_=R[:, k, cs]), reads=[f"Rk{k}"], writes=["xbf"])
                    sc.op("dve", lambda e: e.tensor_copy(out=dsq[:, 0:1], in_=rstd[:, 0:1]),
                          reads=[f"Rk{k}" for k in range(8)] + ["rstd"], writes=["R", "dsq"])

                for tile in range(NTOK_T // BW):
                    C0 = tile * BW
                    CS = slice(C0, C0 + BW)
                    sc.dma("sp", R[:, :, :], xv[:, :, CS], "R", writes=["R"])
                    sc.dma("pool", xbf[:, :, :], xv[:, :, CS], "xbf", writes=["xbf"])
                    sc.dma("pool", obb, obv[:, :, CS], "obb", writes=["obb", "actb"])
                    sc.dma("pool", pb[:, :, :], pv[:, :, CS], "pb", writes=["pb"])
                    for h in range(4):
                        pan = load_panel(Wg, h, 8)
                        for (c0, n) in SUBS:
                            cs = slice(c0, c0 + n)
                            i = st["oaf"] % 2
                            st["oaf"] += 1
                            sc.dma("sp", oaf[i][:, 0:n], oav[:, h, C0 + c0:C0 + c0 + n], f"oaf{i}", writes=[f"oaf{i}"])
                            sc.op("act", lambda e: e.activation(out=t0b[i][:, 0:n], in_=oaf[i][:, 0:n], func=AF.Square),
                                  reads=[f"oaf{i}"], writes=[f"t0b_{i}"])
                            ms, msk = stat(o128b[:], "o128b", n, [t0b[i][:, 0:n]], [f"t0b_{i}"])
                            sc.op("act", lambda e: e.activation(out=dsq[:, 0:n], in_=ms, func=AF.Sqrt, bias=RMS_EPS, scale=1.0),
                                  reads=[msk], writes=["dsq"])
                            sc.op("dve", lambda e: e.reciprocal(out=rstd[:, 0:n], in_=dsq[:, 0:n]), reads=["dsq"], writes=["rstd"])
                            gp, gk = mm(pan, 8, n, lambda k: xbf[:, k, cs], ["xbf"])
                            sc.op("act", lambda e: e.activation(out=t1[i][:, 0:n], in_=gp, func=AF.Silu), reads=[gk], writes=[f"t1_{i}"])
                            sc.op("dve", lambda e: e.scalar_tensor_tensor(out=t2[i][:, 0:n], in0=oaf[i][:, 0:n], scalar=vec[:, V_HG + h:V_HG + h + 1],
                                                                          in1=rstd[:, 0:n], op0=ALU.mult, op1=ALU.mult),
                                  reads=[f"oaf{i}", "rstd", "vec"], writes=[f"t2_{i}"])
                            sc.op("dve", lambda e: e.tensor_tensor(out=oanb[:, h, cs], in0=t2[i][:, 0:n], in1=t1[i][:, 0:n], op=ALU.mult),
                                  reads=[f"t2_{i}", f"t1_{i}"], writes=["oanb", "actb"])
                    for oc in range(8):
                        pga, pgb = load_panel(Wg, 4 + oc, 8), load_panel(Wg, 12 + oc, 8)
                        pya, pyb = load_panel(Wa, oc, 4), load_panel(Wb, oc, 4)
                        for (c0, n) in SUBS:
                            cs = slice(c0, c0 + n)
                            gap, gak = mm(pga, 8, n, lambda k: xbf[:, k, cs], ["xbf"])
                            gbp, gbk = mm(pgb, 8, n, lambda k: xbf[:, k, cs], ["xbf"])
                            yap, yak = mm(pya, 4, n, lambda k: oanb[:, k, cs], ["oanb"])
                            ybp, ybk = mm(pyb, 4, n, lambda k: obb[:, k, cs], ["obb"])
                            sc.op("act", lambda e: e.activation(out=t0[0][:, 0:n], in_=gap, func=AF.Sigmoid), reads=[gak], writes=["t0_0"])
                            sc.op("act", lambda e: e.activation(out=t0[1][:, 0:n], in_=gbp, func=AF.Sigmoid), reads=[gbk], writes=["t0_1"])
                            sc.op("dve", lambda e: e.tensor_tensor(out=t1[0][:, 0:n], in0=t0[0][:, 0:n], in1=yap, op=ALU.mult),
                                  reads=["t0_0", yak], writes=["t1_0"])
                            sc.op("dve", lambda e: e.tensor_tensor(out=t1[1][:, 0:n], in0=t0[1][:, 0:n], in1=ybp, op=ALU.mult),
                                  reads=["t0_1", ybk], writes=["t1_1"])
                            sc.op("dve", lambda e: e.tensor_tensor(out=mgb[:, oc, cs], in0=t1[0][:, 0:n], in1=t1[1][:, 0:n], op=ALU.add),
                                  reads=["t1_0", "t1_1"], writes=["mgb", "actb"])
                    for oc in range(8):
                        pan = load_panel(Wo, oc, 8)
                        for (c0, n) in SUBS:
                            cs = slice(c0, c0 + n)
                            hp, hk = mm(pan, 8, n, lambda k: mgb[:, k, cs], ["mgb"])
                            sc.op("dve", lambda e: e.scalar_tensor_tensor(out=R[:, oc, cs], in0=R[:, oc, cs], scalar=DN_ALPHA, in1=hp,
                                                                          op0=ALU.mult, op1=ALU.add),
                                  reads=["R", hk], writes=["R"])
                    for (c0, n) in SUBS:
                        layer_norm(V_L1G, V_L1B, c0, n, True)
                    for j in range(22):
                        pans = [load_panel(Wu, j, 8), load_panel(Wu, 22 + j, 8)]
                        for si, (c0, n) in enumerate(SUBS):
                            cs = slice(c0, c0 + n)
                            for half in range(2):
                                ch = half * 22 + j
                                i = half + 2 * (si % 2)
                                up, uk = mm(pans[half], 8, n, lambda k: xbf[:, k, cs], ["xbf"])
                                sc.op("act", lambda e: e.activation(out=U_[i][:, 2:2 + n], in_=up, func=AF.Copy),
                                      reads=[uk], writes=[f"U_{i}"])
                                sc.op("dve", lambda e: e.tensor_copy(out=U_[i][:, 0:2], in_=tail[:, ch, :]),
                                      reads=["tail", f"U_{i}"], writes=[f"U_{i}"])
                                if tile == 0 and si == 0:
                                    sc.op("dve", lambda e: e.tensor_scalar(out=U_[i][:, 2:4], in0=U_[i][:, 2:4], scalar1=vec[:, V_HM:V_HM + 1],
                                                                           scalar2=None, op0=ALU.mult),
                                          reads=[f"U_{i}", "vec"], writes=[f"U_{i}"])
                                sc.op("act", lambda e: e.activation(out=cv[i][:, 0:n], in_=U_[i][:, 2:2 + n], func=AF.Identity,
                                                                    bias=vec[:, V_CB + ch:V_CB + ch + 1], scale=vec[:, V_CW2 + ch:V_CW2 + ch + 1]),
                                      reads=[f"U_{i}", "vec"], writes=[f"cv{i}"])
                                sc.op("dve", lambda e: e.scalar_tensor_tensor(out=cv[i][:, 0:n], in0=U_[i][:, 1:1 + n], scalar=vec[:, V_CW1 + ch:V_CW1 + ch + 1],
                                                                              in1=cv[i][:, 0:n], op0=ALU.mult, op1=ALU.add),
                                      reads=[f"U_{i}", f"cv{i}", "vec"], writes=[f"cv{i}"])
                                sc.op("dve", lambda e: e.scalar_tensor_tensor(out=cv[i][:, 0:n], in0=U_[i][:, 0:n], scalar=vec[:, V_CW0 + ch:V_CW0 + ch + 1],
                                                                              in1=cv[i][:, 0:n], op0=ALU.mult, op1=ALU.add),
                                      reads=[f"U_{i}", f"cv{i}", "vec"], writes=[f"cv{i}"])
                                sc.op("act", lambda e: e.activation(out=tail[:, ch, :], in_=U_[i][:, n:n + 2], func=AF.Copy),
                                      reads=[f"U_{i}"], writes=["tail"])
                            iv, ig, gi = 2 * (si % 2), 1 + 2 * (si % 2), si % 2
                            sc.op("act", lambda e: e.activation(out=gl[gi][:, 0:n], in_=cv[ig][:, 0:n], func=AF.Gelu), reads=[f"cv{ig}"], writes=[f"gl{gi}"])
                            sc.op("dve", lambda e: e.tensor_tensor(out=actb[:, j, cs], in0=gl[gi][:, 0:n], in1=cv[iv][:, 0:n], op=ALU.mult),
                                  reads=[f"gl{gi}", f"cv{iv}"], writes=["actb", "mgb", "obb", "oanb"])
                    for oc in range(8):
                        pdn, ppe, ppg = load_panel(Wd, oc, 22), load_panel(Wpe, oc, 2), load_panel(Wpg, oc, 8)
                        for (c0, n) in SUBS:
                            cs = slice(c0, c0 + n)
                            fp_, fk = mm(pdn, 22, n, lambda k: actb[:, k, cs], ["actb"])
                            pep, pek = mm(ppe, 2, n, lambda k: pb[:, k, cs], ["pb"])
                            pgp, pgk = mm(ppg, 8, n, lambda k: xbf[:, k, cs], ["xbf"])
                            sc.op("act", lambda e: e.activation(out=t0[1][:, 0:n], in_=pgp, func=AF.Sigmoid), reads=[pgk], writes=["t0_1"])
                            sc.op("dve", lambda e: e.tensor_tensor(out=t1[0][:, 0:n], in0=t0[1][:, 0:n], in1=pep, op=ALU.mult),
                                  reads=["t0_1", pek], writes=["t1_0"])
                            sc.op("dve", lambda e: e.tensor_tensor(out=t1[1][:, 0:n], in0=t1[0][:, 0:n], in1=fp_, op=ALU.add),
                                  reads=["t1_0", fk], writes=["t1_1"])
                            sc.op("dve", lambda e: e.scalar_tensor_tensor(out=R[:, oc, cs], in0=R[:, oc, cs], scalar=DN_ALPHA, in1=t1[1][:, 0:n],
                                                                          op0=ALU.mult, op1=ALU.add),
                                  reads=["R", "t1_1"], writes=["R"])
                    for si, (c0, n) in enumerate(SUBS):
                        layer_norm(V_L2G, V_L2B, c0, n, False)
                        lo = 2 if (tile == 0 and si == 0) else 0
                        g0 = C0 + c0 + lo - 2
                        sc.dma("sp", outv[:, :, g0:g0 + n - lo], R[:, :, c0 + lo:c0 + n], "out", reads=["R"], is_output=True)
                sc.finish()
    return nc


def _panels(W):
    K_, M_ = W.shape
    return np.ascontiguousarray(W.reshape(K_ // 128, 128, M_ // 128, 128).transpose(2, 1, 0, 3))


def _cols(v):
    return np.ascontiguousarray(v.reshape(-1, 128).T)


_PROGS = {}


def kernel(x, p, lb_logits, w_in, hg_norm_g, w_a, w_b, w_out, ln1_g, ln1_b,
           w_up, conv_w, conv_b, w_down, w_pe, w_pg, ln2_g, ln2_b):
    f32 = np.float32
    x = np.asarray(x, f32)
    p = np.asarray(p, f32)
    S = SEQ
    h = x[0]
    cM = consts_M()
    cores = list(range(NCORES))
    for l in range(DEPTH):
        hT = np.ascontiguousarray(h.T)
        if ("M", l) not in _PROGS:
            _PROGS[("M", l)] = build_M(S, l)
        maps = []
        for c in cores:
            hh = c // 2
            maps.append({"xT": hT, "wM": prep_M_weights(np.asarray(w_in[l], f32), c),
                         "lbl": np.ascontiguousarray(np.asarray(lb_logits, f32)[:, hh * 128:(hh + 1) * 128].T), "cst": cM})
        res = run_bass_kernel_spmd(_PROGS[("M", l)], maps, core_ids=cores)
        obT = np.concatenate([np.asarray(res.results[c]["ob"]).T for c in cores], axis=0)
        oaT = np.concatenate([np.asarray(res.results[c]["oaT"]) for c in cores], axis=0)
        def pad2(a):
            return np.concatenate([np.zeros((a.shape[0], 2), f32), a], axis=1)
        hTp, obTp, oaTp, pTp = pad2(hT), pad2(obT), pad2(oaT), pad2(np.ascontiguousarray(p[l, 0].T))
        wl = np.asarray(w_in[l], f32)
        Wg = _panels(np.concatenate([wl[:, 1536:2048], wl[:, 3584:4608], wl[:, 4608:5632]], axis=1))
        Wa, Wb, Wo = _panels(np.asarray(w_a[l], f32)), _panels(np.asarray(w_b[l], f32)), _panels(np.asarray(w_out[l], f32))
        Wu, Wd = _panels(np.asarray(w_up[l], f32)), _panels(np.asarray(w_down[l], f32))
        Wpe, Wpg = _panels(np.asarray(w_pe[l], f32)), _panels(np.asarray(w_pg[l], f32))
        cw = np.asarray(conv_w[l], f32)
        vec = np.zeros((128, NV_T), f32)
        vec[:, V_HG:V_HG + 4] = _cols(np.asarray(hg_norm_g[l], f32))
        vec[:, V_L1G:V_L1G + 8] = _cols(np.asarray(ln1_g[l], f32))
        vec[:, V_L1B:V_L1B + 8] = _cols(np.asarray(ln1_b[l], f32))
        vec[:, V_CW0:V_CW0 + 44] = _cols(cw[0])
        vec[:, V_CW1:V_CW1 + 44] = _cols(cw[1])
        vec[:, V_CW2:V_CW2 + 44] = _cols(cw[2])
        vec[:, V_CB:V_CB + 44] = _cols(np.asarray(conv_b[l], f32))
        vec[:, V_L2G:V_L2G + 8] = _cols(np.asarray(ln2_g[l], f32))
        vec[:, V_L2B:V_L2B + 8] = _cols(np.asarray(ln2_b[l], f32))
        if "T" not in _PROGS:
            _PROGS["T"] = build_T()
        maps = []
        for c in cores:
            sl = slice(c * 2048, c * 2048 + NTOK_T)
            v = vec.copy()
            v[:, V_HM] = 0.0 if c == 0 else 1.0
            maps.append({"xTc": np.ascontiguousarray(hTp[:, sl]), "obTc": np.ascontiguousarray(obTp[:, sl]),
                         "oaTc": np.ascontiguousarray(oaTp[:, sl]), "pTc": np.ascontiguousarray(pTp[:, sl]), "vec": v,
                         "Wg": Wg, "Wa": Wa, "Wb": Wb, "Wo": Wo, "Wu": Wu, "Wd": Wd, "Wpe": Wpe, "Wpg": Wpg})
        res = run_bass_kernel_spmd(_PROGS["T"], maps, core_ids=cores)
        outT = np.concatenate([np.asarray(res.results[c]["outT"]) for c in cores], axis=1)
        h = np.ascontiguousarray(outT.T)
    return h[None].astype(f32)
```

```python
import contextlib
import types
import numpy as np
import concourse.bass as bass
import concourse.mybir as mybir
from concourse.bass_utils import run_bass_kernel_spmd

F32 = mybir.dt.float32
BF16 = mybir.dt.bfloat16
AF = mybir.ActivationFunctionType
ALU = mybir.AluOpType

D_MODEL = 1024
SEQ = 16384
DEPTH = 2
NCORES = 8
D_FF = 2816
DN_ALPHA = (2 * DEPTH) ** 0.25
LN_EPS = 1e-5
RMS_EPS = 1e-6
F_FLOOR = 1e-30


class Sched:
    WIN = 512

    def __init__(self, nc, es):
        self.nc = nc
        self.es = es
        self.engs = {"pe": nc.tensor, "act": nc.scalar, "dve": nc.vector, "pool": nc.gpsimd, "sp": nc.sync}
        self.ops = []
        self.seq = {}
        self.last_w = {}
        self.readers = {}
        self.out_ids = []

    def _deps(self, reads, writes):
        d = []
        for b in reads:
            if b in self.last_w:
                d.append(self.last_w[b])
        for b in writes:
            if b in self.last_w:
                d.append(self.last_w[b])
            d.extend(self.readers.get(b, []))
        return d

    def _commit(self, i, reads, writes):
        for b in reads:
            self.readers.setdefault(b, []).append(i)
        for b in writes:
            self.last_w[b] = i
            self.readers[b] = []

    def _add(self, kind, e, payload, stream, reads, writes):
        deps = self._deps(reads, writes)
        sq = self.seq.get(stream, 0)
        self.seq[stream] = sq + 1
        i = len(self.ops)
        self.ops.append((kind, e, payload, deps, stream, sq))
        self._commit(i, reads, writes)
        return i

    @staticmethod
    def _freeze(fn):
        if fn.__closure__ is None:
            return fn
        cells = []
        for c in fn.__closure__:
            try:
                cells.append(types.CellType(c.cell_contents))
            except ValueError:
                cells.append(c)
        return types.FunctionType(fn.__code__, fn.__globals__, fn.__name__, fn.__defaults__, tuple(cells))

    def op(self, e, fn, reads=(), writes=()):
        return self._add("op", e, self._freeze(fn), e, reads, writes)

    def dma(self, q, out, in_, key, reads=(), writes=(), is_output=False):
        i = self._add("dma", q, (out, in_), "dma_" + key, reads, writes)
        if is_output:
            self.out_ids.append(i)
        return i

    def finish(self):
        ops = self.ops
        n = len(ops)
        seen = {e: {} for e in self.engs}
        waits = [[] for _ in range(n)]
        needs = [False] * n
        for i, (kind, e, payload, deps, stream, sq) in enumerate(ops):
            best = {}
            for d in deps:
                ds, dq = ops[d][4], ops[d][5]
                if e == "pe" and ds == "pe":
                    continue
                if seen[e].get(ds, -1) >= dq:
                    continue
                if ds not in best or best[ds][1] < dq:
                    best[ds] = (d, dq)
            for ds, (d, dq) in best.items():
                waits[i].append(d)
                seen[e][ds] = dq
                needs[d] = True
        final_waits = []
        lastout = {}
        for i in self.out_ids:
            lastout[ops[i][4]] = i
        for i in ops and self.out_ids:
            needs[i] = True
        cnt = {}
        sems = {}
        tok = [None] * n
        for i, (kind, e, payload, deps, stream, sq) in enumerate(ops):
            if kind == "dma" or needs[i]:
                step = 16 if kind == "dma" else 1
                c = cnt.get(stream, 0) + step
                cnt[stream] = c
                w = (c - 1) // self.WIN
                if (stream, w) not in sems:
                    sems[(stream, w)] = self.es.enter_context(self.nc.semaphore(f"s_{stream}_{w}"))
                tok[i] = (sems[(stream, w)], c - w * self.WIN, step)
        for i, (kind, e, payload, deps, stream, sq) in enumerate(ops):
            eng = self.engs[e]
            for d in waits[i]:
                eng.wait_ge(tok[d][0], tok[d][1])
            if kind == "dma":
                ins = eng.dma_start(out=payload[0], in_=payload[1])
            else:
                ins = payload(eng)
            if tok[i] is not None:
                ins.then_inc(tok[i][0], tok[i][2])
        done = set()
        for i in reversed(self.out_ids):
            key = id(tok[i][0])
            if key in done:
                continue
            done.add(key)
            self.engs["sp"].wait_ge(tok[i][0], tok[i][1])


C_ID, C_J, C_A, C_A2, C_SH, C_SH2, C_NEG, C_MD, C_MH, C_RST = 0, 128, 256, 384, 512, 640, 768, 896, 1024, 1088
CW_M = 1600


def consts_M():
    c = np.zeros((128, CW_M), np.float32)
    i = np.arange(128)
    c[i, C_ID + i] = 1.0
    c[i, C_J + (127 - i)] = 1.0
    for j in range(128):
        if 126 - j >= 0:
            c[126 - j, C_A + j] += 1.0
        c[127 - j, C_A + j] -= 1.0
    c[127, C_A2 + 127] = 1.0
    for t in range(1, 128):
        c[t - 1, C_SH + t] = 1.0
    c[127, C_SH2 + 0] = 1.0
    ii, jj = np.meshgrid(i, i, indexing="ij")
    c[:, C_NEG:C_NEG + 128] = np.where(ii + jj <= 127, -30000.0, 0.0)
    c[:, C_MD:C_MD + 128] = np.where(ii + jj >= 128, 1.0, 0.0)
    s64 = np.arange(64)
    ss, tt = np.meshgrid(s64, s64, indexing="ij")
    c[0:64, C_MH:C_MH + 64] = np.where(ss <= tt, 1.0, 0.0)
    r = np.ones(512, np.float32)
    r[0::64] = 0.0
    c[:, C_RST:C_RST + 512] = r[None, :]
    return c


def build_M(S, layer, upto=9):
    NQB = S // 128
    NTT = S // 512
    nc = bass.Bass("TRN2", target_bir_lowering=False)
    xT = nc.dram_tensor("xT", [D_MODEL, S], F32, kind="ExternalInput").ap()
    wM = nc.dram_tensor("wM", [128, 8, 512], F32, kind="ExternalInput").ap()
    lbl = nc.dram_tensor("lbl", [128, DEPTH], F32, kind="ExternalInput").ap()
    cst = nc.dram_tensor("cst", [128, CW_M], F32, kind="ExternalInput").ap()
    ob_o = nc.dram_tensor("ob", [S, 64], F32, kind="ExternalOutput").ap()
    oaT_o = nc.dram_tensor("oaT", [64, S], F32, kind="ExternalOutput").ap()
    xT_v = xT.rearrange("(kc p) t -> p kc t", p=128)

    with contextlib.ExitStack() as es:
        def sb(name, shape, dt):
            return es.enter_context(nc.sbuf_tensor(name, shape, dt))

        def ps(name, shape, dt):
            return es.enter_context(nc.psum_tensor(name, shape, dt))

        cf = sb("cf", [128, CW_M], F32)
        cb = sb("cb", [128, CW_M], BF16)
        wb = sb("wb", [128, 8, 512], BF16)
        QT = sb("QT", [128, S], BF16)
        KTr = sb("KTr", [128, S], BF16)
        DVr = sb("DVr", [128, NQB, 64], BF16)
        Vf = sb("Vf", [128, NQB, 64], F32)
        xb = [sb(f"xb{i}", [128, 8, 512], BF16) for i in range(2)]
        Kb = sb("Kb", [128, 4, 64], BF16)
        vhb = sb("vhb", [64, 8, 64], BF16)
        zeros = sb("zeros", [128, 1024], F32)
        lbt = sb("lbt", [128, 16], F32)
        bb = sb("bb", [128, 512], F32)
        e1 = sb("e1", [128, 512], F32)
        e2 = sb("e2", [128, 512], F32)
        qp = sb("qp", [128, 512], BF16)
        kp = sb("kp", [128, 512], BF16)
        sc8 = sb("sc8", [128, 64], F32)
        state = sb("state", [128, 64], F32)
        srb = [sb(f"srb{i}", [128, 64], BF16) for i in range(2)]
        scm = [sb(f"scm{i}", [64, 64], BF16) for i in range(8)]
        kTs = [sb(f"kTs{i}", [64, 128], BF16) for i in range(8)]
        utmp = sb("utmp", [128, 64], F32)
        oas = [sb(f"oas{i}", [64, 512], F32) for i in range(2)]
        omb = [sb(f"omb{i}", [128, 1024], F32) for i in range(2)]
        Eb = [sb(f"Eb{i}", [128, 1024], BF16) for i in range(4)]
        ETs = [sb(f"ETs{i}", [128, 1024], BF16) for i in range(2)]
        qs, sg, ff, kk = omb[0][:, 0:512], omb[0][:, 512:1024], omb[1][:, 0:512], omb[1][:, 512:1024]
        gg = ff
        obs = [sb(f"obs{i}", [128, 64], F32) for i in range(2)]
        Zp = [ps(f"Zp{i}", [128, 1024], F32) for i in range(2)]
        Ap = [ps(f"Ap{i}", [128, 512], F32) for i in range(2)]
        Fp = [Zp[0][:, 0:512], Zp[0][:, 512:1024], Zp[1][:, 0:512], Ap[0][:, :], Ap[1][:, :], Zp[1][:, 512:1024]]
        ET32 = ps("ET32", [128, 1024], F32)

        identb = cb[:, C_ID:C_ID + 128]
        Jb = cb[:, C_J:C_J + 128]
        NEGb = cb[:, C_NEG:C_NEG + 128]
        MDb = cb[:, C_MD:C_MD + 128]
        Af = cf[:, C_A:C_A + 128]
        A2f = cf[:, C_A2:C_A2 + 128]
        SHf = cf[:, C_SH:C_SH + 128]
        SH2f = cf[:, C_SH2:C_SH2 + 128]
        MHf = cf[0:64, C_MH:C_MH + 64]
        RSTf = cf[:, C_RST:C_RST + 512]

        with nc.Block() as block:
            @block.sync
            def _(_e):
                sc = Sched(nc, es)
                sc.dma("sp", cf[:], cst[:], "cf", writes=["cf"])
                sc.dma("pool", cb[:], cst[:], "cb", writes=["cb"])
                sc.dma("pool", wb[:], wM[:], "wb", writes=["wb"])
                sc.dma("sp", lbt[:, 0:DEPTH], lbl[:], "lbt", writes=["lbt"])
                sc.op("dve", lambda e: e.memset(zeros[:], 0.0), writes=["zeros"])
                sc.op("dve", lambda e: e.memset(state[:], 0.0), writes=["state"])
                sc.op("pool", lambda e: e.memset(QT[64:128, :], 0.0), writes=["QTpad"])
                sc.op("pool", lambda e: e.memset(KTr[64:128, :], 0.0), writes=["KTpad"])
                LB, OM, NOM = lbt[:, 8:9], lbt[:, 9:10], lbt[:, 10:11]
                mx = lbt[:, 4:5]
                sc.op("dve", lambda e: e.tensor_reduce(out=mx, in_=lbt[:, 0:DEPTH], axis=mybir.AxisListType.X, op=ALU.max),
                      reads=["lbt"], writes=["lb_mx"])
                sc.op("dve", lambda e: e.tensor_scalar(out=lbt[:, 5:6], in0=mx, scalar1=-1.0, scalar2=None, op0=ALU.mult),
                      reads=["lb_mx"], writes=["lb_nmx"])
                sc.op("act", lambda e: e.activation(out=lbt[:, 2:2 + DEPTH], in_=lbt[:, 0:DEPTH], func=AF.Exp,
                                                    bias=lbt[:, 5:6], scale=1.0),
                      reads=["lbt", "lb_nmx"], writes=["lb_e"])
                sc.op("dve", lambda e: e.tensor_reduce(out=lbt[:, 6:7], in_=lbt[:, 2:2 + DEPTH], axis=mybir.AxisListType.X, op=ALU.add),
                      reads=["lb_e"], writes=["lb_s"])
                sc.op("dve", lambda e: e.reciprocal(out=lbt[:, 7:8], in_=lbt[:, 6:7]), reads=["lb_s"], writes=["lb_r"])
                if layer == 0:
                    sc.op("dve", lambda e: e.memset(LB, 0.0), reads=["lb_r"], writes=["lb"])
                else:
                    sc.op("dve", lambda e: e.tensor_reduce(out=lbt[:, 11:12], in_=lbt[:, 3:2 + layer + 1],
                                                           axis=mybir.AxisListType.X, op=ALU.add),
                          reads=["lb_e"], writes=["lb_p"])
                    sc.op("dve", lambda e: e.tensor_tensor(out=LB, in0=lbt[:, 11:12], in1=lbt[:, 7:8], op=ALU.mult),
                          reads=["lb_p", "lb_r"], writes=["lb"])
                sc.op("dve", lambda e: e.tensor_scalar(out=OM, in0=LB, scalar1=-1.0, scalar2=1.0, op0=ALU.mult, op1=ALU.add),
                      reads=["lb"], writes=["om"])
                sc.op("dve", lambda e: e.tensor_scalar(out=NOM, in0=OM, scalar1=-1.0, scalar2=None, op0=ALU.mult),
                      reads=["om"], writes=["nom"])

                for tt in range(NTT if upto >= 1 else 0):
                    X = xb[tt % 2]
                    xk = f"xb{tt % 2}"
                    sc.dma("pool", X[:], xT_v[:, :, tt * 512:(tt + 1) * 512], xk, writes=[xk])
                    for oi, (bank, c0) in enumerate([(0, 0), (1, 128), (2, 256)]):
                        for kc in range(8):
                            sc.op("pe", lambda e, bank=bank, c0=c0, kc=kc: e.matmul(
                                Fp[bank][:, :], lhsT=wb[:, kc, c0:c0 + 128], rhs=X[:, kc, :], start=(kc == 0), stop=(kc == 7)),
                                reads=[xk, "wb"], writes=[f"F{bank}"])
                    if upto < 1.05:
                        sc.op("act", lambda e: e.activation(out=QT[0:64, tt * 512:(tt + 1) * 512], in_=Fp[2][0:64, :], func=AF.Copy),
                              reads=["F2"], writes=[f"QT{tt}"])
                        continue
                    for blk in range(4):
                        for kc in range(8):
                            sc.op("pe", lambda e, blk=blk, kc=kc: e.matmul(
                                Fp[3][:, blk * 128:(blk + 1) * 128], lhsT=X[:, kc, blk * 128:(blk + 1) * 128],
                                rhs=wb[:, kc, 320:448], start=(kc == 0), stop=(kc == 7)),
                                reads=[xk, "wb"], writes=["F3"])
                    for c in range(8 if upto >= 1.2 else 0):
                        for kc in range(8):
                            sc.op("pe", lambda e, c=c, kc=kc: e.matmul(
                                Fp[4][0:64, c * 64:(c + 1) * 64], lhsT=X[:, kc, c * 64:(c + 1) * 64],
                                rhs=wb[:, kc, 448:512], start=(kc == 0), stop=(kc == 7)),
                                reads=[xk, "wb"], writes=["F4"])
                    sc.op("act", lambda e: e.activation(out=QT[0:64, tt * 512:(tt + 1) * 512], in_=Fp[2][0:64, :], func=AF.Copy),
                          reads=["F2"], writes=[f"QT{tt}"])
                    F3v = Fp[3][:, :].rearrange("p (b c) -> p b c", b=4)
                    if upto >= 1.12:
                        sc.op("dve", lambda e: e.tensor_copy(out=Kb[:, :, :], in_=F3v[:, :, 0:64]), reads=["F3"], writes=["Kb"])
                    if upto >= 1.14:
                        sc.op("dve", lambda e: e.tensor_copy(out=Vf[:, tt * 4:(tt + 1) * 4, :], in_=F3v[:, :, 64:128]),
                              reads=["F3"], writes=[f"Vf{tt}"])
                    if upto >= 1.2:
                        sc.op("dve", lambda e: e.tensor_copy(out=vhb[:, :, :], in_=Fp[4][0:64, :].rearrange("p (c v) -> p c v", c=8)),
                              reads=["F4"], writes=["vhb"])
                    if upto < 1.3:
                        continue
                    for blk in range(4):
                        sc.op("pe", lambda e, blk=blk: e.matmul(
                            Fp[2][0:64, (3 - blk) * 128:(4 - blk) * 128], lhsT=Kb[:, blk, :], rhs=Jb, start=True, stop=True),
                            reads=["Kb", "cb"], writes=["F2"])
                    rb0 = NQB - 1 - (tt * 4 + 3)
                    sc.op("act", lambda e: e.activation(out=KTr[0:64, rb0 * 128:(rb0 + 4) * 128], in_=Fp[2][0:64, :], func=AF.Copy),
                          reads=["F2"], writes=[f"KTr{tt}"])
                    if upto < 1.5:
                        continue
                    for blk in range(4):
                        bs = tt * 4 + blk
                        sc.op("pe", lambda e, blk=blk, bs=bs: e.matmul(
                            Fp[3][:, (3 - blk) * 64:(4 - blk) * 64], lhsT=Af, rhs=Vf[:, bs, :], start=True, stop=(bs == 0)),
                            reads=[f"Vf{tt}", "cf"], writes=["F3"])
                        if bs > 0:
                            sc.op("pe", lambda e, blk=blk, bs=bs: e.matmul(
                                Fp[3][:, (3 - blk) * 64:(4 - blk) * 64], lhsT=A2f, rhs=Vf[:, bs - 1, :], start=False, stop=True),
                                reads=[f"Vf{(bs - 1) // 4}", "cf"], writes=["F3"])
                    sc.op("dve", lambda e: e.tensor_copy(out=DVr[:, rb0:rb0 + 4, :],
                                                         in_=Fp[3][:, 0:256].rearrange("p (b c) -> p b c", b=4)),
                          reads=["F3"], writes=[f"DVr{tt}"])

                    if upto < 2:
                        continue
                    sc.op("act", lambda e: e.activation(out=qs[:], in_=Fp[0][:, :], func=AF.Sigmoid), reads=["F0"], writes=["qs"])
                    sc.op("act", lambda e: e.activation(out=sg[:], in_=Fp[1][:, :], func=AF.Sigmoid), reads=["F1"], writes=["sg"])
                    sc.op("dve", lambda e: e.tensor_tensor(out=qs[:], in0=qs[:], in1=Fp[0][:, :], op=ALU.mult), reads=["qs", "F0"], writes=["qs"])
                    sc.op("dve", lambda e: e.tensor_scalar(out=ff[:], in0=sg[:], scalar1=OM, scalar2=LB, op0=ALU.mult, op1=ALU.add),
                          reads=["sg", "om", "lb"], writes=["ff"])
                    sc.op("dve", lambda e: e.tensor_scalar(out=ff[:], in0=ff[:], scalar1=F_FLOOR, scalar2=None, op0=ALU.max),
                          reads=["ff"], writes=["ff"])
                    sc.op("act", lambda e: e.activation(out=ff[:], in_=ff[:], func=AF.Ln), reads=["ff"], writes=["ff"])
                    sc.op("dve", lambda e: e.tensor_scalar(out=kk[:], in0=sg[:], scalar1=NOM, scalar2=OM, op0=ALU.mult, op1=ALU.add),
                          reads=["sg", "nom", "om"], writes=["kk"])
                    sc.op("dve", lambda e: e.tensor_tensor_scan(out=bb[:], data0=RSTf, data1=gg[:], initial=0.0,
                                                                op0=ALU.mult, op1=ALU.add),
                          reads=["ff", "cf"], writes=["bb"])
                    bbv = bb[:].rearrange("p (c t) -> p c t", c=8)
                    BM, D8, BL, NBM = sc8[:, 0:8], sc8[:, 8:16], sc8[:, 16:24], sc8[:, 24:32]
                    sc.op("dve", lambda e: e.tensor_copy(out=BM, in_=bbv[:, :, 31]), reads=["bb"], writes=["bm"])
                    sc.op("dve", lambda e: e.tensor_copy(out=BL, in_=bbv[:, :, 63]), reads=["bb"], writes=["bl"])
                    sc.op("dve", lambda e: e.tensor_scalar(out=NBM, in0=BM, scalar1=-1.0, scalar2=None, op0=ALU.mult),
                          reads=["bm"], writes=["nbm"])
                    sc.op("dve", lambda e: e.tensor_tensor(out=D8, in0=BL, in1=BM, op=ALU.subtract),
                          reads=["bl", "bm"], writes=["d8"])
                    sc.op("act", lambda e: e.activation(out=sc8[:, 32:56], in_=sc8[:, 0:24], func=AF.Exp),
                          reads=["bm", "d8", "bl"], writes=["er", "c2", "el"])
                    e1v = e1[:].rearrange("p (c t) -> p c t", c=8)
                    bmb = BM.unsqueeze(2).broadcast_to([128, 8, 64])
                    sc.op("dve", lambda e: e.tensor_tensor(out=e1v, in0=bbv, in1=bmb, op=ALU.subtract),
                          reads=["bb", "bm"], writes=["e1"])
                    sc.op("act", lambda e: e.activation(out=e2[:], in_=e1[:], func=AF.Exp, scale=-1.0), reads=["e1"], writes=["e2"])
                    sc.op("act", lambda e: e.activation(out=e1[:], in_=e1[:], func=AF.Exp), reads=["e1"], writes=["e1"])
                    sc.op("dve", lambda e: e.tensor_tensor(out=qp[:], in0=qs[:], in1=e1[:], op=ALU.mult),
                          reads=["qs", "e1"], writes=["qp"])
                    sc.op("dve", lambda e: e.tensor_tensor(out=kp[:], in0=kk[:], in1=e2[:], op=ALU.mult),
                          reads=["kk", "e2"], writes=["kp"])
                    for c in range(8):
                        sl = slice(c * 64, (c + 1) * 64)
                        sc.op("pe", lambda e: e.matmul(Fp[3][0:64, sl], lhsT=kp[:, sl], rhs=qp[:, sl], start=True, stop=True),
                              reads=["kp", "qp"], writes=["F3"])
                    for c in range(8):
                        sl = slice(c * 64, (c + 1) * 64)
                        sc.op("pe", lambda e: e.matmul(ET32[0:64, c * 128:(c + 1) * 128], lhsT=kp[:, sl], rhs=identb, start=True, stop=True),
                              reads=["kp", "cb"], writes=["B0"])
                    for c in range(8):
                        sl = slice(c * 64, (c + 1) * 64)
                        sc.op("dve", lambda e: e.tensor_tensor(out=scm[c][:], in0=Fp[3][0:64, sl], in1=MHf, op=ALU.mult),
                              reads=["F3", "cf"], writes=[f"scm{c}"])
                        sc.op("act", lambda e: e.activation(out=kTs[c][:], in_=ET32[0:64, c * 128:(c + 1) * 128], func=AF.Copy),
                              reads=["B0"], writes=[f"kTs{c}"])
                    for c in range(8):
                        sc.op("pe", lambda e: e.matmul(Fp[5][:, c * 64:(c + 1) * 64], lhsT=kTs[c][:], rhs=vhb[:, c, :], start=True, stop=True),
                              reads=[f"kTs{c}", "vhb"], writes=["F5"])
                    for c in range(8):
                        sl = slice(c * 64, (c + 1) * 64)
                        s2 = (tt * 8 + c) % 2
                        sc.op("dve", lambda e: e.tensor_scalar(out=srb[s2][:], in0=state[:], scalar1=sc8[:, 32 + c:33 + c],
                                                               scalar2=None, op0=ALU.mult),
                              reads=["state", "er"], writes=[f"srb{s2}"])
                        sc.op("pe", lambda e: e.matmul(Fp[4][0:64, sl], lhsT=srb[s2][:], rhs=qp[:, sl], start=True, stop=False),
                              reads=[f"srb{s2}", "qp"], writes=["F4"])
                        sc.op("pe", lambda e: e.matmul(Fp[4][0:64, sl], lhsT=vhb[:, c, :], rhs=scm[c][:], start=False, stop=True),
                              reads=[f"scm{c}", "vhb"], writes=["F4"])
                        sc.op("dve", lambda e: e.tensor_scalar(out=utmp[:], in0=Fp[5][:, c * 64:(c + 1) * 64], scalar1=sc8[:, 40 + c:41 + c],
                                                               scalar2=None, op0=ALU.mult),
                              reads=["F5", "c2"], writes=["utmp"])
                        sc.op("dve", lambda e: e.scalar_tensor_tensor(out=state[:], in0=state[:], scalar=sc8[:, 48 + c:49 + c],
                                                                      in1=utmp[:], op0=ALU.mult, op1=ALU.add),
                              reads=["state", "utmp", "el"], writes=["state"])
                    ok = f"oas{tt % 2}"
                    sc.op("act", lambda e: e.activation(out=oas[tt % 2][:], in_=Fp[4][0:64, :], func=AF.Copy), reads=["F4"], writes=[ok])
                    sc.dma("sp", oaT_o[:, tt * 512:(tt + 1) * 512], oas[tt % 2][:], f"oaT_out{tt % 2}", reads=[ok], is_output=True)

                TWK = 1024
                def qb_tiles(qb):
                    rstart = (NQB - 1 - qb) * 128
                    L = S - rstart
                    nt = (L + TWK - 1) // TWK
                    return [(qb, ti, rstart + ti * TWK, min(TWK, S - rstart - ti * TWK), ti == nt - 1) for ti in range(nt)]

                tiles = []
                prev_of = []
                for q0 in range(0, NQB, 2):
                    la, lb_ = qb_tiles(q0), (qb_tiles(q0 + 1) if q0 + 1 < NQB else [])
                    last_idx = {}
                    for i in range(max(len(la), len(lb_))):
                        for lst in (la, lb_):
                            if i < len(lst):
                                t = lst[i]
                                prev_of.append(last_idx.get(t[0], -1))
                                last_idx[t[0]] = len(tiles)
                                tiles.append(t)
                if upto < 3:
                    tiles = []
                NT = len(tiles)
                qt_keys = [f"QT{t}" for t in range(NTT)] + ["QTpad"]
                kt_keys = [f"KTr{t}" for t in range(NTT)] + ["KTpad"]
                dv_keys = [f"DVr{t}" for t in range(NTT)]
                vf_keys = [f"Vf{t}" for t in range(NTT)]
                zkeys = [["F0", "F1"], ["F2", "F5"]]
                hg_alias = [["qs", "sg"], ["ff", "kk"]]

                def stA(n):
                    qb, ti, r0, w, last = tiles[n]
                    z = Zp[n % 2]
                    for h0 in range(0, w, 512):
                        wh = min(512, w - h0)
                        zk = zkeys[n % 2][h0 // 512]
                        diag = (ti == 0 and h0 == 0)
                        sc.op("pe", lambda e: e.matmul(z[:, h0:h0 + wh], lhsT=QT[:, qb * 128:(qb + 1) * 128], rhs=KTr[:, r0 + h0:r0 + h0 + wh],
                                                       start=True, stop=(not diag)),
                              reads=qt_keys + kt_keys, writes=[zk])
                        if diag:
                            sc.op("pe", lambda e: e.matmul(z[:, 0:128], lhsT=identb, rhs=NEGb, start=False, stop=True),
                                  reads=["cb"], writes=[zk])

                def stB(n):
                    qb, ti, r0, w, last = tiles[n]
                    sc.op("act", lambda e: e.activation(out=omb[n % 2][:, 0:w], in_=Zp[n % 2][:, 0:w], func=AF.Sigmoid, scale=-0.125),
                          reads=zkeys[n % 2], writes=[f"omb{n % 2}"] + hg_alias[n % 2])

                def stC(n):
                    qb, ti, r0, w, last = tiles[n]
                    if ti == 0:
                        init = 1.0
                        rd = []
                    else:
                        pn = prev_of[n]
                        wp = tiles[pn][3]
                        init = Eb[pn % 4][:, wp - 1:wp]
                        rd = [f"Eb{pn % 4}"]
                    sc.op("dve", lambda e: e.tensor_tensor_scan(out=Eb[n % 4][:, 0:w], data0=omb[n % 2][:, 0:w], data1=zeros[:, 0:w],
                                                                initial=init, op0=ALU.mult, op1=ALU.add),
                          reads=[f"omb{n % 2}", "zeros"] + rd, writes=[f"Eb{n % 4}"])

                def stD(n):
                    qb, ti, r0, w, last = tiles[n]
                    for sbk in range(w // 128):
                        sl = slice(sbk * 128, (sbk + 1) * 128)
                        sc.op("pe", lambda e: e.matmul(ET32[:, sl], lhsT=Eb[n % 4][:, sl], rhs=identb, start=True, stop=True),
                              reads=[f"Eb{n % 4}", "cb"], writes=["B0"])

                def stF(n):
                    qb, ti, r0, w, last = tiles[n]
                    sc.op("act", lambda e: e.activation(out=ETs[n % 2][:, 0:w], in_=ET32[:, 0:w], func=AF.Copy),
                          reads=["B0"], writes=[f"ETs{n % 2}"])
                    if ti == 0:
                        sc.op("dve", lambda e: e.tensor_tensor(out=ETs[n % 2][:, 0:128], in0=ETs[n % 2][:, 0:128], in1=MDb, op=ALU.mult),
                              reads=[f"ETs{n % 2}", "cb"], writes=[f"ETs{n % 2}"])

                def stG(n):
                    qb, ti, r0, w, last = tiles[n]
                    acc = Ap[qb % 2]
                    ak = f"F{3 + qb % 2}"
                    for sbk in range(w // 128):
                        sl = slice(sbk * 128, (sbk + 1) * 128)
                        br = r0 // 128 + sbk
                        sc.op("pe", lambda e: e.matmul(acc[:, 0:64], lhsT=ETs[n % 2][:, sl], rhs=DVr[:, br, :],
                                                       start=(ti == 0 and sbk == 0), stop=False),
                              reads=[f"ETs{n % 2}"] + dv_keys, writes=[ak])
                    if last:
                        sc.op("pe", lambda e: e.matmul(acc[:, 0:64], lhsT=SHf, rhs=Vf[:, qb, :], start=False, stop=(qb == 0)),
                              reads=["cf"] + vf_keys, writes=[ak])
                        if qb > 0:
                            sc.op("pe", lambda e: e.matmul(acc[:, 0:64], lhsT=SH2f, rhs=Vf[:, qb - 1, :], start=False, stop=True),
                                  reads=["cf"] + vf_keys, writes=[ak])
                        sc.op("dve", lambda e: e.tensor_copy(out=obs[qb % 2][:], in_=acc[:, 0:64]), reads=[ak], writes=[f"obs{qb % 2}"])
                        sc.dma("sp", ob_o[qb * 128:(qb + 1) * 128, :], obs[qb % 2][:], f"ob_out{qb % 2}", reads=[f"obs{qb % 2}"], is_output=True)

                for n in range(-2, NT + 1):
                    if 0 <= n + 2 < NT:
                        stA(n + 2)
                    if 0 <= n < NT and upto >= 3.3:
                        stC(n)
                    if 0 <= n + 2 < NT and upto >= 3.2:
                        stB(n + 2)
                    if 0 <= n < NT:
                        if upto >= 3.4:
                            stD(n)
                        if upto >= 3.5:
                            stF(n)
                    if 0 <= n - 1 < NT and upto >= 3.6:
                        stG(n - 1)
                sc.finish()
    return nc


def prep_M_weights(w_in_l, core):
    hh, vh = core // 2, core % 2
    cols = np.concatenate([
        np.arange(hh * 128, (hh + 1) * 128),
        512 + np.arange(hh * 128, (hh + 1) * 128),
        2048 + np.arange(core * 64, (core + 1) * 64),
        2560 + np.arange(core * 64, (core + 1) * 64),
        3072 + np.arange(core * 64, (core + 1) * 64),
        1024 + hh * 128 + vh * 64 + np.arange(64),
    ])
    w = w_in_l[:, cols]
    return np.ascontiguousarray(w.reshape(8, 128, 512).transpose(1, 0, 2))


NTOK_T = 2050
TW = 410
V_HG, V_L1G, V_L1B, V_CW0, V_CW1, V_CW2, V_CB, V_L2G, V_L2B, V_HM, NV_T = 0, 4, 12, 20, 64, 108, 152, 196, 204, 212, 213


def build_T():
    BW = 1025
    SUBS = [(0, 410), (410, 410), (820, 205)]
    NMAX = 410
    nc = bass.Bass("TRN2", target_bir_lowering=False)
    xTd = nc.dram_tensor("xTc", [D_MODEL, NTOK_T], F32, kind="ExternalInput").ap()
    obd = nc.dram_tensor("obTc", [512, NTOK_T], F32, kind="ExternalInput").ap()
    oad = nc.dram_tensor("oaTc", [512, NTOK_T], F32, kind="ExternalInput").ap()
    pd = nc.dram_tensor("pTc", [256, NTOK_T], F32, kind="ExternalInput").ap()
    vecd = nc.dram_tensor("vec", [128, NV_T], F32, kind="ExternalInput").ap()
    Wg = nc.dram_tensor("Wg", [20, 128, 8, 128], F32, kind="ExternalInput").ap()
    Wa = nc.dram_tensor("Wa", [8, 128, 4, 128], F32, kind="ExternalInput").ap()
    Wb = nc.dram_tensor("Wb", [8, 128, 4, 128], F32, kind="ExternalInput").ap()
    Wo = nc.dram_tensor("Wo", [8, 128, 8, 128], F32, kind="ExternalInput").ap()
    Wu = nc.dram_tensor("Wu", [44, 128, 8, 128], F32, kind="ExternalInput").ap()
    Wd = nc.dram_tensor("Wd", [8, 128, 22, 128], F32, kind="ExternalInput").ap()
    Wpe = nc.dram_tensor("Wpe", [8, 128, 2, 128], F32, kind="ExternalInput").ap()
    Wpg = nc.dram_tensor("Wpg", [8, 128, 8, 128], F32, kind="ExternalInput").ap()
    outd = nc.dram_tensor("outT", [D_MODEL, 2048], F32, kind="ExternalOutput").ap()
    xv = xTd.rearrange("(kc p) t -> p kc t", p=128)
    obv = obd.rearrange("(kc p) t -> p kc t", p=128)
    oav = oad.rearrange("(kc p) t -> p kc t", p=128)
    pv = pd.rearrange("(kc p) t -> p kc t", p=128)
    outv = outd.rearrange("(kc p) t -> p kc t", p=128)

    with contextlib.ExitStack() as es:
        def sb(name, shape, dt):
            return es.enter_context(nc.sbuf_tensor(name, shape, dt))

        def ps(name, shape, dt):
            return es.enter_context(nc.psum_tensor(name, shape, dt))

        vec = sb("vec_sb", [128, NV_T], F32)
        o1024 = sb("o1024", [128, 128], F32)
        o128 = sb("o128", [128, 128], F32)
        o1024b = sb("o1024b", [128, 128], BF16)
        o128b = sb("o128b", [128, 128], BF16)
        t0b = [sb(f"t0b_{i}", [128, NMAX], BF16) for i in range(2)]
        R = sb("R", [128, 8, BW], F32)
        xbf = sb("xbf", [128, 8, BW], BF16)
        pb = sb("pb", [128, 2, BW], BF16)
        actb = sb("actb", [128, 22, BW], BF16)
        mgb = actb[:, 0:8, :]
        obb = actb[:, 8:12, :]
        oanb = actb[:, 12:16, :]
        oaf = [sb(f"oaf{i}", [128, NMAX], F32) for i in range(2)]
        tail = sb("tail", [128, 44, 2], F32)
        t0 = [sb(f"t0_{i}", [128, NMAX], F32) for i in range(2)]
        t1 = [sb(f"t1_{i}", [128, NMAX], F32) for i in range(2)]
        t2 = [sb(f"t2_{i}", [128, NMAX], F32) for i in range(2)]
        rstd = sb("rstd", [128, NMAX], F32)
        dsq = sb("dsq", [128, NMAX], F32)
        U_ = [sb(f"U_{i}", [128, NMAX + 2], F32) for i in range(4)]
        cv = [sb(f"cv{i}", [128, NMAX], F32) for i in range(4)]
        gl = [sb(f"gl{i}", [128, NMAX], F32) for i in range(2)]
        pan8 = [sb(f"pan8_{i}", [128, 8, 128], BF16) for i in range(6)]
        pan22 = [sb(f"pan22_{i}", [128, 22, 128], BF16) for i in range(2)]
        P = [ps(f"P{i}", [128, 512], F32) for i in range(8)]

        with nc.Block() as block:
            @block.sync
            def _(_e):
                sc = Sched(nc, es)
                st = {"pan8": 0, "pan22": 0, "ps": 0, "oaf": 0}
                sc.dma("sp", vec[:], vecd[:], "vec", writes=["vec"])
                sc.op("dve", lambda e: e.memset(o1024[:], 1.0 / 1024.0), writes=["o1024"])
                sc.op("dve", lambda e: e.memset(o128[:], 1.0 / 128.0), writes=["o128"])
                sc.op("dve", lambda e: e.memset(o1024b[:], 1.0 / 1024.0), writes=["o1024b"])
                sc.op("dve", lambda e: e.memset(o128b[:], 1.0 / 128.0), writes=["o128b"])
                sc.op("dve", lambda e: e.memset(tail[:], 0.0), writes=["tail"])

                def load_panel(W_ap, idx, kc):
                    if kc == 22:
                        i = st["pan22"] % 2
                        st["pan22"] += 1
                        buf, key = pan22[i], f"pan22_{i}"
                        sc.dma("pool", buf[:, :, :], W_ap[idx], key, writes=[key])
                        return buf, key
                    i = st["pan8"] % 6
                    st["pan8"] += 1
                    buf, key = pan8[i], f"pan8_{i}"
                    sc.dma("pool", buf[:, 0:kc, :], W_ap[idx], key, writes=[key])
                    return buf, key

                def mm(panel, kc, n, rhs_fn, rhs_keys):
                    buf, key = panel
                    b = st["ps"] % 8
                    st["ps"] += 1
                    for k in range(kc):
                        r = rhs_fn(k)
                        sc.op("pe", lambda e: e.matmul(P[b][:, 0:n], lhsT=buf[:, k, :], rhs=r, start=(k == 0), stop=(k == kc - 1)),
                              reads=[key] + rhs_keys, writes=[f"P{b}"])
                    return P[b][:, 0:n], f"P{b}"

                def stat(ones_ap, okey, n, rhs_list, rhs_keys):
                    b = st["ps"] % 8
                    st["ps"] += 1
                    m = len(rhs_list)
                    for k, r in enumerate(rhs_list):
                        sc.op("pe", lambda e: e.matmul(P[b][:, 0:n], lhsT=ones_ap, rhs=r, start=(k == 0), stop=(k == m - 1)),
                              reads=[okey] + rhs_keys, writes=[f"P{b}"])
                    return P[b][:, 0:n], f"P{b}"

                def layer_norm(gcol, bcol, c0, n, also_bf):
                    cs = slice(c0, c0 + n)
                    mean, mk = stat(o1024[:], "o1024", n, [R[:, k, cs] for k in range(8)], ["R"])
                    for k in range(8):
                        sc.op("dve", lambda e: e.tensor_tensor(out=R[:, k, cs], in0=R[:, k, cs], in1=mean, op=ALU.subtract),
                              reads=["R", mk], writes=["R"])
                    b = st["ps"] % 8
                    st["ps"] += 1
                    for k in range(8):
                        i = k % 2
                        sc.op("act", lambda e: e.activation(out=t0b[i][:, 0:n], in_=R[:, k, cs], func=AF.Square),
                              reads=["R"], writes=[f"t0b_{i}"])
                        sc.op("pe", lambda e: e.matmul(P[b][:, 0:n], lhsT=o1024b[:], rhs=t0b[i][:, 0:n], start=(k == 0), stop=(k == 7)),
                              reads=["o1024b", f"t0b_{i}"], writes=[f"P{b}"])
                    sc.op("act", lambda e: e.activation(out=dsq[:, 0:n], in_=P[b][:, 0:n], func=AF.Sqrt, bias=LN_EPS, scale=1.0),
                          reads=[f"P{b}"], writes=["dsq"])
                    sc.op("dve", lambda e: e.reciprocal(out=rstd[:, 0:n], in_=dsq[:, 0:n]), reads=["dsq"], writes=["rstd"])
                    sc.op("dve", lambda e: e.tensor_tensor(out=R[:, 0, cs], in0=R[:, 0, cs], in1=rstd[:, 0:n], op=ALU.mult),
                          reads=["R", "rstd"], writes=["R"] + [f"Rk{k}" for k in range(8)])
                    for k in range(8):
                        if k > 0:
                            sc.op("dve", lambda e: e.tensor_tensor(out=R[:, k, cs], in0=R[:, k, cs], in1=rstd[:, 0:n], op=ALU.mult),
                                  reads=["rstd", f"Rk{k}"], writes=[f"Rk{k}"])
                        sc.op("act", lambda e: e.activation(out=R[:, k, cs], in_=R[:, k, cs], func=AF.Identity,
                                                            bias=vec[:, bcol + k:bcol + k + 1], scale=vec[:, gcol + k:gcol + k + 1]),
                              reads=[f"Rk{k}", "vec"], writes=[f"Rk{k}"])
                        if also_bf:
                            sc.op("dve", lambda e: e.tensor_copy(out=xbf[:, k, cs], in_=R[:, k, cs]), reads=[f"Rk{k}"], writes=["xbf"])
                    sc.op("dve", lambda e: e.tensor_copy(out=dsq[:, 0:1], in_=rstd[:, 0:1]),
                          reads=[f"Rk{k}" for k in range(8)] + ["rstd"], writes=["R", "dsq"])

                for tile in range(NTOK_T // BW):
                    C0 = tile * BW
                    CS = slice(C0, C0 + BW)
                    sc.dma("sp", R[:, :, :], xv[:, :, CS], "R", writes=["R"])
                    sc.dma("pool", xbf[:, :, :], xv[:, :, CS], "xbf", writes=["xbf"])
                    sc.dma("pool", obb, obv[:, :, CS], "obb", writes=["obb", "actb"])
                    sc.dma("pool", pb[:, :, :], pv[:, :, CS], "pb", writes=["pb"])
                    for h in range(4):
                        pan = load_panel(Wg, h, 8)
                        for (c0, n) in SUBS:
                            cs = slice(c0, c0 + n)
                            i = st["oaf"] % 2
                            st["oaf"] += 1
                            sc.dma("sp", oaf[i][:, 0:n], oav[:, h, C0 + c0:C0 + c0 + n], f"oaf{i}", writes=[f"oaf{i}"])
                            sc.op("act", lambda e: e.activation(out=t0b[i][:, 0:n], in_=oaf[i][:, 0:n], func=AF.Square),
                                  reads=[f"oaf{i}"], writes=[f"t0b_{i}"])
                            ms, msk = stat(o128b[:], "o128b", n, [t0b[i][:, 0:n]], [f"t0b_{i}"])
                            sc.op("act", lambda e: e.activation(out=dsq[:, 0:n], in_=ms, func=AF.Sqrt, bias=RMS_EPS, scale=1.0),
                                  reads=[msk], writes=["dsq"])
                            sc.op("dve", lambda e: e.reciprocal(out=rstd[:, 0:n], in_=dsq[:, 0:n]), reads=["dsq"], writes=["rstd"])
                            gp, gk = mm(pan, 8, n, lambda k: xbf[:, k, cs], ["xbf"])
                            sc.op("act", lambda e: e.activation(out=t1[i][:, 0:n], in_=gp, func=AF.Silu), reads=[gk], writes=[f"t1_{i}"])
                            sc.op("dve", lambda e: e.scalar_tensor_tensor(out=t2[i][:, 0:n], in0=oaf[i][:, 0:n], scalar=vec[:, V_HG + h:V_HG + h + 1],
                                                                          in1=rstd[:, 0:n], op0=ALU.mult, op1=ALU.mult),
                                  reads=[f"oaf{i}", "rstd", "vec"], writes=[f"t2_{i}"])
                            sc.op("dve", lambda e: e.tensor_tensor(out=oanb[:, h, cs], in0=t2[i][:, 0:n], in1=t1[i][:, 0:n], op=ALU.mult),
                                  reads=[f"t2_{i}", f"t1_{i}"], writes=["oanb", "actb"])
                    for oc in range(8):
                        pga, pgb = load_panel(Wg, 4 + oc, 8), load_panel(Wg, 12 + oc, 8)
                        pya, pyb = load_panel(Wa, oc, 4), load_panel(Wb, oc, 4)
                        for (c0, n) in SUBS:
                            cs = slice(c0, c0 + n)
                            gap, gak = mm(pga, 8, n, lambda k: xbf[:, k, cs], ["xbf"])
                            gbp, gbk = mm(pgb, 8, n, lambda k: xbf[:, k, cs], ["xbf"])
                            yap, yak = mm(pya, 4, n, lambda k: oanb[:, k, cs], ["oanb"])
                            ybp, ybk = mm(pyb, 4, n, lambda k: obb[:, k, cs], ["obb"])
                            sc.op("act", lambda e: e.activation(out=t0[0][:, 0:n], in_=gap, func=AF.Sigmoid), reads=[gak], writes=["t0_0"])
                            sc.op("act", lambda e: e.activation(out=t0[1][:, 0:n], in_=gbp, func=AF.Sigmoid), reads=[gbk], writes=["t0_1"])
                            sc.op("dve", lambda e: e.tensor_tensor(out=t1[0][:, 0:n], in0=t0[0][:, 0:n], in1=yap, op=ALU.mult),
                                  reads=["t0_0", yak], writes=["t1_0"])
                            sc.op("dve", lambda e: e.tensor_tensor(out=t1[1][:, 0:n], in0=t0[1][:, 0:n], in1=ybp, op=ALU.mult),
                                  reads=["t0_1", ybk], writes=["t1_1"])
                            sc.op("dve", lambda e: e.tensor_tensor(out=mgb[:, oc, cs], in0=t1[0][:, 0:n], in1=t1[1][:, 0:n], op=ALU.add),
                                  reads=["t1_0", "t1_1"], writes=["mgb", "actb"])
                    for oc in range(8):
                        pan = load_panel(Wo, oc, 8)
                        for (c0, n) in SUBS:
                            cs = slice(c0, c0 + n)
                            hp, hk = mm(pan, 8, n, lambda k: mgb[:, k, cs], ["mgb"])
                            sc.op("dve", lambda e: e.scalar_tensor_tensor(out=R[:, oc, cs], in0=R[:, oc, cs], scalar=DN_ALPHA, in1=hp,
                                                                          op0=ALU.mult, op1=ALU.add),
                                  reads=["R", hk], writes=["R"])
                    for (c0, n) in SUBS:
                        layer_norm(V_L1G, V_L1B, c0, n, True)
                    for j in range(22):
                        pans = [load_panel(Wu, j, 8), load_panel(Wu, 22 + j, 8)]
                        for si, (c0, n) in enumerate(SUBS):
                            cs = slice(c0, c0 + n)
                            for half in range(2):
                                ch = half * 22 + j
                                i = half + 2 * (si % 2)
                                up, uk = mm(pans[half], 8, n, lambda k: xbf[:, k, cs], ["xbf"])
                                sc.op("act", lambda e: e.activation(out=U_[i][:, 2:2 + n], in_=up, func=AF.Copy),
                                      reads=[uk], writes=[f"U_{i}"])
                                sc.op("dve", lambda e: e.tensor_copy(out=U_[i][:, 0:2], in_=tail[:, ch, :]),
                                      reads=["tail", f"U_{i}"], writes=[f"U_{i}"])
                                if tile == 0 and si == 0:
                                    sc.op("dve", lambda e: e.tensor_scalar(out=U_[i][:, 2:4], in0=U_[i][:, 2:4], scalar1=vec[:, V_HM:V_HM + 1],
                                                                           scalar2=None, op0=ALU.mult),
                                          reads=[f"U_{i}", "vec"], writes=[f"U_{i}"])
                                sc.op("act", lambda e: e.activation(out=cv[i][:, 0:n], in_=U_[i][:, 2:2 + n], func=AF.Identity,
                                                                    bias=vec[:, V_CB + ch:V_CB + ch + 1], scale=vec[:, V_CW2 + ch:V_CW2 + ch + 1]),
                                      reads=[f"U_{i}", "vec"], writes=[f"cv{i}"])
                                sc.op("dve", lambda e: e.scalar_tensor_tensor(out=cv[i][:, 0:n], in0=U_[i][:, 1:1 + n], scalar=vec[:, V_CW1 + ch:V_CW1 + ch + 1],
                                                                              in1=cv[i][:, 0:n], op0=ALU.mult, op1=ALU.add),
                                      reads=[f"U_{i}", f"cv{i}", "vec"], writes=[f"cv{i}"])
                                sc.op("dve", lambda e: e.scalar_tensor_tensor(out=cv[i][:, 0:n], in0=U_[i][:, 0:n], scalar=vec[:, V_CW0 + ch:V_CW0 + ch + 1],
                                                                              in1=cv[i][:, 0:n], op0=ALU.mult, op1=ALU.add),
                                      reads=[f"U_{i}", f"cv{i}", "vec"], writes=[f"cv{i}"])
                                sc.op("act", lambda e: e.activation(out=tail[:, ch, :], in_=U_[i][:, n:n + 2], func=AF.Copy),
                                      reads=[f"U_{i}"], writes=["tail"])
                            iv, ig, gi = 2 * (si % 2), 1 + 2 * (si % 2), si % 2
                            sc.op("act", lambda e: e.activation(out=gl[gi][:, 0:n], in_=cv[ig][:, 0:n], func=AF.Gelu), reads=[f"cv{ig}"], writes=[f"gl{gi}"])
                            sc.op("dve", lambda e: e.tensor_tensor(out=actb[:, j, cs], in0=gl[gi][:, 0:n], in1=cv[iv][:, 0:n], op=ALU.mult),
                                  reads=[f"gl{gi}", f"cv{iv}"], writes=["actb", "mgb", "obb", "oanb"])
                    for oc in range(8):
                        pdn, ppe, ppg = load_panel(Wd, oc, 22), load_panel(Wpe, oc, 2), load_panel(Wpg, oc, 8)
                        for (c0, n) in SUBS:
                            cs = slice(c0, c0 + n)
                            fp_, fk = mm(pdn, 22, n, lambda k: actb[:, k, cs], ["actb"])
                            pep, pek = mm(ppe, 2, n, lambda k: pb[:, k, cs], ["pb"])
                            pgp, pgk = mm(ppg, 8, n, lambda k: xbf[:, k, cs], ["xbf"])
                            sc.op("act", lambda e: e.activation(out=t0[1][:, 0:n], in_=pgp, func=AF.Sigmoid), reads=[pgk], writes=["t0_1"])
                            sc.op("dve", lambda e: e.tensor_tensor(out=t1[0][:, 0:n], in0=t0[1][:, 0:n], in1=pep, op=ALU.mult),
                                  reads=["t0_1", pek], writes=["t1_0"])
                            sc.op("dve", lambda e: e.tensor_tensor(out=t1[1][:, 0:n], in0=t1[0][:, 0:n], in1=fp_, op=ALU.add),
                                  reads=["t1_0", fk], writes=["t1_1"])
                            sc.op("dve", lambda e: e.scalar_tensor_tensor(out=R[:, oc, cs], in0=R[:, oc, cs], scalar=DN_ALPHA, in1=t1[1][:, 0:n],
                                                                          op0=ALU.mult, op1=ALU.add),
                                  reads=["R", "t1_1"], writes=["R"])
                    for si, (c0, n) in enumerate(SUBS):
                        layer_norm(V_L2G, V_L2B, c0, n, False)
                        lo = 2 if (tile == 0 and si == 0) else 0
                        g0 = C0 + c0 + lo - 2
                        sc.dma("sp", outv[:, :, g0:g0 + n - lo], R[:, :, c0 + lo:c0 + n], "out", reads=["R"], is_output=True)
                sc.finish()
    return nc


def _panels(W):
    K_, M_ = W.shape
    return np.ascontiguousarray(W.reshape(K_ // 128, 128, M_ // 128, 128).transpose(2, 1, 0, 3))


def _cols(v):
    return np.ascontiguousarray(v.reshape(-1, 128).T)


_PROGS = {}


def kernel(x, p, lb_logits, w_in, hg_norm_g, w_a, w_b, w_out, ln1_g, ln1_b,
           w_up, conv_w, conv_b, w_down, w_pe, w_pg, ln2_g, ln2_b):
    f32 = np.float32
    x = np.asarray(x, f32)
    p = np.asarray(p, f32)
    S = SEQ
    h = x[0]
    cM = consts_M()
    cores = list(range(NCORES))
    for l in range(DEPTH):
        hT = np.ascontiguousarray(h.T)
        if ("M", l) not in _PROGS:
            _PROGS[("M", l)] = build_M(S, l)
        maps = []
        for c in cores:
            hh = c // 2
            maps.append({"xT": hT, "wM": prep_M_weights(np.asarray(w_in[l], f32), c),
                         "lbl": np.ascontiguousarray(np.asarray(lb_logits, f32)[:, hh * 128:(hh + 1) * 128].T), "cst": cM})
        res = run_bass_kernel_spmd(_PROGS[("M", l)], maps, core_ids=cores)
        obT = np.concatenate([np.asarray(res.results[c]["ob"]).T for c in cores], axis=0)
        oaT = np.concatenate([np.asarray(res.results[c]["oaT"]) for c in cores], axis=0)
        def pad2(a):
            return np.concatenate([np.zeros((a.shape[0], 2), f32), a], axis=1)
        hTp, obTp, oaTp, pTp = pad2(hT), pad2(obT), pad2(oaT), pad2(np.ascontiguousarray(p[l, 0].T))
        wl = np.asarray(w_in[l], f32)
        Wg = _panels(np.concatenate([wl[:, 1536:2048], wl[:, 3584:4608], wl[:, 4608:5632]], axis=1))
        Wa, Wb, Wo = _panels(np.asarray(w_a[l], f32)), _panels(np.asarray(w_b[l], f32)), _panels(np.asarray(w_out[l], f32))
        Wu, Wd = _panels(np.asarray(w_up[l], f32)), _panels(np.asarray(w_down[l], f32))
        Wpe, Wpg = _panels(np.asarray(w_pe[l], f32)), _panels(np.asarray(w_pg[l], f32))
        cw = np.asarray(conv_w[l], f32)
        vec = np.zeros((128, NV_T), f32)
        vec[:, V_HG:V_HG + 4] = _cols(np.asarray(hg_norm_g[l], f32))
        vec[:, V_L1G:V_L1G + 8] = _cols(np.asarray(ln1_g[l], f32))
        vec[:, V_L1B:V_L1B + 8] = _cols(np.asarray(ln1_b[l], f32))
        vec[:, V_CW0:V_CW0 + 44] = _cols(cw[0])
        vec[:, V_CW1:V_CW1 + 44] = _cols(cw[1])
        vec[:, V_CW2:V_CW2 + 44] = _cols(cw[2])
        vec[:, V_CB:V_CB + 44] = _cols(np.asarray(conv_b[l], f32))
        vec[:, V_L2G:V_L2G + 8] = _cols(np.asarray(ln2_g[l], f32))
        vec[:, V_L2B:V_L2B + 8] = _cols(np.asarray(ln2_b[l], f32))
        if "T" not in _PROGS:
            _PROGS["T"] = build_T()
        maps = []
        for c in cores:
            sl = slice(c * 2048, c * 2048 + NTOK_T)
            v = vec.copy()
            v[:, V_HM] = 0.0 if c == 0 else 1.0
            maps.append({"xTc": np.ascontiguousarray(hTp[:, sl]), "obTc": np.ascontiguousarray(obTp[:, sl]),
                         "oaTc": np.ascontiguousarray(oaTp[:, sl]), "pTc": np.ascontiguousarray(pTp[:, sl]), "vec": v,
                         "Wg": Wg, "Wa": Wa, "Wb": Wb, "Wo": Wo, "Wu": Wu, "Wd": Wd, "Wpe": Wpe, "Wpg": Wpg})
        res = run_bass_kernel_spmd(_PROGS["T"], maps, core_ids=cores)
        outT = np.concatenate([np.asarray(res.results[c]["outT"]) for c in cores], axis=1)
        h = np.ascontiguousarray(outT.T)
    return h[None].astype(f32)
```
